# Optimizing a Trainium2 kernel written in Bass

```python
import math
import jax, jax.numpy as jnp
from jax import lax
import numpy as np

D_MODEL = 2048
BATCH = 16
SEQ = 2048
DEPTH = 2
DEC_BATCH = 4
DEC_SEQ = 2048
PAST_LEN = 128

GRID_W = 64
N_MEM = 256
EPS = 1e-6
ROPE_THETA = 10000.0
BLOCK_Q = 128

RET_HEADS = 4
RET_DK = 256
RET_DV = 256
RET_CHUNK = 128
RET_DECAY_EXP_FWD = 5.0
RET_DECAY_EXP_BWD = 5.5
ATT_HEADS = 8
ATT_KV_HEADS = 2
ATT_HD = 128
EVEN_SPLITS = (RET_HEADS * RET_DK, RET_HEADS * RET_DK, RET_HEADS * RET_DV, RET_HEADS * RET_DV,
               ATT_HEADS * ATT_HD, ATT_KV_HEADS * ATT_HD, ATT_KV_HEADS * ATT_HD)
EVEN_IN = 5632
EVEN_OUT = RET_HEADS * RET_DV + ATT_HEADS * ATT_HD
SSD_EXPAND = 2
SSD_INNER = SSD_EXPAND * D_MODEL
SSD_HEADDIM = 64
SSD_HEADS = SSD_INNER // SSD_HEADDIM
SSD_STATE = 128
SSD_GROUPS = 8
SSD_CONV = 5
SSD_CHUNK = 128
SSD_CONV_DIM = SSD_INNER + 2 * SSD_GROUPS * SSD_STATE
ODD_IN = SSD_INNER + SSD_CONV_DIM + 2 * SSD_HEADS
XA_HEADS = 4
XA_HD = D_MODEL // XA_HEADS
D_FF = ((8 * D_MODEL // 3 + 255) // 256) * 256
N_NORMS = 7
N_EVEN = (DEPTH + 1) // 2
N_ODD = DEPTH // 2

kernel_name = 'hybrid_retention_gqa_ssd_encoder'


def rms_norm(x, g):
    xf = x.astype(jnp.float32)
    y = xf * lax.rsqrt(jnp.mean(xf * xf, axis=-1, keepdims=True) + EPS)
    return (y * g.astype(jnp.float32)).astype(x.dtype)


def rope_angles(pos, dim):
    inv = ROPE_THETA ** (-jnp.arange(0, dim, 2, dtype=jnp.float32) / dim)
    ang = pos.astype(jnp.float32)[:, None] * inv[None, :]
    return jnp.cos(ang), jnp.sin(ang)


def apply_rope(x, cos, sin):
    xf = x.astype(jnp.float32)
    x1, x2 = jnp.split(xf, 2, axis=-1)
    c, s = cos[:, None, :], sin[:, None, :]
    return jnp.concatenate([x1 * c - x2 * s, x2 * c + x1 * s], axis=-1).astype(x.dtype)


def retention_scan(q, k, v, log_gamma, strict):
    b, h, t, dk = q.shape
    dv = v.shape[-1]
    c = RET_CHUNK
    n = t // c
    idx = jnp.arange(c, dtype=jnp.float32)
    diff = idx[:, None] - idx[None, :]
    mask = (diff > 0) if strict else (diff >= 0)
    lg = log_gamma[:, None, None]
    dmat = jnp.where(mask, jnp.exp(lg * jnp.where(mask, diff, 0.0)), 0.0)
    xi = jnp.exp(lg * (idx + 1.0)[None, :, None])
    zeta = jnp.exp(lg * (c - 1.0 - idx)[None, :, None])
    g_chunk = jnp.exp(lg * c)

    def step(state, inp):
        qc, kc, vc = inp
        inner = jnp.einsum('bhij,bhjv->bhiv', jnp.einsum('bhid,bhjd->bhij', qc, kc) * dmat, vc)
        cross = jnp.einsum('bhid,bhdv->bhiv', qc * xi, state)
        state = g_chunk * state + jnp.einsum('bhjd,bhjv->bhdv', kc, vc * zeta)
        return state, inner + cross

    to_chunks = lambda a: jnp.moveaxis(a.reshape(b, h, n, c, a.shape[-1]), 2, 0)
    state0 = jnp.zeros((b, h, dk, dv), jnp.float32)
    _, out = lax.scan(step, state0, (to_chunks(q), to_chunks(k), to_chunks(v)))
    return jnp.moveaxis(out, 0, 2).reshape(b, h, t, dv)


def retention_attention_mixer(h, w_in, q_gain, k_gain, w_out):
    b, t, _ = h.shape
    proj = h @ w_in
    offs = np.cumsum(EVEN_SPLITS)[:-1].tolist()
    rq, rk, rv, rg, aq, ak, av = jnp.split(proj, offs, axis=-1)

    pos = jnp.arange(t)
    cos1, sin1 = rope_angles(pos, RET_DK)
    rq = apply_rope(rq.reshape(b, t, RET_HEADS, RET_DK), cos1, sin1)
    rk = apply_rope(rk.reshape(b, t, RET_HEADS, RET_DK), cos1, sin1) * (RET_DK ** -0.5)
    to_bhtd = lambda a: a.transpose(0, 2, 1, 3).astype(jnp.float32)
    rq, rk = to_bhtd(rq), to_bhtd(rk)
    rv = to_bhtd(rv.reshape(b, t, RET_HEADS, RET_DV))
    heads = jnp.arange(RET_HEADS, dtype=jnp.float32)
    lg_f = jnp.log1p(-jnp.exp2(-(RET_DECAY_EXP_FWD + heads)))
    lg_b = jnp.log1p(-jnp.exp2(-(RET_DECAY_EXP_BWD + heads)))
    flip = lambda a: jnp.flip(a, axis=2)
    ret = (retention_scan(rq, rk, rv, lg_f, False)
           + flip(retention_scan(flip(rq), flip(rk), flip(rv), lg_b, True)))
    mu = jnp.mean(ret, axis=-1, keepdims=True)
    var = jnp.mean(jnp.square(ret - mu), axis=-1, keepdims=True)
    ret = ((ret - mu) * lax.rsqrt(var + EPS)).transpose(0, 2, 1, 3).reshape(b, t, RET_HEADS * RET_DV)
    ret = (jax.nn.silu(rg.astype(jnp.float32)) * ret).astype(h.dtype)

    aq = rms_norm(aq.reshape(b, t, ATT_HEADS, ATT_HD), q_gain)
    ak = rms_norm(ak.reshape(b, t, ATT_KV_HEADS, ATT_HD), k_gain)
    av = av.reshape(b, t, ATT_KV_HEADS, ATT_HD)
    rows = t // GRID_W
    row = jnp.repeat(jnp.arange(rows), GRID_W)
    col = jnp.tile(jnp.arange(GRID_W), rows)
    half = ATT_HD // 2
    cos_r, sin_r = rope_angles(row, half)
    cos_c, sin_c = rope_angles(col, half)
    axial = lambda a: jnp.concatenate([apply_rope(a[..., :half], cos_r, sin_r),
                                       apply_rope(a[..., half:], cos_c, sin_c)], axis=-1)
    aq, ak = axial(aq), axial(ak)
    grp = ATT_HEADS // ATT_KV_HEADS
    nb = t // BLOCK_Q
    qb = aq.reshape(b, nb, BLOCK_Q, ATT_KV_HEADS, grp, ATT_HD).transpose(1, 0, 3, 4, 2, 5)
    kh = ak.transpose(0, 2, 1, 3)
    vh = av.transpose(0, 2, 1, 3)

    def attend(qblk):
        s = jnp.einsum('bkgqd,bktd->bkgqt', qblk, kh).astype(jnp.float32) * (ATT_HD ** -0.5)
        p = jax.nn.softmax(s, axis=-1).astype(vh.dtype)
        return jnp.einsum('bkgqt,bktd->bkgqd', p, vh)

    att = lax.map(attend, qb)
    att = att.transpose(1, 0, 4, 2, 3, 5).reshape(b, t, ATT_HEADS * ATT_HD)
    return jnp.concatenate([ret, att], axis=-1) @ w_out


def ssd_scan(x, dt, a, bm, cm):
    b, t, nh, p = x.shape
    g, n = bm.shape[2], bm.shape[3]
    kh = nh // g
    L = SSD_CHUNK
    nc = t // L
    chunks = lambda arr, tail: jnp.moveaxis(arr.reshape((b, nc, L) + tail), 1, 0)
    xs = chunks(x * dt[..., None], (g, kh, p))
    a_s = chunks(dt * a, (g, kh))
    bs = chunks(bm, (g, n))
    cs = chunks(cm, (g, n))
    causal = (jnp.arange(L)[:, None] >= jnp.arange(L)[None, :])[None, :, :, None, None]

    def step(state, inp):
        xc, ac, bc, cc = inp
        acum = jnp.cumsum(ac, axis=1)
        seg = acum[:, :, None] - acum[:, None, :]
        lmat = jnp.where(causal, jnp.exp(jnp.where(causal, seg, 0.0)), 0.0)
        cb = jnp.einsum('blgn,bsgn->blsg', cc, bc)
        y_diag = jnp.einsum('blsgk,bsgkp->blgkp', cb[..., None] * lmat, xc)
        y_off = jnp.einsum('blgn,bgkpn->blgkp', cc, state) * jnp.exp(acum)[..., None]
        decay = jnp.exp(acum[:, -1:] - acum)[..., None]
        state = (state * jnp.exp(acum[:, -1])[..., None, None]
                 + jnp.einsum('blgn,blgkp->bgkpn', bc, decay * xc))
        return state, y_diag + y_off

    state0 = jnp.zeros((b, g, kh, p, n), jnp.float32)
    _, y = lax.scan(step, state0, (xs, a_s, bs, cs))
    return jnp.moveaxis(y, 0, 1).reshape(b, t, nh, p)


def ssd_mixer(h, w_in, conv_w, conv_b, a_log, dt_bias, d_skip, norm_g, w_out):
    b, t, _ = h.shape
    proj = h @ w_in
    z, xbc, dt = jnp.split(proj, [SSD_INNER, SSD_INNER + SSD_CONV_DIM], axis=-1)
    pad = SSD_CONV // 2
    xbc = lax.conv_general_dilated(xbc, conv_w[:, None, :], window_strides=(1,), padding=[(pad, pad)],
                                   dimension_numbers=('NWC', 'WIO', 'NWC'),
                                   feature_group_count=SSD_CONV_DIM)
    xbc = jax.nn.silu((xbc + conv_b).astype(jnp.float32))
    xs, bm, cm = jnp.split(xbc, [SSD_INNER, SSD_INNER + SSD_GROUPS * SSD_STATE], axis=-1)
    xs = xs.reshape(b, t, SSD_HEADS, SSD_HEADDIM)
    bm = bm.reshape(b, t, SSD_GROUPS, SSD_STATE)
    cm = cm.reshape(b, t, SSD_GROUPS, SSD_STATE)
    dt = jax.nn.softplus(dt.reshape(b, t, 2, SSD_HEADS).astype(jnp.float32) + dt_bias.astype(jnp.float32))
    a = -jnp.exp(a_log.astype(jnp.float32))
    flip = lambda arr: jnp.flip(arr, axis=1)
    y = (ssd_scan(xs, dt[:, :, 0], a[0], bm, cm)
         + flip(ssd_scan(flip(xs), flip(dt[:, :, 1]), a[1], flip(bm), flip(cm)))
         + d_skip.astype(jnp.float32)[:, None] * xs)
    y = y.reshape(b, t, SSD_INNER) * jax.nn.silu(z.astype(jnp.float32))
    yg = y.reshape(b, t, SSD_GROUPS, SSD_INNER // SSD_GROUPS)
    yg = yg * lax.rsqrt(jnp.mean(yg * yg, axis=-1, keepdims=True) + EPS)
    y = (yg.reshape(b, t, SSD_INNER) * norm_g.astype(jnp.float32)).astype(h.dtype)
    return y @ w_out


def memory_cross_attention(h, m, wq, wkv, wo):
    b, t, _ = h.shape
    q = (h @ wq).reshape(b, t, XA_HEADS, XA_HD)
    kv = (m @ wkv).reshape(b, m.shape[1], 2, XA_HEADS, XA_HD)
    s = jnp.einsum('bthd,bmhd->bhtm', q, kv[:, :, 0]).astype(jnp.float32) * (XA_HD ** -0.5)
    p = jax.nn.softmax(s, axis=-1).astype(h.dtype)
    o = jnp.einsum('bhtm,bmhd->bthd', p, kv[:, :, 1]).reshape(b, t, D_MODEL)
    return o @ wo


def swiglu(h, w_gu, w_down):
    g, u = jnp.split(h @ w_gu, 2, axis=-1)
    return (jax.nn.silu(g) * u) @ w_down


def encoder_trunk(x, mem, norm_g, ev_w_in, ev_q_gain, ev_k_gain, ev_w_out,
                  od_w_in, od_conv_w, od_conv_b, od_a_log, od_dt_bias, od_d, od_norm_g, od_w_out,
                  xa_wq, xa_wkv, xa_wo, ffn_w_gu, ffn_w_down):
    for i in range(DEPTH):
        gn = norm_g[i]
        j = i // 2
        hin = rms_norm(x, gn[0])
        if i % 2 == 0:
            mix = retention_attention_mixer(hin, ev_w_in[j], ev_q_gain[j], ev_k_gain[j], ev_w_out[j])
        else:
            mix = ssd_mixer(hin, od_w_in[j], od_conv_w[j], od_conv_b[j], od_a_log[j], od_dt_bias[j],
                            od_d[j], od_norm_g[j], od_w_out[j])
        x = x + rms_norm(mix, gn[1])
        xa = memory_cross_attention(rms_norm(x, gn[2]), rms_norm(mem, gn[4]), xa_wq[i], xa_wkv[i], xa_wo[i])
        x = x + rms_norm(xa, gn[3])
        ff = swiglu(rms_norm(x, gn[5]), ffn_w_gu[i], ffn_w_down[i])
        x = x + rms_norm(ff, gn[6])
    return x


def setup_inputs(seed: int = 0) -> dict:
    key = jax.random.key(seed)
    ks = jax.random.split(key, 24)
    f32 = jnp.float32
    nrm = lambda k, shape, scale: scale * jax.random.normal(k, shape, f32)
    gain = lambda k, shape: 1.0 + 0.02 * jax.random.normal(k, shape, f32)
    dt0 = jnp.exp(jax.random.uniform(ks[13], (N_ODD, 2, SSD_HEADS), f32, math.log(1e-3), math.log(1e-1)))
    return {
        'x_prompt': nrm(ks[0], (BATCH, SEQ, D_MODEL), 1.0),
        'x_sample': nrm(ks[1], (DEC_BATCH, DEC_SEQ, D_MODEL), 1.0),
        'mem_prompt': nrm(ks[2], (BATCH, N_MEM, D_MODEL), 1.0),
        'mem_sample': nrm(ks[3], (DEC_BATCH, N_MEM, D_MODEL), 1.0),
        'norm_g': gain(ks[4], (DEPTH, N_NORMS, D_MODEL)),
        'ev_w_in': nrm(ks[5], (N_EVEN, D_MODEL, EVEN_IN), D_MODEL ** -0.5),
        'ev_q_gain': gain(ks[6], (N_EVEN, ATT_HD)),
        'ev_k_gain': gain(ks[7], (N_EVEN, ATT_HD)),
        'ev_w_out': nrm(ks[8], (N_EVEN, EVEN_OUT, D_MODEL), EVEN_OUT ** -0.5),
        'od_w_in': nrm(ks[9], (N_ODD, D_MODEL, ODD_IN), D_MODEL ** -0.5),
        'od_conv_w': nrm(ks[10], (N_ODD, SSD_CONV, SSD_CONV_DIM), SSD_CONV ** -0.5),
        'od_conv_b': nrm(ks[11], (N_ODD, SSD_CONV_DIM), 0.01),
        'od_a_log': jnp.log(jax.random.uniform(ks[12], (N_ODD, 2, SSD_HEADS), f32, 1.0, 16.0)),
        'od_dt_bias': dt0 + jnp.log(-jnp.expm1(-dt0)),
        'od_d': gain(ks[14], (N_ODD, SSD_HEADS)),
        'od_norm_g': gain(ks[15], (N_ODD, SSD_INNER)),
        'od_w_out': nrm(ks[16], (N_ODD, SSD_INNER, D_MODEL), SSD_INNER ** -0.5),
        'xa_wq': nrm(ks[17], (DEPTH, D_MODEL, D_MODEL), D_MODEL ** -0.5),
        'xa_wkv': nrm(ks[18], (DEPTH, D_MODEL, 2 * D_MODEL), D_MODEL ** -0.5),
        'xa_wo': nrm(ks[19], (DEPTH, D_MODEL, D_MODEL), D_MODEL ** -0.5),
        'ffn_w_gu': nrm(ks[20], (DEPTH, D_MODEL, 2 * D_FF), D_MODEL ** -0.5),
        'ffn_w_down': nrm(ks[21], (DEPTH, D_FF, D_MODEL), D_FF ** -0.5),
    }


def reference(x_prompt, x_sample, mem_prompt, mem_sample, norm_g, ev_w_in, ev_q_gain, ev_k_gain, ev_w_out,
              od_w_in, od_conv_w, od_conv_b, od_a_log, od_dt_bias, od_d, od_norm_g, od_w_out,
              xa_wq, xa_wkv, xa_wo, ffn_w_gu, ffn_w_down):
    y_prompt = encoder_trunk(x_prompt, mem_prompt, norm_g, ev_w_in, ev_q_gain, ev_k_gain, ev_w_out,
                             od_w_in, od_conv_w, od_conv_b, od_a_log, od_dt_bias, od_d, od_norm_g, od_w_out,
                             xa_wq, xa_wkv, xa_wo, ffn_w_gu, ffn_w_down)
    y_sample = encoder_trunk(x_sample, mem_sample, norm_g, ev_w_in, ev_q_gain, ev_k_gain, ev_w_out,
                             od_w_in, od_conv_w, od_conv_b, od_a_log, od_dt_bias, od_d, od_norm_g, od_w_out,
                             xa_wq, xa_wkv, xa_wo, ffn_w_gu, ffn_w_down)
    return (y_prompt, y_sample)
```

```python
import numpy as np
import ml_dtypes
import concourse.bass as bass
import concourse.mybir as mybir
from concourse.bass_utils import run_bass_kernel_spmd

F32 = mybir.dt.float32
BF16 = mybir.dt.bfloat16
AF = mybir.ActivationFunctionType
ALU = mybir.AluOpType
AX = mybir.AxisListType

T = 2048
D = 2048
NT = 16
EPS = 1e-6
LIM = 30000


class DSem:
    __slots__ = ("sem", "cnt")

    def __init__(self, sem):
        self.sem = sem
        self.cnt = 0


class Buf:
    __slots__ = ("name", "w", "r", "ds", "excl")

    def __init__(self, name):
        self.name = name
        self.w = {}
        self.r = {}
        self.ds = None
        self.excl = False


class Sched:
    def __init__(self, nc):
        self.nc = nc
        self.E = {"pe": nc.tensor, "act": nc.scalar, "dve": nc.vector, "pool": nc.gpsimd, "sp": nc.sync}
        self.ctr = {}
        self.nsem = 0
        for k in ("pe", "act", "dve", "pool"):
            self._newctr(k)
        self.waited = {e: {} for e in self.E}
        self.free_ds = []
        self.live_ds = []
        self.persist = []
        self.ninst = 0
        self.nwait = 0

    def _sem(self, name):
        self.nsem += 1
        return self.nc.alloc_semaphore(name="%s_%d" % (name, self.nsem))

    def _newctr(self, k):
        self.ctr[k] = [self._sem("c" + k), 0]

    def buf(self, name="b", persist=False):
        b = Buf(name)
        if persist:
            self.persist.append(b)
        return b

    def _ensure(self, e, sem, val, ds):
        if ds is not None:
            val = max(val, ds.cnt)
        key = id(sem)
        if self.waited[e].get(key, 0) >= val:
            return
        self.E[e].wait_ge(sem, val)
        self.waited[e][key] = val
        self.nwait += 1

    def _deps(self, e, reads, writes):
        own = self.ctr[e][0] if e in self.ctr else None
        for b in reads:
            for (sem, ds), v in b.w.items():
                if sem is own and e == "pe":
                    continue
                self._ensure(e, sem, v, ds)
            if b.excl:
                for (sem, ds), v in b.r.items():
                    if sem is not own:
                        self._ensure(e, sem, v, ds)
        for b in writes:
            for d in (b.w, b.r):
                for (sem, ds), v in d.items():
                    if sem is own and e == "pe":
                        continue
                    self._ensure(e, sem, v, ds)

    def _record(self, ev, val, reads, writes):
        for b in writes:
            b.w = {ev: val}
            b.r = {}
        for b in reads:
            if b in writes:
                continue
            b.r[ev] = max(b.r.get(ev, 0), val)

    def _tick(self, e, ins):
        c = self.ctr[e]
        if c[1] >= LIM:
            self._newctr(e)
            c = self.ctr[e]
        c[1] += 1
        ins.then_inc(c[0], 1)
        return (c[0], None), c[1]

    def op(self, e, fn, reads=(), writes=()):
        self._deps(e, reads, writes)
        ins = fn(self.E[e])
        ev, val = self._tick(e, ins)
        self._record(ev, val, reads, writes)
        self.ninst += 1
        return ins

    def mm(self, fns, reads=(), writes=()):
        self._deps("pe", reads, writes)
        ins = None
        for fn in fns:
            ins = fn(self.E["pe"])
        ev, val = self._tick("pe", ins)
        self._record(ev, val, reads, writes)
        self.ninst += len(fns)

    def dma(self, q, out, in_, sb, reads=(), writes=(), **kw):
        self._deps(q, reads, writes)
        if sb.ds is None or sb.ds.cnt + 16 > LIM or sb.ds not in self.live_ds:
            sb.ds = self.free_ds.pop() if self.free_ds else None
            if sb.ds is None or sb.ds.cnt + 16 > LIM:
                sb.ds = DSem(self._sem("d"))
                self.all_ds = getattr(self, "all_ds", []) + [sb.ds]
            self.live_ds.append(sb.ds)
        ds = sb.ds
        ins = self.E[q].dma_start(out=out, in_=in_, **kw)
        ds.cnt += 16
        ins.then_inc(ds.sem, 16)
        self._record((ds.sem, ds), ds.cnt, reads, writes)
        self.ninst += 1
        return ins

    def barrier(self):
        for e in self.E:
            for k, c in self.ctr.items():
                if k != e and c[1] > 0:
                    self._ensure(e, c[0], c[1], None)
            for ds in getattr(self, "all_ds", []):
                self._ensure(e, ds.sem, ds.cnt, ds)
        self.free_ds.extend(d for d in self.live_ds if d.cnt + 16 <= LIM)
        self.live_ds = []
        for b in self.persist:
            b.w = {}
            b.r = {}


KB = 1024
RET_H, ATT_H, KV_H = 4, 8, 2
DFF = 5632
NFT = DFF // 128


class KB_:
    pass


class Builder:
    def __init__(self, nc, NS, want):
        self.nc = nc
        self.S = Sched(nc)
        self.NS = NS
        self.uid = 0
        self.rot = 0
        self.erot = 0
        S = self.S
        di = lambda n, s, dt=F32: nc.dram_tensor(n, s, dt, kind="ExternalInput").ap()
        ds = lambda n, s, dt=F32: (nc.dram_tensor(n, s, dt, kind="Internal").ap(), S.buf(n, persist=True))
        self.x = di("x", [NS, T, D])
        self.mem = di("mem", [NS, 256, D])
        self.y = nc.dram_tensor("y", [NS, T, D], F32, kind="ExternalOutput").ap()
        self.b_y = S.buf("y", persist=True)
        self.norm_g = di("norm_g", [2, 7, D])
        self.w = {}
        shapes = dict(ev_w_in=[D, 5632], ev_w_out=[D, D], od_w_in=[D, 10368], od_w_out=[4096, D],
                      xa_wq0=[D, D], xa_wkv0=[D, 2 * D], xa_wo0=[D, D], ffn_w_gu0=[D, 2 * DFF], ffn_w_down0=[DFF, D],
                      xa_wq1=[D, D], xa_wkv1=[D, 2 * D], xa_wo1=[D, D], ffn_w_gu1=[D, 2 * DFF], ffn_w_down1=[DFF, D])
        for k, s in shapes.items():
            if k in want:
                self.w[k] = di(k, s)
        self.qk_gain = di("qk_gain", [2, 128])
        self.c_ident = di("c_ident", [128, 128])
        self.c_rope1 = di("c_rope1", [2, 128, T])
        self.c_rope2 = di("c_rope2", [2, T, 128])
        self.c_rmask = di("c_rmask", [RET_H, 128, 31 * 128])
        self.c_tri = di("c_tri", [4, 128, 128])
        if "od_w_in" in want:
            self.od_conv_w = di("od_conv_w", [128, 48, 5])
            self.od_conv_b = di("od_conv_b", [128, 48])
            self.od_a_log = di("od_a_log", [128])
            self.od_dt_bias = di("od_dt_bias", [128])
            self.od_d = di("od_d", [64])
            self.od_norm_g = di("od_norm_g", [4096])
        self.rqT, self.b_rqT = ds("rqT", [RET_H, 2, 128, T], BF16)
        self.rkT, self.b_rkT = ds("rkT", [RET_H, 2, 128, T], BF16)
        self.rv, self.b_rv = ds("rv", [T, 1024], BF16)
        self.rg, self.b_rg = ds("rg", [T, 1024], F32)
        self.aqT, self.b_aqT = ds("aqT", [ATT_H, 128, T], BF16)
        self.akT, self.b_akT = ds("akT", [KV_H, 128, T], BF16)
        self.av, self.b_av = ds("av", [T, 256], BF16)
        self.catT, self.b_catT = ds("catT", [32, 128, T], BF16)
        self.xs1, self.b_xs1 = ds("xs1", [T, D])
        self.xs2, self.b_xs2 = ds("xs2", [T, D])
        self.xl, self.b_xl = ds("xl", [T, D])
        self.memk, self.b_memk = ds("memk", [16, 128, 256], BF16)
        self.memv, self.b_memv = ds("memv", [256, D], BF16)
        al = lambda n, s, dt: nc.alloc_sbuf_tensor(n, s, dt).ap()
        self.identb = al("identb", [128, 128], BF16)
        self.onesb = al("onesb", [128, 128], BF16)
        self.idf = al("idf", [128, 128], F32)
        self.b_const = S.buf("const")
        self.stat = al("stat", [128, 256], F32)
        self.b_stat = [S.buf("stat%d" % i) for i in range(32)]
        self.srot = 0
        S.dma("sp", self.idf, self.c_ident, self.b_const, writes=[self.b_const])
        S.op("dve", lambda e: e.tensor_copy(out=self.identb, in_=self.idf), reads=[self.b_const], writes=[self.b_const])
        S.op("dve", lambda e: e.memset(self.onesb, 1.0), writes=[self.b_const])
        self.stat2 = al("stat2", [128, 64], F32)
        self.b_stat2 = [S.buf("stat2_%d" % i) for i in range(2)]
        self.junk512 = al("junk512", [128, 512], BF16)
        self.b_junk512 = S.buf("junk512")
        self.banks = [nc.alloc_psum_tensor("bank%d" % i, [128, 512], F32).ap() for i in range(8)]
        self.b_bank = [S.buf("bank%d" % i) for i in range(8)]
        for b in self.b_bank:
            b.excl = True
        self.base = nc.sbuf_base
        assert nc.sbuf_bytes_remaining >= 12 * 16 * KB, nc.sbuf_bytes_remaining

    def A(self, off, shape, dt):
        self.uid += 1
        return self.nc.alloc_sbuf_tensor_at("a%d" % self.uid, shape, dt, offset=self.base + off).ap()

    def st(self, n=1):
        i = self.srot % 32
        self.srot += 1
        return self.stat[:, i * 8:i * 8 + n], self.b_stat[i]

    def bank(self):
        i = self.rot % getattr(self, "nrot", 8)
        self.rot += 1
        return self.banks[i], self.b_bank[i]

    def ev_eng(self):
        self.erot += 1
        return "act" if self.erot % 2 else "dve"

    def copy(self, eng, out, in_, reads, writes):
        if eng == "act":
            self.S.op("act", lambda e: e.activation(out=out, in_=in_, func=AF.Copy), reads=reads, writes=writes)
        else:
            self.S.op(eng, lambda e: e.tensor_copy(out=out, in_=in_), reads=reads, writes=writes)

    def rstd_from_ss(self, ss, b_ss, n, scale, eps=EPS):
        S = self.S
        S.op("dve", lambda e: e.tensor_scalar(out=ss, in0=ss, scalar1=scale, scalar2=eps, op0=ALU.mult, op1=ALU.add), reads=[b_ss], writes=[b_ss])
        S.op("act", lambda e: e.activation(out=ss, in_=ss, func=AF.Sqrt), reads=[b_ss], writes=[b_ss])
        S.op("dve", lambda e: e.reciprocal(out=ss, in_=ss), reads=[b_ss], writes=[b_ss])

    def norm_T(self, xt, b_x, gB, b_gB, xn, b_xn, junk, b_junk, dst, b_dst):
        S = self.S
        ss, b_ss = self.st(1)
        S.op("act", lambda e: e.activation(out=junk, in_=xt, func=AF.Square, accum_out=ss), reads=[b_x], writes=[b_junk, b_ss])
        self.rstd_from_ss(ss, b_ss, 1, 1.0 / D)
        S.op("dve", lambda e: e.scalar_tensor_tensor(out=xn, in0=xt, scalar=ss, in1=gB, op0=ALU.mult, op1=ALU.mult),
             reads=[b_x, b_ss, b_gB], writes=[b_xn])
        self.transpose_to(xn, b_xn, 16, dst, b_dst)

    def transpose_to(self, src, b_src, ntile, dst, b_dst):
        S = self.S
        for g0 in range(0, ntile, 4):
            n = min(4, ntile - g0)
            bk, b_bk = self.bank()
            pv = bk.bitcast(BF16)[:, 0:512].rearrange("p (a b) -> p a b", a=4)
            for j in range(n):
                S.op("pe", lambda e, j=j: e.transpose(out=pv[:, j, :], in_=src[:, (g0 + j) * 128:(g0 + j + 1) * 128], identity=self.identb),
                     reads=[b_src, self.b_const], writes=[b_bk])
            self.copy(self.ev_eng(), dst[:, g0:g0 + n, :], pv[:, 0:n, :], [b_bk], [b_dst])

    def load_gB(self, off, layer, slot):
        gB = self.A(off, [128, D], F32)
        b = self.S.buf("gB")
        self.S.dma("sp", gB, self.norm_g[layer, slot, :].partition_broadcast(128), b, writes=[b])
        return gB, b

    def phase_A(self, xsrc, b_xsrc, layer, ntok=T, gslot=0, hoff=0):
        S = self.S
        S.barrier()
        ntt = ntok // 128
        hT = self.A(hoff, [128, 16, ntok], BF16)
        b_hT = S.buf("hT")
        o = hoff + 32 * ntok
        gB, b_gB = self.load_gB(o, layer, gslot)
        xt = [self.A(o + 8 * KB + i * 8 * KB, [128, D], F32) for i in range(2)]
        b_xt = [S.buf("xt") for i in range(2)]
        xn = [self.A(o + 24 * KB + i * 4 * KB, [128, D], BF16) for i in range(2)]
        b_xn = [S.buf("xn") for i in range(2)]
        junk = self.A(o + 32 * KB, [128, D], BF16)
        b_junk = S.buf("junk")
        for tt in range(ntt):
            i = tt % 2
            S.dma("sp", xt[i], xsrc[tt * 128:(tt + 1) * 128, :], b_xt[i], reads=[b_xsrc], writes=[b_xt[i]])
            self.norm_T(xt[i], b_xt[i], gB, b_gB, xn[i], b_xn[i], junk, b_junk, hT[:, :, tt * 128:(tt + 1) * 128], b_hT)
        return hT, b_hT

    def wbufs(self, off, n=2):
        self.wb = [self.A(off + i * 16 * KB, [128, 16, 512], BF16) for i in range(n)]
        self.b_wb = [self.S.buf("wb") for i in range(n)]
        self.wi = 0

    def wload(self, wd, k0, kp, c0, nc_):
        i = self.wi % len(self.wb)
        self.wi += 1
        wb, b = self.wb[i], self.b_wb[i]
        self.S.dma("pool", wb[:, 0:kp, 0:nc_], wd[k0 * 128:(k0 + kp) * 128, c0:c0 + nc_].rearrange("(a p) c -> p a c", p=128), b, writes=[b])
        return wb, b

    def tm_linear(self, srcT, b_src, KT, wd, c0, nct, ntt, evac, cw=512):
        S = self.S
        pieces = [(k0, min(16, KT - k0)) for k0 in range(0, KT, 16)]
        for ct in range(nct):
            if len(pieces) == 1:
                wb, b_wb = self.wload(wd, 0, KT, c0 + ct * cw, cw)
                for tt in range(ntt):
                    bk, b_bk = self.bank()
                    S.mm([(lambda e, k=k: e.matmul(bk[:, 0:cw], lhsT=srcT[:, k, tt * 128:(tt + 1) * 128], rhs=wb[:, k, 0:cw], start=(k == 0), stop=(k == KT - 1)))
                          for k in range(KT)], reads=[b_src, b_wb], writes=[b_bk])
                    evac(ct, tt, bk[:, 0:cw], b_bk)
            else:
                assert ntt <= 4
                bks = [((ct % 2) * 4 + tt) for tt in range(ntt)]
                for pi, (k0, kp) in enumerate(pieces):
                    wb, b_wb = self.wload(wd, k0, kp, c0 + ct * cw, cw)
                    for tt in range(ntt):
                        bk, b_bk = self.banks[bks[tt]], self.b_bank[bks[tt]]
                        S.mm([(lambda e, k=k: e.matmul(bk[:, 0:cw], lhsT=srcT[:, k0 + k, tt * 128:(tt + 1) * 128], rhs=wb[:, k, 0:cw],
                                                       start=(pi == 0 and k == 0), stop=(pi == len(pieces) - 1 and k == kp - 1)))
                              for k in range(kp)], reads=[b_src, b_wb], writes=[b_bk])
                for tt in range(ntt):
                    evac(ct, tt, self.banks[bks[tt]][:, 0:cw], self.b_bank[bks[tt]])

    def fm_linear(self, srcT, b_src, wd, c0, ncols, chunks, evac, pair=False, pc=512):
        S = self.S
        for p0 in range(0, ncols, pc):
            n = min(pc, ncols - p0)
            wb, b_wb = self.wload(wd, 0, 16, c0 + p0, n)
            ncc = n // 128

            def one(cc, t0, tn):
                bk, b_bk = self.bank()
                S.mm([(lambda e, k=k: e.matmul(bk[:, 0:tn], lhsT=wb[:, k, cc * 128:(cc + 1) * 128], rhs=srcT[:, k, t0:t0 + tn], start=(k == 0), stop=(k == 15)))
                      for k in range(16)], reads=[b_src, b_wb], writes=[b_bk])
                return bk[:, 0:tn], b_bk
            if pair:
                for cp in range(ncc // 2):
                    for ci, (t0, tn) in enumerate(chunks):
                        r0 = one(2 * cp, t0, tn)
                        r1 = one(2 * cp + 1, t0, tn)
                        evac((p0 // 128) // 2 + cp, ci, r0, r1)
            else:
                for cc in range(ncc):
                    for ci, (t0, tn) in enumerate(chunks):
                        ps, b_ps = one(cc, t0, tn)
                        evac(p0 // 128 + cc, ci, ps, b_ps)

    def qknorm_rope(self, ps, b_ps, nh, gi, tt, P):
        S = self.S
        n = nh * 128
        if P.get("tab_gi") != gi:
            P["tab_gi"] = gi
            g3 = P["gainB"][:, gi, :]
            S.op("dve", lambda e: e.tensor_tensor(out=P["COSg"], in0=P["COS"], in1=P["gainB"][:, gi:gi + 1, :].broadcast_to([128, 16, 128]), op=ALU.mult),
                 reads=[P["b_rope2"], P["b_gainB"]], writes=[P["b_tabg"]])
            v = lambda ap, f: ap.rearrange("p a (r f i) -> p a r f i", r=2, f=2, i=32)[:, :, :, f, :]
            for f in range(2):
                gsw = g3.rearrange("p (r f i) -> p r f i", r=2, f=2, i=32)[:, :, 1 - f, :].unsqueeze(1).broadcast_to([128, 16, 2, 32])
                S.op("dve", lambda e, f=f, gsw=gsw: e.tensor_tensor(out=v(P["SINSg"], f), in0=v(P["SINS"], f), in1=gsw, op=ALU.mult),
                     reads=[P["b_rope2"], P["b_gainB"]], writes=[P["b_tabg"]])
        pp = P["par"] = 1 - P.get("par", 0)
        sqj, b_sqj, q1, b_q1, tmpr, b_tmpr, qo, b_qo, qc, b_qc = [P[k][pp] for k in ("sqj", "b_sqj", "q1", "b_q1", "tmpr", "b_tmpr", "qo", "b_qo", "qc", "b_qc")]
        ss, b_ss = self.st(nh)
        v3 = lambda ap: ap.rearrange("p (h d) -> p h d", h=nh)
        v5 = lambda ap, f: ap.rearrange("p (h r f i) -> p h r f i", h=nh, r=2, f=2, i=32)[:, :, :, f, :]
        t4 = lambda tab, f: tab[:, tt, :].rearrange("p (r f i) -> p r f i", r=2, f=2, i=32)[:, :, f, :].unsqueeze(1).broadcast_to([128, nh, 2, 32])
        S.op("act", lambda e: e.activation(out=sqj[:, 0:n], in_=ps, func=AF.Square), reads=[b_ps], writes=[b_sqj])
        S.op("act", lambda e: e.activation(out=qc[:, 0:n], in_=ps, func=AF.Copy), reads=[b_ps], writes=[b_qc])
        S.op("dve", lambda e: e.tensor_tensor(out=v3(q1[:, 0:n]), in0=v3(ps), in1=P["COSg"][:, tt:tt + 1, :].broadcast_to([128, nh, 128]), op=ALU.mult),
             reads=[b_ps, P["b_tabg"]], writes=[b_q1])
        for f in range(2):
            S.op("pool", lambda e, f=f: e.tensor_tensor(out=v5(tmpr[:, 0:n], f), in0=v5(qc[:, 0:n], 1 - f), in1=t4(P["SINSg"], f), op=ALU.mult),
                 reads=[b_qc, P["b_tabg"]], writes=[b_tmpr])
        S.op("dve", lambda e: e.tensor_reduce(out=ss, in_=sqj[:, 0:n].rearrange("p (h d) -> p h d", h=nh), axis=AX.X, op=ALU.add), reads=[b_sqj], writes=[b_ss])
        self.rstd_from_ss(ss, b_ss, nh, 1.0 / 128)
        S.op("dve", lambda e: e.tensor_tensor(out=q1[:, 0:n], in0=q1[:, 0:n], in1=tmpr[:, 0:n], op=ALU.add), reads=[b_q1, b_tmpr], writes=[b_q1])
        S.op("dve", lambda e: e.tensor_tensor(out=v3(qo[:, 0:n]), in0=v3(q1[:, 0:n]), in1=ss.unsqueeze(2).broadcast_to([128, nh, 128]), op=ALU.mult),
             reads=[b_q1, b_ss], writes=[b_qo])
        return qo, b_qo

    def phase_L0proj(self, hT, b_hT):
        S = self.S
        S.barrier()
        wd = self.w["ev_w_in"]
        self.wbufs(64 * KB)
        o = 96 * KB
        c1 = self.A(o, [128, T], F32)
        s1 = self.A(o + 8 * KB, [128, T], F32)
        b_rope1 = S.buf("rope1")
        S.dma("sp", c1, self.c_rope1[0], b_rope1, writes=[b_rope1])
        S.dma("sp", s1, self.c_rope1[1], b_rope1, writes=[b_rope1])
        COS = self.A(o + 16 * KB, [128, 16, 128], F32)
        SINS = self.A(o + 24 * KB, [128, 16, 128], F32)
        b_rope2 = S.buf("rope2")
        S.dma("sp", COS, self.c_rope2[0].rearrange("(a p) d -> p a d", p=128), b_rope2, writes=[b_rope2])
        S.dma("sp", SINS, self.c_rope2[1].rearrange("(a p) d -> p a d", p=128), b_rope2, writes=[b_rope2])
        o = 128 * KB
        tmp = [self.A(o + i * 2 * KB, [128, 512], F32) for i in range(4)]
        b_tmp = [S.buf("tmp") for i in range(4)]
        o += 8 * KB
        sto = [self.A(o + i * 2 * KB, [128, 2, 512], BF16) for i in range(2)]
        b_sto = [S.buf("sto") for i in range(2)]
        o += 4 * KB
        stv = [self.A(o + i * KB, [128, 512], BF16) for i in range(2)]
        b_stv = [S.buf("stv") for i in range(2)]
        o += 2 * KB
        stg = [self.A(o + i * 2 * KB, [128, 512], F32) for i in range(2)]
        b_stg = [S.buf("stg") for i in range(2)]
        o += 4 * KB
        P = {}
        for nm, sz, dt in (("q1", 2, F32), ("sqj", 2, F32), ("tmpr", 2, F32), ("qo", 1, BF16), ("qc", 2, F32)):
            P[nm] = [self.A(o + i * sz * KB, [128, 512], dt) for i in range(2)]
            P["b_" + nm] = [S.buf(nm) for i in range(2)]
            o += 2 * sz * KB
        qst = [self.A(o + i * 4 * KB, [128, 4, 512], BF16) for i in range(2)]
        b_qst = [S.buf("qst") for i in range(2)]
        o += 8 * KB
        P["gainB"] = self.A(o, [128, 2, 128], F32)
        P["b_gainB"] = S.buf("gainB")
        S.dma("sp", P["gainB"], self.qk_gain.partition_broadcast(128), P["b_gainB"], writes=[P["b_gainB"]])
        P["COS"], P["SINS"], P["b_rope2"] = COS, SINS, b_rope2
        o += KB
        P["COSg"] = self.A(o, [128, 16, 128], F32)
        P["SINSg"] = self.A(o + 8 * KB, [128, 16, 128], F32)
        P["b_tabg"] = S.buf("tabg")
        o += 16 * KB
        assert o <= 192 * KB, o
        chunks = [(i * 512, 512) for i in range(4)]
        cnt = [0]

        def mk_rope(dst, b_dst):
            def ev(hp, ci, r0, r1):
                (x1, b1), (x2, b2) = r0, r1
                t0 = ci * 512
                i = cnt[0] % 2
                cnt[0] += 1
                cs, sn = c1[:, t0:t0 + 512], s1[:, t0:t0 + 512]
                S.op("dve", lambda e: e.tensor_tensor(out=tmp[0], in0=x1, in1=cs, op=ALU.mult), reads=[b1, b_rope1], writes=[b_tmp[0]])
                S.op("dve", lambda e: e.tensor_tensor(out=tmp[1], in0=x2, in1=sn, op=ALU.mult), reads=[b2, b_rope1], writes=[b_tmp[1]])
                S.op("pool", lambda e: e.tensor_tensor(out=sto[i][:, 0, :], in0=tmp[0], in1=tmp[1], op=ALU.subtract), reads=[b_tmp[0], b_tmp[1]], writes=[b_sto[i]])
                S.op("dve", lambda e: e.tensor_tensor(out=tmp[2], in0=x2, in1=cs, op=ALU.mult), reads=[b2, b_rope1], writes=[b_tmp[2]])
                S.op("dve", lambda e: e.tensor_tensor(out=tmp[3], in0=x1, in1=sn, op=ALU.mult), reads=[b1, b_rope1], writes=[b_tmp[3]])
                S.op("pool", lambda e: e.tensor_tensor(out=sto[i][:, 1, :], in0=tmp[2], in1=tmp[3], op=ALU.add), reads=[b_tmp[2], b_tmp[3]], writes=[b_sto[i]])
                S.dma("sp", dst[hp, :, :, t0:t0 + 512].rearrange("j p t -> p j t"), sto[i], b_sto[i], reads=[b_sto[i]], writes=[b_dst])
            return ev
        self.fm_linear(hT, b_hT, wd, 0, 1024, chunks, mk_rope(self.rqT, self.b_rqT), pair=True)
        self.fm_linear(hT, b_hT, wd, 1024, 1024, chunks, mk_rope(self.rkT, self.b_rkT), pair=True)

        def ev_v(ct, tt, ps, b_ps):
            i = cnt[0] % 2
            cnt[0] += 1
            self.copy(self.ev_eng(), stv[i], ps, [b_ps], [b_stv[i]])
            S.dma("sp", self.rv[tt * 128:(tt + 1) * 128, ct * 512:(ct + 1) * 512], stv[i], b_stv[i], reads=[b_stv[i]], writes=[self.b_rv])
        self.tm_linear(hT, b_hT, 16, wd, 2048, 2, 16, ev_v)

        def ev_g(ct, tt, ps, b_ps):
            i = cnt[0] % 2
            cnt[0] += 1
            S.op("act", lambda e: e.activation(out=stg[i], in_=ps, func=AF.Silu), reads=[b_ps], writes=[b_stg[i]])
            S.dma("sp", self.rg[tt * 128:(tt + 1) * 128, ct * 512:(ct + 1) * 512], stg[i], b_stg[i], reads=[b_stg[i]], writes=[self.b_rg])
        self.tm_linear(hT, b_hT, 16, wd, 3072, 2, 16, ev_g)

        def ev_q(ct, tt, ps, b_ps):
            qo, b_qo = self.qknorm_rope(ps, b_ps, 4, 0, tt, P)
            i = (tt // 4) % 2
            self.transpose_to(qo, b_qo, 4, qst[i][:, :, (tt % 4) * 128:(tt % 4 + 1) * 128], b_qst[i])
            if tt % 4 == 3:
                t0 = (tt // 4) * 512
                S.dma("sp", self.aqT[ct * 4:ct * 4 + 4, :, t0:t0 + 512].rearrange("h p t -> p h t"), qst[i], b_qst[i], reads=[b_qst[i]], writes=[self.b_aqT])
        self.tm_linear(hT, b_hT, 16, wd, 4096, 2, 16, ev_q)

        def ev_kv(ct, tt, ps, b_ps):
            qo, b_qo = self.qknorm_rope(ps[:, 0:256], b_ps, 2, 1, tt, P)
            i = (tt // 4) % 2
            self.transpose_to(qo, b_qo, 2, qst[i][:, 0:2, (tt % 4) * 128:(tt % 4 + 1) * 128], b_qst[i])
            if tt % 4 == 3:
                t0 = (tt // 4) * 512
                S.dma("sp", self.akT[:, :, t0:t0 + 512].rearrange("h p t -> p h t"), qst[i][:, 0:2, :], b_qst[i], reads=[b_qst[i]], writes=[self.b_akT])
            j = cnt[0] % 2
            cnt[0] += 1
            self.copy("dve", stv[j][:, 0:256], ps[:, 256:512], [b_ps], [b_stv[j]])
            S.dma("sp", self.av[tt * 128:(tt + 1) * 128, :], stv[j][:, 0:256], b_stv[j], reads=[b_stv[j]], writes=[self.b_av])
        self.tm_linear(hT, b_hT, 16, wd, 5120, 1, 16, ev_kv)

    def bank4(self):
        return self.bank()

    def phase_ret(self):
        S = self.S
        S.barrier()
        self.nrot = 4
        SETB = 56 * KB
        o = 2 * SETB
        pT = [self.A(o + i * KB, [128, 512], BF16) for i in range(3)]
        b_pT = [S.buf("pT") for i in range(3)]
        o += 3 * KB
        osb = [self.A(o + i * KB, [128, 256], F32) for i in range(4)]
        b_osb = [S.buf("osb") for i in range(4)]
        o += 4 * KB
        junk = self.A(o, [128, 256], F32)
        b_junk = S.buf("junk")
        o += KB
        on = [self.A(o + i * KB, [128, 256], F32) for i in range(4)]
        b_on = [S.buf("on") for i in range(4)]
        o += 4 * KB
        og = [self.A(o + i * 512, [128, 256], BF16) for i in range(4)]
        b_og = [S.buf("og") for i in range(4)]
        o += 2 * KB
        stT = [self.A(o + i * 2 * KB, [128, 2, 512], BF16) for i in range(2)]
        b_stT = [S.buf("stT") for i in range(2)]
        pi = 0
        for hh in range(RET_H):
            so = (hh % 2) * SETB
            qT = self.A(so, [128, 2, T], BF16)
            kT = self.A(so + 8 * KB, [128, 2, T], BF16)
            v = self.A(so + 16 * KB, [128, 16, 256], BF16)
            gate = self.A(so + 24 * KB, [128, 16, 256], F32)
            strip = self.A(so + 40 * KB, [128, 31 * 128], F32)
            b_q, b_k, b_v, b_g, b_s = [S.buf(n) for n in ("rq", "rk", "rv", "rg", "strip")]
            S.dma("sp", qT, self.rqT[hh].rearrange("j p t -> p j t"), b_q, reads=[self.b_rqT], writes=[b_q])
            S.dma("sp", kT, self.rkT[hh].rearrange("j p t -> p j t"), b_k, reads=[self.b_rkT], writes=[b_k])
            S.dma("sp", v, self.rv[:, hh * 256:(hh + 1) * 256].rearrange("(a p) c -> p a c", p=128), b_v, reads=[self.b_rv], writes=[b_v])
            S.dma("sp", gate, self.rg[:, hh * 256:(hh + 1) * 256].rearrange("(a p) c -> p a c", p=128), b_g, reads=[self.b_rg], writes=[b_g])
            S.dma("sp", strip, self.c_rmask[hh], b_s, writes=[b_s])
            for c in range(4):
                ob = 4 + 2 * (c % 2)
                OA, OB = self.banks[ob].rearrange("p (a b) -> p a b", a=2), self.banks[ob + 1].rearrange("p (a b) -> p a b", a=2)
                b_OA, b_OB = self.b_bank[ob], self.b_bank[ob + 1]

                def smm(j):
                    bk, b_bk = self.bank4()
                    S.mm([(lambda e, kk=kk: e.matmul(bk, lhsT=kT[:, kk, j * 128:(j + 1) * 128], rhs=qT[:, kk, c * 512:(c + 1) * 512], start=(kk == 0), stop=(kk == 1)))
                          for kk in range(2)], reads=[b_q, b_k], writes=[b_bk])
                    return bk, b_bk
                nxt = smm(0)
                for j in range(16):
                    bk, b_bk = nxt
                    if j < 15:
                        nxt = smm(j + 1)
                    p, b_p = pT[pi % 3], b_pT[pi % 3]
                    pi += 1
                    d0 = c * 4 - j + 15
                    S.op("dve", lambda e: e.tensor_tensor(out=p, in0=bk, in1=strip[:, d0 * 128:d0 * 128 + 512], op=ALU.mult), reads=[b_bk, b_s], writes=[b_p])
                    S.mm([(lambda e, i=i: e.matmul((OA if i < 2 else OB)[:, i % 2, :], lhsT=p[:, i * 128:(i + 1) * 128], rhs=v[:, j, :], start=(j == 0 and i % 2 == 0), stop=(j == 15), skip_group_check=True))
                          for i in range(4)], reads=[b_p, b_v], writes=[b_OA, b_OB])
                si = c % 2
                Os = [((OA if i < 2 else OB)[:, i % 2, :], (b_OA if i < 2 else b_OB)) for i in range(4)]
                sta, b_sta = self.st(8)
                stb_, b_stb = self.st(8)
                for i in range(4):
                    O, b_O = Os[i]
                    S.op("act", lambda e, i=i, O=O: e.activation(out=osb[i], in_=O, func=AF.Copy, accum_out=sta[:, i:i + 1]), reads=[b_O], writes=[b_osb[i], b_sta])
                    S.op("act", lambda e, i=i, O=O: e.activation(out=junk, in_=O, func=AF.Square, accum_out=sta[:, 4 + i:5 + i]), reads=[b_O], writes=[b_junk, b_sta])
                mean, msq, var, nb = stb_[:, 0:4], stb_[:, 4:8], sta[:, 4:8], sta[:, 0:4]
                S.op("dve", lambda e: e.tensor_scalar(out=mean, in0=sta[:, 0:4], scalar1=1.0 / 256, scalar2=None, op0=ALU.mult), reads=[b_sta], writes=[b_stb])
                S.op("dve", lambda e: e.tensor_tensor(out=msq, in0=mean, in1=mean, op=ALU.mult), reads=[b_stb], writes=[b_stb])
                S.op("dve", lambda e: e.scalar_tensor_tensor(out=var, in0=var, scalar=1.0 / 256, in1=msq, op0=ALU.mult, op1=ALU.subtract), reads=[b_sta, b_stb], writes=[b_sta])
                self.rstd_from_ss(var, b_sta, 4, 1.0)
                S.op("dve", lambda e: e.scalar_tensor_tensor(out=nb, in0=mean, scalar=-1.0, in1=var, op0=ALU.mult, op1=ALU.mult), reads=[b_sta, b_stb], writes=[b_sta])
                for i in range(4):
                    tt = c * 4 + i
                    S.op("dve", lambda e, i=i: e.tensor_scalar(out=on[i], in0=osb[i], scalar1=var[:, i:i + 1], scalar2=nb[:, i:i + 1], op0=ALU.mult, op1=ALU.add),
                         reads=[b_osb[i], b_sta], writes=[b_on[i]])
                    S.op("pool", lambda e, i=i, tt=tt: e.tensor_tensor(out=og[i], in0=on[i], in1=gate[:, tt, :], op=ALU.mult), reads=[b_on[i], b_g], writes=[b_og[i]])
                for i in range(4):
                    self.transpose_to(og[i], b_og[i], 2, stT[si][:, :, i * 128:(i + 1) * 128], b_stT[si])
                S.dma("sp", self.catT[hh * 2:hh * 2 + 2, :, c * 512:(c + 1) * 512].rearrange("k p t -> p k t"), stT[si], b_stT[si], reads=[b_stT[si]], writes=[self.b_catT])

    def phase_att(self):
        S = self.S
        S.barrier()
        self.nrot = 4
        SETB = 13 * KB
        o = 2 * SETB
        pT = [self.A(o + i * KB, [128, 512], BF16) for i in range(3)]
        b_pT = [S.buf("pT") for i in range(3)]
        o += 3 * KB
        ob4 = [self.A(o + i * KB, [128, 4, 128], BF16) for i in range(2)]
        b_ob4 = [S.buf("ob4") for i in range(2)]
        o += 2 * KB
        stA = [self.A(o + i * KB, [128, 512], BF16) for i in range(2)]
        b_stA = [S.buf("stA") for i in range(2)]
        pi = 0
        scale = 128.0 ** -0.5
        for h in range(ATT_H):
            kv = h // 4
            so = (h % 2) * SETB
            qT = self.A(so, [128, T], BF16)
            kT = self.A(so + 4 * KB, [128, T], BF16)
            vx = self.A(so + 8 * KB, [128, 16, 132], BF16)
            b_q, b_k, b_v = [S.buf(n) for n in ("aq", "ak", "av")]
            S.dma("sp", qT, self.aqT[h], b_q, reads=[self.b_aqT], writes=[b_q])
            S.dma("sp", kT, self.akT[kv], b_k, reads=[self.b_akT], writes=[b_k])
            S.dma("sp", vx[:, :, 0:128], self.av[:, kv * 128:(kv + 1) * 128].rearrange("(a p) c -> p a c", p=128), b_v, reads=[self.b_av], writes=[b_v])
            S.op("pool", lambda e: e.memset(vx[:, :, 128:129], 1.0), writes=[b_v])
            for c in range(4):
                obk = 4 + 2 * (c % 2)
                OA, OB = self.banks[obk].rearrange("p (a b) -> p a b", a=2), self.banks[obk + 1].rearrange("p (a b) -> p a b", a=2)
                b_OA, b_OB = self.b_bank[obk], self.b_bank[obk + 1]

                def smm(j):
                    bk, b_bk = self.bank4()
                    S.mm([lambda e: e.matmul(bk, lhsT=kT[:, j * 128:(j + 1) * 128], rhs=qT[:, c * 512:(c + 1) * 512], start=True, stop=True)],
                         reads=[b_q, b_k], writes=[b_bk])
                    return bk, b_bk
                nxt = smm(0)
                for j in range(16):
                    bk, b_bk = nxt
                    if j < 15:
                        nxt = smm(j + 1)
                    p, b_p = pT[pi % 3], b_pT[pi % 3]
                    pi += 1
                    S.op("act", lambda e: e.activation(out=p, in_=bk, func=AF.Exp, scale=scale), reads=[b_bk], writes=[b_p])
                    S.mm([(lambda e, i=i: e.matmul((OA if i < 2 else OB)[:, i % 2, 0:129], lhsT=p[:, i * 128:(i + 1) * 128], rhs=vx[:, j, 0:129], start=(j == 0 and i % 2 == 0), stop=(j == 15), skip_group_check=True))
                          for i in range(4)], reads=[b_p, b_v], writes=[b_OA, b_OB])
                si = c % 2
                for i in range(4):
                    O, b_O = (OA if i < 2 else OB)[:, i % 2, :], (b_OA if i < 2 else b_OB)
                    st, b_st = self.st(1)
                    S.op("dve", lambda e: e.reciprocal(out=st, in_=O[:, 128:129]), reads=[b_O], writes=[b_st])
                    S.op("dve", lambda e: e.tensor_scalar(out=ob4[si][:, i, :], in0=O[:, 0:128], scalar1=st, scalar2=None, op0=ALU.mult), reads=[b_O, b_st], writes=[b_ob4[si]])
                bk, b_bk = self.bank4()
                pv = bk.bitcast(BF16)[:, 0:512].rearrange("p (a b) -> p a b", a=4)
                for i in range(4):
                    S.op("pe", lambda e, i=i: e.transpose(out=pv[:, i, :], in_=ob4[si][:, i, :], identity=self.identb), reads=[b_ob4[si], self.b_const], writes=[b_bk])
                self.copy(self.ev_eng(), stA[si], bk.bitcast(BF16)[:, 0:512], [b_bk], [b_stA[si]])
                S.dma("sp", self.catT[8 + h, :, c * 512:(c + 1) * 512], stA[si], b_stA[si], reads=[b_stA[si]], writes=[self.b_catT])

    def phase_mem(self, s, layer):
        S = self.S
        self.nrot = 8
        mT, b_mT = self.phase_A(self.mem[s], S.buf("memin", persist=True), layer, ntok=256, gslot=4, hoff=0)
        wd = self.w["xa_wkv%d" % layer]
        self.wbufs(64 * KB)
        o = 96 * KB
        stk = [self.A(o + i * 2 * KB, [128, 4, 256], BF16) for i in range(2)]
        b_stk = [S.buf("stk") for i in range(2)]
        o += 4 * KB
        stv = [self.A(o + i * KB, [128, 512], BF16) for i in range(2)]
        b_stv = [S.buf("stv") for i in range(2)]
        cnt = [0]

        def ev_k(ctile, ci, ps, b_ps):
            i = (ctile // 4) % 2
            self.copy(self.ev_eng(), stk[i][:, ctile % 4, :], ps, [b_ps], [b_stk[i]])
            if ctile % 4 == 3:
                S.dma("sp", self.memk[ctile - 3:ctile + 1].rearrange("c p m -> p c m"), stk[i], b_stk[i], reads=[b_stk[i]], writes=[self.b_memk])
        self.fm_linear(mT, b_mT, wd, 0, D, [(0, 256)], ev_k)

        def ev_v(ct, tt, ps, b_ps):
            i = cnt[0] % 2
            cnt[0] += 1
            self.copy(self.ev_eng(), stv[i], ps, [b_ps], [b_stv[i]])
            S.dma("sp", self.memv[tt * 128:(tt + 1) * 128, ct * 512:(ct + 1) * 512], stv[i], b_stv[i], reads=[b_stv[i]], writes=[self.b_memv])
        self.tm_linear(mT, b_mT, 16, wd, D, 4, 2, ev_v)

    def resid_stage(self, srcT, b_src, KT, wd, layer, gpost, xold_dram, b_xold, t0, xnew_dram, b_xnew, gpre, dstT, b_dstT, M):
        S = self.S
        XR, b_XR = M["XR"], M["b_XR"]
        gP, b_gP, gN, b_gN = M["gP"], M["b_gP"], M["gN"], M["b_gN"]
        S.dma("sp", gP, self.norm_g[layer, gpost, :].partition_broadcast(128), b_gP, writes=[b_gP])
        if gpre is not None:
            S.dma("sp", gN, self.norm_g[layer, gpre, :].partition_broadcast(128), b_gN, writes=[b_gN])
        k = M["srot"] = 1 - M.get("srot", 0)
        st, b_st = self.stat2[:, k * 32:(k + 1) * 32], self.b_stat2[k]

        def ev(ct, tt, ps, b_ps):
            S.op("dve", lambda e: e.tensor_tensor(out=XR[tt][:, ct * 512:(ct + 1) * 512], in0=ps, in1=gP[:, ct * 512:(ct + 1) * 512], op=ALU.mult),
                 reads=[b_ps, b_gP], writes=[b_XR[tt]])
            S.op("act", lambda e: e.activation(out=M["junk"][:, 0:512], in_=ps, func=AF.Square, accum_out=st[:, tt * 4 + ct:tt * 4 + ct + 1]), reads=[b_ps], writes=[M["b_junk"], b_st])
        def load_xo(tt):
            S.dma("sp", M["xo"][tt % 2], xold_dram[t0 + tt * 128:t0 + (tt + 1) * 128, :], M["b_xo"][tt % 2], reads=[b_xold], writes=[M["b_xo"][tt % 2]])
        load_xo(0)
        load_xo(1)
        self.tm_linear(srcT, b_src, KT, wd, 0, 4, 4, ev)
        S.op("dve", lambda e: e.tensor_reduce(out=st[:, 16:20], in_=st[:, 0:16].rearrange("p (t c) -> p t c", t=4), axis=AX.X, op=ALU.add), reads=[b_st], writes=[b_st])
        self.rstd_from_ss(st[:, 16:20], b_st, 4, 1.0 / D)
        for tt in range(4):
            xo, b_xo = M["xo"][tt % 2], M["b_xo"][tt % 2]
            S.op("dve", lambda e: e.scalar_tensor_tensor(out=XR[tt], in0=XR[tt], scalar=st[:, 16 + tt:17 + tt], in1=xo, op0=ALU.mult, op1=ALU.add),
                 reads=[b_XR[tt], b_st, b_xo], writes=[b_XR[tt]])
            if tt + 2 < 4:
                load_xo(tt + 2)
            S.dma("sp", xnew_dram[t0 + tt * 128:t0 + (tt + 1) * 128, :], XR[tt], b_XR[tt], reads=[b_XR[tt]], writes=[b_xnew])
            if gpre is not None:
                xn, b_xn = M["xn"][tt % 2], M["b_xn"][tt % 2]
                S.op("act", lambda e: e.activation(out=xn, in_=XR[tt], func=AF.Square, accum_out=st[:, 20 + tt:21 + tt]), reads=[b_XR[tt]], writes=[b_xn, b_st])
        if gpre is None:
            return
        self.rstd_from_ss(st[:, 20:24], b_st, 4, 1.0 / D)
        for tt in range(4):
            xn, b_xn = M["xn"][tt % 2], M["b_xn"][tt % 2]
            S.op("dve", lambda e: e.scalar_tensor_tensor(out=xn, in0=XR[tt], scalar=st[:, 20 + tt:21 + tt], in1=gN, op0=ALU.mult, op1=ALU.mult),
                 reads=[b_XR[tt], b_st, b_gN], writes=[b_xn])
            self.transpose_to(xn, b_xn, 16, dstT[:, :, tt * 128:(tt + 1) * 128], b_dstT)

    def phase_chain(self, s, layer, KT_mix, w_out_name, x_in, b_x_in, x_out, b_x_out):
        S = self.S
        S.barrier()
        if True:
            cstop = getattr(self, 'chain_stop', 9)
            M = {}
            M["XR"] = [self.A(i * 8 * KB, [128, D], F32) for i in range(4)]
            M["b_XR"] = [S.buf("XR") for i in range(4)]
            TA = self.A(32 * KB, [128, 16, 512], BF16)
            TB = self.A(48 * KB, [128, 16, 512], BF16)
            b_TA, b_TB = S.buf("TA"), S.buf("TB")
            ACTo = 64 * KB
            self.wbufs(112 * KB)
            M["misc"] = 144 * KB
            o = 160 * KB
            M["xo"] = [self.A(o + i * 8 * KB, [128, D], F32) for i in range(2)]
            M["b_xo"] = [S.buf("xo") for i in range(2)]
            o += 16 * KB
            M["xn"] = [self.A(o + i * 4 * KB, [128, D], BF16) for i in range(2)]
            M["b_xn"] = [S.buf("xn") for i in range(2)]
            o += 8 * KB
            M["junk"] = self.junk512
            M["b_junk"] = self.b_junk512
            M["gP"], M["gN"] = self.A(M["misc"], [128, D], F32), self.A(M["misc"] + 8 * KB, [128, D], F32)
            M["b_gP"], M["b_gN"] = S.buf("gP"), S.buf("gN")
            pT = [self.A(o + i * KB, [128, 512], BF16) for i in range(2)]
            b_pT = [S.buf("pT") for i in range(2)]
            o += 2 * KB
            rinv = self.A(o, [128, 512], F32)
            b_rinv = S.buf("rinv")
            o += 2 * KB
            sg = [self.A(o + i * 2 * KB, [128, 512], F32) for i in range(2)]
            b_sg = [S.buf("sg") for i in range(2)]
            o += 4 * KB
            b_ACT = S.buf("ACT")
            mk = self.A(ACTo, [128, 16, 256], BF16)
            mv = self.A(ACTo + 8 * KB, [128, 2, D], BF16)
            oT = self.A(ACTo + 16 * KB, [128, 16, 512], BF16)
            actT = self.A(ACTo, [128, NFT, 512], BF16)
            cat32 = self.A(ACTo, [128, 32, 512], BF16)
        for blk in range(getattr(self, 'chain_blocks', 4)):
            t0 = blk * 512
            if KT_mix == 16:
                if "nocat" not in getattr(self, "variant", ""):
                    S.dma("sp", TA, self.catT[0:16, :, t0:t0 + 512].rearrange("k p t -> p k t"), b_TA, reads=[self.b_catT], writes=[b_TA])
                src, b_src = TA, b_TA
            else:
                src = cat32
                b_src = b_ACT
                S.dma("sp", src, self.catT[0:32, :, t0:t0 + 512].rearrange("k p t -> p k t"), b_src, reads=[self.b_catT], writes=[b_src])
            self.resid_stage(src, b_src, KT_mix, self.w[w_out_name], layer, 1, x_in, b_x_in, t0, self.xs1, self.b_xs1, 2, TB, b_TB, M)
            if cstop < 2:
                continue
            b_mk = b_mv = b_oT = b_ACT
            S.dma("sp", mk, self.memk.rearrange("c p m -> p c m"), b_mk, reads=[self.b_memk], writes=[b_mk])
            S.dma("sp", mv, self.memv.rearrange("(a p) d -> p a d", p=128), b_mv, reads=[self.b_memv], writes=[b_mv])

            def ev_q(ctile, ci, ps, b_ps):
                self.copy(self.ev_eng(), TA[:, ctile, :], ps, [b_ps], [b_TA])
            self.fm_linear(TB, b_TB, self.w["xa_wq%d" % layer], 0, D, [(0, 512)], ev_q)
            xscale = 512.0 ** -0.5
            pT4, b_pT4 = [pT[0], pT[1], sg[1][:, 0:256].bitcast(BF16), sg[1][:, 256:512].bitcast(BF16)], [b_pT[0], b_pT[1], b_sg[1], b_sg[1]]
            rinv2, b_rinv2 = [rinv, sg[0]], [b_rinv, b_sg[0]]
            for h in range(4):
                pT, b_pT = pT4[(h % 2) * 2:(h % 2) * 2 + 2], b_pT4[(h % 2) * 2:(h % 2) * 2 + 2]
                rinv, b_rinv = rinv2[h % 2], b_rinv2[h % 2]
                for mt in range(2):
                    bk, b_bk = self.bank()
                    S.mm([(lambda e, dd=dd: e.matmul(bk, lhsT=mk[:, h * 4 + dd, mt * 128:(mt + 1) * 128], rhs=TA[:, h * 4 + dd, :], start=(dd == 0), stop=(dd == 3)))
                          for dd in range(4)], reads=[b_mk, b_TA], writes=[b_bk])
                    S.op("act", lambda e: e.activation(out=pT[mt], in_=bk, func=AF.Exp, scale=xscale), reads=[b_bk], writes=[b_pT[mt]])
                bk, b_bk = self.bank()
                S.mm([(lambda e, mt=mt: e.matmul(bk, lhsT=self.onesb, rhs=pT[mt], start=(mt == 0), stop=(mt == 1))) for mt in range(2)],
                     reads=[b_pT[0], b_pT[1], self.b_const], writes=[b_bk])
                S.op("dve", lambda e: e.reciprocal(out=rinv, in_=bk), reads=[b_bk], writes=[b_rinv])
                for ee in range(4):
                    bk, b_bk = self.bank()
                    S.mm([(lambda e, mt=mt: e.matmul(bk, lhsT=mv[:, mt, h * 512 + ee * 128:h * 512 + (ee + 1) * 128], rhs=pT[mt], start=(mt == 0), stop=(mt == 1)))
                          for mt in range(2)], reads=[b_mv, b_pT[0], b_pT[1]], writes=[b_bk])
                    S.op("dve", lambda e: e.tensor_tensor(out=oT[:, h * 4 + ee, :], in0=bk, in1=rinv, op=ALU.mult), reads=[b_bk, b_rinv], writes=[b_oT])
            if cstop < 3:
                continue
            pT, b_pT, rinv, b_rinv = pT4[0:2], b_pT4[0:2], rinv2[0], b_rinv2[0]
            self.resid_stage(oT, b_oT, 16, self.w["xa_wo%d" % layer], layer, 3, self.xs1, self.b_xs1, t0, self.xs2, self.b_xs2, 5, TB, b_TB, M)
            if cstop < 4:
                continue
            b_actT = b_ACT
            wgu = self.w["ffn_w_gu%d" % layer]
            for f2 in range(NFT // 2):
                i = self.wi % 2
                self.wi += 1
                wb, b_wb = self.wb[i], self.b_wb[i]
                S.dma("pool", wb[:, :, 0:256], wgu[:, f2 * 256:(f2 + 1) * 256].rearrange("(a p) c -> p a c", p=128), b_wb, writes=[b_wb])
                S.dma("pool", wb[:, :, 256:512], wgu[:, DFF + f2 * 256:DFF + (f2 + 1) * 256].rearrange("(a p) c -> p a c", p=128), b_wb, writes=[b_wb])
                for cc in range(2):
                    f = f2 * 2 + cc
                    bg, b_bg = self.bank()
                    S.mm([(lambda e, k=k: e.matmul(bg, lhsT=wb[:, k, cc * 128:(cc + 1) * 128], rhs=TB[:, k, :], start=(k == 0), stop=(k == 15))) for k in range(16)],
                         reads=[b_TB, b_wb], writes=[b_bg])
                    bu, b_bu = self.bank()
                    S.mm([(lambda e, k=k: e.matmul(bu, lhsT=wb[:, k, 256 + cc * 128:256 + (cc + 1) * 128], rhs=TB[:, k, :], start=(k == 0), stop=(k == 15))) for k in range(16)],
                         reads=[b_TB, b_wb], writes=[b_bu])
                    S.op("act", lambda e: e.activation(out=sg[f % 2], in_=bg, func=AF.Silu), reads=[b_bg], writes=[b_sg[f % 2]])
                    S.op("dve", lambda e: e.tensor_tensor(out=actT[:, f, :], in0=bu, in1=sg[f % 2], op=ALU.mult), reads=[b_bu, b_sg[f % 2]], writes=[b_actT])
            if cstop < 5:
                continue
            self.resid_stage(actT, b_actT, NFT, self.w["ffn_w_down%d" % layer], layer, 6, self.xs2, self.b_xs2, t0, x_out, b_x_out, None, None, None, M)

    def layer0(self, s, x_in, b_x_in, x_out, b_x_out):
        hT, b_hT = self.phase_A(x_in, b_x_in, 0)
        self.phase_L0proj(hT, b_hT)
        self.phase_ret()
        self.phase_att()
        self.phase_mem(s, 0)
        self.phase_chain(s, 0, 16, "ev_w_out", x_in, b_x_in, x_out, b_x_out)

    def l1_scratch(self):
        nc, S = self.nc, self.S
        if hasattr(self, "zs"):
            return
        ds = lambda n, s, dt=F32: (nc.dram_tensor(n, s, dt, kind="Internal").ap(), S.buf(n, persist=True))
        self.zs, self.b_zs = ds("zs", [T, 4096])
        self.xbcT, self.b_xbcT = ds("xbcT", [48, 128, T], BF16)
        self.dtv, self.b_dtv = ds("dtv", [T, 128])
        self.dta, self.b_dta = ds("dta", [T, 128])

    def phase_L1proj(self, hT, b_hT):
        S = self.S
        S.barrier()
        self.l1_scratch()
        self.nrot = 8
        wd = self.w["od_w_in"]
        self.wbufs(64 * KB)
        o = 96 * KB
        stg = [self.A(o + i * 2 * KB, [128, 512], F32) for i in range(2)]
        b_stg = [S.buf("stg") for i in range(2)]
        o += 4 * KB
        cnt = [0]

        def ev_z(ct, tt, ps, b_ps):
            i = cnt[0] % 2
            cnt[0] += 1
            S.op("act", lambda e: e.activation(out=stg[i], in_=ps, func=AF.Silu), reads=[b_ps], writes=[b_stg[i]])
            S.dma("sp", self.zs[tt * 128:(tt + 1) * 128, ct * 512:(ct + 1) * 512], stg[i], b_stg[i], reads=[b_stg[i]], writes=[self.b_zs])
        self.tm_linear(hT, b_hT, 16, wd, 0, 8, 16, ev_z)
        biasB = self.A(o, [128, 128], F32)
        aB = self.A(o + 512, [128, 128], F32)
        b_cb = S.buf("cb")
        S.dma("sp", biasB, self.od_dt_bias.partition_broadcast(128), b_cb, writes=[b_cb])
        S.dma("sp", aB, self.od_a_log.partition_broadcast(128), b_cb, writes=[b_cb])
        S.op("act", lambda e: e.activation(out=aB, in_=aB, func=AF.Exp), reads=[b_cb], writes=[b_cb])
        S.op("dve", lambda e: e.tensor_scalar(out=aB, in0=aB, scalar1=-1.0, scalar2=None, op0=ALU.mult), reads=[b_cb], writes=[b_cb])
        o += KB
        dtt = [self.A(o + i * KB, [128, 2, 128], F32) for i in range(2)]
        b_dtt = [S.buf("dtt") for i in range(2)]
        o += 2 * KB

        def ev_dt(ct, tt, ps, b_ps):
            i = tt % 2
            d, b_d = dtt[i], b_dtt[i]
            S.op("dve", lambda e: e.tensor_tensor(out=d[:, 0, :], in0=ps, in1=biasB, op=ALU.add), reads=[b_ps, b_cb], writes=[b_d])
            S.op("act", lambda e: e.activation(out=d[:, 0, :], in_=d[:, 0, :], func=AF.Exp), reads=[b_d], writes=[b_d])
            S.op("act", lambda e: e.activation(out=d[:, 0, :], in_=d[:, 0, :], func=AF.Ln, bias=1.0), reads=[b_d], writes=[b_d])
            S.op("dve", lambda e: e.tensor_tensor(out=d[:, 1, :], in0=d[:, 0, :], in1=aB, op=ALU.mult), reads=[b_d, b_cb], writes=[b_d])
            S.dma("sp", self.dtv[tt * 128:(tt + 1) * 128, :], d[:, 0, :], b_d, reads=[b_d], writes=[self.b_dtv])
            S.dma("sp", self.dta[tt * 128:(tt + 1) * 128, :], d[:, 1, :], b_d, reads=[b_d], writes=[self.b_dta])
        self.tm_linear(hT, b_hT, 16, wd, 4096 + 6144, 1, 16, ev_dt, cw=128)
        cw_t = self.A(o, [128, 48, 5], F32)
        cb_t = self.A(o + KB, [128, 48], F32)
        b_cw = S.buf("cw")
        S.dma("sp", cw_t, self.od_conv_w, b_cw, writes=[b_cw])
        S.dma("sp", cb_t, self.od_conv_b, b_cw, writes=[b_cw])
        o += 2 * KB
        raw = [self.A(o + i * 8224, [128, T + 4], F32) for i in range(2)]
        b_raw = [S.buf("raw") for i in range(2)]
        o += 2 * 8224
        acc = [self.A(o + i * 8 * KB, [128, T], F32) for i in range(2)]
        b_acc = [S.buf("acc") for i in range(2)]
        o += 16 * KB
        cvo = [self.A(o + i * 4 * KB, [128, T], BF16) for i in range(2)]
        b_cvo = [S.buf("cvo") for i in range(2)]
        for i in range(2):
            S.op("pool", lambda e, i=i: e.memset(raw[i][:, 0:2], 0.0), writes=[b_raw[i]])
            S.op("pool", lambda e, i=i: e.memset(raw[i][:, T + 2:T + 4], 0.0), writes=[b_raw[i]])

        def ev_x(ctile, ci, ps, b_ps):
            i = ctile % 2
            self.copy("act", raw[i][:, 2 + ci * 512:2 + (ci + 1) * 512], ps, [b_ps], [b_raw[i]])
            if ci == 3:
                a, b_a = acc[i], b_acc[i]
                S.op("dve", lambda e: e.tensor_scalar(out=a, in0=raw[i][:, 0:T], scalar1=cw_t[:, ctile, 0:1], scalar2=None, op0=ALU.mult),
                     reads=[b_raw[i], b_cw], writes=[b_a])
                for k in range(1, 5):
                    S.op("dve", lambda e, k=k: e.scalar_tensor_tensor(out=a, in0=raw[i][:, k:k + T], scalar=cw_t[:, ctile, k:k + 1], in1=a, op0=ALU.mult, op1=ALU.add),
                         reads=[b_raw[i], b_cw, b_a], writes=[b_a])
                S.op("act", lambda e: e.activation(out=cvo[i], in_=a, func=AF.Silu, bias=cb_t[:, ctile:ctile + 1]), reads=[b_a, b_cw], writes=[b_cvo[i]])
                S.dma("sp", self.xbcT[ctile], cvo[i], b_cvo[i], reads=[b_cvo[i]], writes=[self.b_xbcT])
        self.fm_linear(hT, b_hT, wd, 4096, 6144, [(i * 512, 512) for i in range(4)], ev_x)

    def phase_ssd(self):
        S = self.S
        S.barrier()
        self.nrot = 4
        bf = lambda n: S.buf(n)
        o = 0
        tri = self.A(o, [128, 4, 128], F32); o += 2 * KB
        trib = self.A(o, [128, 4, 128], BF16); o += KB
        b_tri = bf("tri")
        S.dma("sp", tri, self.c_tri.rearrange("k p l -> p k l"), b_tri, writes=[b_tri])
        S.op("dve", lambda e: e.tensor_copy(out=trib, in_=tri), reads=[b_tri], writes=[b_tri])
        dtv = self.A(o, [128, 16, 128], F32); o += 8 * KB
        o_dta = o
        dta = self.A(o, [128, 16, 128], F32); o += 8 * KB
        dtmp = self.A(o, [128, 16, 128], F32); o += 8 * KB
        dth = self.A(o, [128, 16, 128], BF16); o += 4 * KB
        dtl = self.A(o, [128, 16, 128], BF16); o += 4 * KB
        dtl32 = self.A(o, [128, 16, 128], F32); o += 8 * KB
        b_dt = bf("dt")
        S.dma("sp", dtv, self.dtv.rearrange("(a p) c -> p a c", p=128), b_dt, reads=[self.b_dtv], writes=[b_dt])
        S.dma("sp", dta, self.dta.rearrange("(a p) c -> p a c", p=128), b_dt, reads=[self.b_dta], writes=[b_dt])
        S.op("dve", lambda e: e.tensor_copy(out=dth, in_=dta), reads=[b_dt], writes=[b_dt])
        S.op("dve", lambda e: e.tensor_tensor(out=dtmp, in0=dta, in1=dth, op=ALU.subtract), reads=[b_dt], writes=[b_dt])
        S.op("dve", lambda e: e.tensor_copy(out=dtl, in_=dtmp), reads=[b_dt], writes=[b_dt])
        S.op("dve", lambda e: e.tensor_copy(out=dtl32, in_=dtl), reads=[b_dt], writes=[b_dt])
        dB = self.A(o, [128, 64], F32); o += 256
        ngB = self.A(o, [128, 512], F32); o += 2 * KB
        b_dB = bf("dB")
        S.dma("sp", dB, self.od_d.partition_broadcast(128), b_dB, writes=[b_dB])
        S.barrier()
        o_reuse = o_dta
        xsT = self.A(o, [128, 4, T], BF16); o += 16 * KB
        xs = self.A(o, [128, 16, 512], BF16); o += 16 * KB
        Bt = self.A(o, [128, 16, 128], BF16); o += 4 * KB
        BT = self.A(o, [128, T], BF16); o += 4 * KB
        CT = self.A(o, [128, T], BF16); o += 4 * KB
        xdt = [self.A(o + i * 16 * KB, [128, 16, 512], BF16) for i in range(2)]
        zg = self.A(o, [128, 16, 512], F32); o += 32 * KB
        y = self.A(o, [128, 16, 512], F32); o += 32 * KB
        NR = 3
        Ah = [self.A(o_reuse + i * 2 * KB, [128, 8, 128], BF16) for i in range(NR)]; o_reuse += NR * 2 * KB
        Al = [self.A(o_reuse + i * 2 * KB, [128, 8, 128], BF16) for i in range(NR)]; o_reuse += NR * 2 * KB
        assert o_reuse <= o_dta + 16 * KB
        Ee = [self.A(o + i * 4 * KB, [128, 1024], F32) for i in range(2)]; o += 2 * 4 * KB
        MT = [self.A(o + i * 2 * KB, [128, 8, 128], BF16) for i in range(NR)]; o += NR * 2 * KB
        cbm = [self.A(o + i * 512, [128, 128], F32) for i in range(2)]; o += 2 * 512
        xd = [self.A(o + i * KB, [128, 512], BF16) for i in range(NR)]; o += NR * KB
        t1 = [self.A(o + i * 2 * KB, [128, 512], F32) for i in range(2)]; o += 2 * 2 * KB
        st32 = [self.A(o + i * 2 * KB, [128, 512], F32) for i in range(2)]; o += 4 * KB
        stb = [self.A(o + i * KB, [128, 512], BF16) for i in range(2)]; o += 2 * KB
        sm = [self.A(o + i * 128, [128, 24], F32) for i in range(NR)]; o += 512
        yb = self.A(o, [128, 512], BF16); o += KB
        yst = [self.A(o + i * 4 * KB, [128, 4, 512], BF16) for i in range(2)]; o += 8 * KB
        junk = self.A(o, [128, 512], BF16); o += KB
        assert o <= 192 * KB, o
        b_Ah, b_Al, b_MT, b_xd, b_sm = [[bf("w") for i in range(NR)] for _ in range(5)]
        b_Ee, b_cbm, b_t1 = [[bf("w") for i in range(2)] for _ in range(3)]
        b_yst = [bf("yst") for i in range(2)]
        b_st = [bf("st") for i in range(2)]
        b_yb, b_junk = bf("yb"), bf("junk")
        b_xsT, b_xs, b_Bt, b_BT, b_CT, b_X = [bf(n) for n in ("xsT", "xs", "Bt", "BT", "CT", "xdt_zg")]
        b_y = [bf("y") for tt in range(16)]
        Ule, Uge, Sgt, Slt = 0, 1, 2, 3
        h3 = lambda ap: ap.rearrange("p (h q) -> p h q", h=8)
        it = [0]
        for g in range(8):
            S.dma("sp", xsT, self.xbcT[g * 4:(g + 1) * 4].rearrange("c p t -> p c t"), b_xsT, reads=[self.b_xbcT], writes=[b_xsT])
            S.dma("sp", BT, self.xbcT[32 + g], b_BT, reads=[self.b_xbcT], writes=[b_BT])
            S.dma("sp", CT, self.xbcT[40 + g], b_CT, reads=[self.b_xbcT], writes=[b_CT])
            S.dma("sp", ngB, self.od_norm_g[g * 512:(g + 1) * 512].partition_broadcast(128), b_dB, writes=[b_dB])
            for tt in range(16):
                bk, b_bk = self.bank()
                pv = bk.bitcast(BF16)[:, 0:512].rearrange("p (a b) -> p a b", a=4)
                for c in range(4):
                    S.op("pe", lambda e, c=c: e.transpose(out=pv[:, c, :], in_=xsT[:, c, tt * 128:(tt + 1) * 128], identity=self.identb),
                         reads=[b_xsT, self.b_const], writes=[b_bk])
                self.copy(self.ev_eng(), xs[:, tt, :], bk.bitcast(BF16)[:, 0:512], [b_bk], [b_xs])
                bk, b_bk = self.bank()
                pv = bk.bitcast(BF16)[:, 0:128]
                S.op("pe", lambda e: e.transpose(out=pv, in_=BT[:, tt * 128:(tt + 1) * 128], identity=self.identb), reads=[b_BT, self.b_const], writes=[b_bk])
                self.copy(self.ev_eng(), Bt[:, tt, :], pv, [b_bk], [b_Bt])
            for tt in range(16):
                x3 = h3(xs[:, tt, :])
                for dr in range(2):
                    S.op("pool" if dr else "dve", lambda e, dr=dr: e.tensor_tensor(out=h3(xdt[dr][:, tt, :]), in0=x3,
                                                                                   in1=dtv[:, tt, dr * 64 + g * 8:dr * 64 + g * 8 + 8].unsqueeze(2).broadcast_to([128, 8, 64]), op=ALU.mult),
                         reads=[b_xs, b_dt], writes=[b_X])
                S.op("dve", lambda e: e.tensor_tensor(out=h3(y[:, tt, :]), in0=x3, in1=dB[:, g * 8:(g + 1) * 8].unsqueeze(2).broadcast_to([128, 8, 64]), op=ALU.mult),
                     reads=[b_xs, b_dB], writes=[b_y[tt]])

            def indep(dr, ci, tt, step):
                i, j = step % NR, step % 2
                U, Sx = (Ule, Sgt) if dr == 0 else (Uge, Slt)
                hsl = slice(dr * 64 + g * 8, dr * 64 + g * 8 + 8)
                S.op("pool", lambda e: e.tensor_tensor(out=Ah[i], in0=trib[:, U:U + 1, :].broadcast_to([128, 8, 128]),
                                                       in1=dth[:, tt, hsl].unsqueeze(2).broadcast_to([128, 8, 128]), op=ALU.mult),
                     reads=[b_tri, b_dt], writes=[b_Ah[i]])
                for h in range(8):
                    S.op("act", lambda e, h=h: e.activation(out=Al[i][:, h, :], in_=trib[:, U, :], func=AF.Copy, scale=dtl32[:, tt, hsl.start + h:hsl.start + h + 1]),
                         reads=[b_tri, b_dt], writes=[b_Al[i]])
                for half in range(2):
                    bk, b_bk = self.banks[4 + half], self.b_bank[4 + half]
                    S.mm([lambda e: e.matmul(bk, lhsT=trib[:, Sx, :], rhs=Ah[i][:, half * 4:(half + 1) * 4, :].rearrange("p a b -> p (a b)"), start=True, stop=False),
                          lambda e: e.matmul(bk, lhsT=trib[:, Sx, :], rhs=Al[i][:, half * 4:(half + 1) * 4, :].rearrange("p a b -> p (a b)"), start=False, stop=True)],
                         reads=[b_tri, b_Ah[i], b_Al[i]], writes=[b_bk])
                    S.op("act", lambda e: e.activation(out=Ee[j][:, half * 512:(half + 1) * 512], in_=bk, func=AF.Exp), reads=[b_bk], writes=[b_Ee[j]])
                b6, b_b6 = self.banks[6], self.b_bank[6]
                fns = []
                for k, lt in enumerate((trib[:, U, :], trib[:, Sx, :], self.onesb)):
                    fns.append(lambda e, k=k, lt=lt: e.matmul(b6[:, k * 8:(k + 1) * 8], lhsT=lt, rhs=dth[:, tt, hsl], start=(k == 0), stop=False, skip_group_check=True))
                    fns.append(lambda e, k=k, lt=lt: e.matmul(b6[:, k * 8:(k + 1) * 8], lhsT=lt, rhs=dtl[:, tt, hsl], start=False, stop=True, skip_group_check=True))
                fns.append(lambda e: e.matmul(b6[:, 128:256], lhsT=BT[:, tt * 128:(tt + 1) * 128], rhs=CT[:, tt * 128:(tt + 1) * 128], start=False, stop=True, skip_group_check=True))
                S.mm(fns, reads=[b_tri, b_dt, self.b_const, b_BT, b_CT], writes=[b_b6])
                S.op("act", lambda e: e.activation(out=sm[i], in_=b6[:, 0:24], func=AF.Exp), reads=[b_b6], writes=[b_sm[i]])
                S.op("dve", lambda e: e.tensor_tensor(out=cbm[j], in0=b6[:, 128:256], in1=tri[:, U, :], op=ALU.mult), reads=[b_b6, b_tri], writes=[b_cbm[j]])
                S.op("dve", lambda e: e.tensor_tensor(out=MT[i], in0=Ee[j].rearrange("p (a b) -> p a b", a=8), in1=cbm[j].unsqueeze(1).broadcast_to([128, 8, 128]), op=ALU.mult),
                     reads=[b_Ee[j], b_cbm[j]], writes=[b_MT[i]])
                bY, b_bY = self.bank()
                S.mm([(lambda e, h=h: e.matmul(bY[:, h * 64:(h + 1) * 64], lhsT=MT[i][:, h, :], rhs=xdt[dr][:, tt, h * 64:(h + 1) * 64], start=(h == 0), stop=True, skip_group_check=True))
                      for h in range(8)], reads=[b_MT[i], b_X], writes=[b_bY])
                S.op("dve", lambda e: e.tensor_tensor(out=y[:, tt, :], in0=y[:, tt, :], in1=bY, op=ALU.add), reads=[b_y[tt], b_bY], writes=[b_y[tt]])
                if ci < 15:
                    S.op("pool", lambda e: e.tensor_tensor(out=h3(xd[i]), in0=h3(xdt[dr][:, tt, :]), in1=sm[i][:, 8:16].unsqueeze(2).broadcast_to([128, 8, 64]), op=ALU.mult),
                         reads=[b_X, b_sm[i]], writes=[b_xd[i]])

            def chain(dr, ci, tt, step):
                i, j = step % NR, step % 2
                if ci > 0:
                    bO, b_bO = self.bank()
                    S.mm([lambda e: e.matmul(bO, lhsT=CT[:, tt * 128:(tt + 1) * 128], rhs=stb[dr], start=True, stop=True)], reads=[b_CT, b_st[dr]], writes=[b_bO])
                    S.op("dve", lambda e: e.tensor_tensor(out=h3(t1[j]), in0=h3(bO), in1=sm[i][:, 0:8].unsqueeze(2).broadcast_to([128, 8, 64]), op=ALU.mult),
                         reads=[b_bO, b_sm[i]], writes=[b_t1[j]])
                    S.op("dve", lambda e: e.tensor_tensor(out=y[:, tt, :], in0=y[:, tt, :], in1=t1[j], op=ALU.add), reads=[b_y[tt], b_t1[j]], writes=[b_y[tt]])
                if ci < 15:
                    b7, b_b7 = self.banks[7], self.b_bank[7]
                    S.mm([lambda e: e.matmul(b7, lhsT=Bt[:, tt, :], rhs=xd[i], start=True, stop=True)], reads=[b_Bt, b_xd[i]], writes=[b_b7])
                    if ci == 0:
                        S.op("dve", lambda e: e.tensor_copy(out=st32[dr], in_=b7), reads=[b_b7], writes=[b_st[dr]])
                    else:
                        S.op("dve", lambda e: e.tensor_tensor(out=h3(st32[dr]), in0=h3(st32[dr]), in1=sm[i][:, 16:24].unsqueeze(2).broadcast_to([128, 8, 64]), op=ALU.mult),
                             reads=[b_st[dr], b_sm[i]], writes=[b_st[dr]])
                        S.op("dve", lambda e: e.tensor_tensor(out=st32[dr], in0=st32[dr], in1=b7, op=ALU.add), reads=[b_st[dr], b_b7], writes=[b_st[dr]])
                    S.op("dve", lambda e: e.tensor_copy(out=stb[dr], in_=st32[dr]), reads=[b_st[dr]], writes=[b_st[dr]])
            steps = []
            for ci in range(16):
                steps.append((0, ci, ci))
                steps.append((1, ci, 15 - ci))
            base = it[0]
            indep(*steps[0], base)
            for k in range(len(steps)):
                if k + 1 < len(steps):
                    indep(*steps[k + 1], base + k + 1)
                chain(*steps[k], base + k)
            it[0] = base + len(steps)
            S.dma("sp", zg, self.zs[:, g * 512:(g + 1) * 512].rearrange("(a p) c -> p a c", p=128), b_X, reads=[self.b_zs], writes=[b_X])
            for tt in range(16):
                ss, b_ss = self.st(1)
                S.op("pool", lambda e: e.tensor_tensor(out=y[:, tt, :], in0=y[:, tt, :], in1=zg[:, tt, :], op=ALU.mult), reads=[b_y[tt], b_X], writes=[b_y[tt]])
                S.op("act", lambda e: e.activation(out=junk, in_=y[:, tt, :], func=AF.Square, accum_out=ss), reads=[b_y[tt]], writes=[b_junk, b_ss])
                self.rstd_from_ss(ss, b_ss, 1, 1.0 / 512)
                S.op("dve", lambda e: e.scalar_tensor_tensor(out=yb, in0=y[:, tt, :], scalar=ss, in1=ngB, op0=ALU.mult, op1=ALU.mult), reads=[b_y[tt], b_ss, b_dB], writes=[b_yb])
                si = (tt // 4) % 2
                self.transpose_to(yb, b_yb, 4, yst[si][:, :, (tt % 4) * 128:(tt % 4 + 1) * 128], b_yst[si])
                if tt % 4 == 3:
                    t0 = (tt // 4) * 512
                    S.dma("sp", self.catT[g * 4:(g + 1) * 4, :, t0:t0 + 512].rearrange("k p t -> p k t"), yst[si], b_yst[si], reads=[b_yst[si]], writes=[self.b_catT])
        self.nrot = 8

    def layer1(self, s, x_in, b_x_in, x_out, b_x_out):
        hT, b_hT = self.phase_A(x_in, b_x_in, 1)
        self.phase_L1proj(hT, b_hT)
        self.phase_ssd()
        self.phase_mem(s, 1)
        self.phase_chain(s, 1, 32, "od_w_out", x_in, b_x_in, x_out, b_x_out)


THETA = 10000.0
def host_consts():
    c = {}
    c["c_ident"] = np.eye(128, dtype=np.float32)
    t = np.arange(2048, dtype=np.float32)
    inv = (THETA ** (-np.arange(0, 256, 2, dtype=np.float32) / 256)).astype(np.float32)
    ang = (t[None, :] * inv[:, None]).astype(np.float32)
    c["c_rope1"] = np.stack([np.cos(ang), np.sin(ang)]).astype(np.float32)
    inv2 = (THETA ** (-np.arange(0, 64, 2, dtype=np.float32) / 64)).astype(np.float32)
    row = (np.arange(2048) // 64).astype(np.float32); col = (np.arange(2048) % 64).astype(np.float32)
    ar = (row[:, None] * inv2[None, :]).astype(np.float32); ac = (col[:, None] * inv2[None, :]).astype(np.float32)
    COS = np.concatenate([np.cos(ar), np.cos(ar), np.cos(ac), np.cos(ac)], axis=1)
    SINS = np.concatenate([-np.sin(ar), np.sin(ar), -np.sin(ac), np.sin(ac)], axis=1)
    c["c_rope2"] = np.stack([COS, SINS]).astype(np.float32)
    heads = np.arange(4, dtype=np.float32)
    lgf = np.log1p(-np.exp2(-(5.0 + heads))).astype(np.float32)
    lgb = np.log1p(-np.exp2(-(5.5 + heads))).astype(np.float32)
    dd = np.arange(-15, 16)[None, :, None] * 128 + np.arange(128)[None, None, :] - np.arange(128)[:, None, None]
    dd = dd.astype(np.float64)
    m = np.zeros((4, 128, 31, 128), np.float64)
    for h in range(4):
        m[h] = np.where(dd >= 0, np.exp(np.float64(lgf[h]) * np.maximum(dd, 0)), np.exp(np.float64(lgb[h]) * np.maximum(-dd, 0)))
    c["c_rmask"] = (m / 16.0).astype(np.float32).reshape(4, 128, 31 * 128)
    j = np.arange(128)[:, None]; l = np.arange(128)[None, :]
    c["c_tri"] = np.stack([(j <= l), (j >= l), (j > l), (j < l)]).astype(np.float32)
    return c


NS = 3
CORE_SEQS = [[0, 1, 2], [3, 4, 5], [6, 7, 8], [9, 10, 11], [12, 13], [14, 15], [16, 17], [18, 19]]
WANT = {"ev_w_in", "ev_w_out", "od_w_in", "od_w_out", "xa_wq0", "xa_wkv0", "xa_wo0", "ffn_w_gu0", "ffn_w_down0",
        "xa_wq1", "xa_wkv1", "xa_wo1", "ffn_w_gu1", "ffn_w_down1"}
_NC = None


def build_program():
    nc = bass.Bass("TRN2", target_bir_lowering=False)
    B = Builder(nc, NS, WANT)
    S = B.S
    bx = S.buf("xin", persist=True)
    for s in range(NS):
        B.layer0(s, B.x[s], bx, B.xl, B.b_xl)
        B.layer1(s, B.xl, B.b_xl, B.y[s], B.b_y)
    S.barrier()
    return nc


def kernel(**inputs):
    global _NC
    f = lambda a: np.ascontiguousarray(np.asarray(a, dtype=np.float32))
    xs = [inputs["x_prompt"][i] for i in range(16)] + [inputs["x_sample"][i] for i in range(4)]
    ms = [inputs["mem_prompt"][i] for i in range(16)] + [inputs["mem_sample"][i] for i in range(4)]
    shared = dict(
        norm_g=f(inputs["norm_g"]),
        ev_w_in=f(inputs["ev_w_in"][0]), ev_w_out=f(inputs["ev_w_out"][0]),
        od_w_in=f(inputs["od_w_in"][0]), od_w_out=f(inputs["od_w_out"][0]),
        qk_gain=f(np.stack([inputs["ev_q_gain"][0], inputs["ev_k_gain"][0]])),
        od_conv_w=f(np.asarray(inputs["od_conv_w"][0]).reshape(5, 48, 128).transpose(2, 1, 0)),
        od_conv_b=f(np.asarray(inputs["od_conv_b"][0]).reshape(48, 128).T),
        od_a_log=f(np.asarray(inputs["od_a_log"][0]).reshape(128)),
        od_dt_bias=f(np.asarray(inputs["od_dt_bias"][0]).reshape(128)),
        od_d=f(inputs["od_d"][0]), od_norm_g=f(inputs["od_norm_g"][0]),
    )
    for l in range(2):
        for nm in ("xa_wq", "xa_wkv", "xa_wo", "ffn_w_gu", "ffn_w_down"):
            shared["%s%d" % (nm, l)] = f(inputs[nm][l])
    shared.update(host_consts())
    in_maps = []
    for c in range(8):
        ids = list(CORE_SEQS[c])
        while len(ids) < NS:
            ids.append(ids[0])
        m = dict(shared)
        m["x"] = f(np.stack([xs[i] for i in ids]))
        m["mem"] = f(np.stack([ms[i] for i in ids]))
        in_maps.append(m)
    if _NC is None:
        _NC = build_program()
    res = run_bass_kernel_spmd(_NC, in_maps, core_ids=list(range(8)))
    outs = [None] * 20
    for c in range(8):
        y = np.asarray(res.results[c]["y"], dtype=np.float32)
        for slot, i in enumerate(CORE_SEQS[c]):
            outs[i] = y[slot]
    y_prompt = np.stack(outs[:16]).astype(np.float32)
    y_sample = np.stack(outs[16:]).astype(np.float32)
    return (y_prompt, y_sample)
```

```python
import numpy as np
import ml_dtypes
import concourse.bass as bass
import concourse.mybir as mybir
from concourse.bass_utils import run_bass_kernel_spmd

F32 = mybir.dt.float32
BF16 = mybir.dt.bfloat16
AF = mybir.ActivationFunctionType
ALU = mybir.AluOpType
AX = mybir.AxisListType

T = 2048
D = 2048
NT = 16
EPS = 1e-6
LIM = 30000


class DSem:
    __slots__ = ("sem", "cnt")

    def __init__(self, sem):
        self.sem = sem
        self.cnt = 0


class Buf:
    __slots__ = ("name", "w", "r", "ds", "excl")

    def __init__(self, name):
        self.name = name
        self.w = {}
        self.r = {}
        self.ds = None
        self.excl = False


class Sched:
    def __init__(self, nc):
        self.nc = nc
        self.E = {"pe": nc.tensor, "act": nc.scalar, "dve": nc.vector, "pool": nc.gpsimd, "sp": nc.sync}
        self.ctr = {}
        self.nsem = 0
        for k in ("pe", "act", "dve", "pool"):
            self._newctr(k)
        self.waited = {e: {} for e in self.E}
        self.free_ds = []
        self.live_ds = []
        self.persist = []
        self.ninst = 0
        self.nwait = 0

    def _sem(self, name):
        self.nsem += 1
        return self.nc.alloc_semaphore(name="%s_%d" % (name, self.nsem))

    def _newctr(self, k):
        self.ctr[k] = [self._sem("c" + k), 0]

    def buf(self, name="b", persist=False):
        b = Buf(name)
        if persist:
            self.persist.append(b)
        return b

    def _ensure(self, e, sem, val, ds):
        if ds is not None:
            val = max(val, ds.cnt)
        key = id(sem)
        if self.waited[e].get(key, 0) >= val:
            return
        self.E[e].wait_ge(sem, val)
        self.waited[e][key] = val
        self.nwait += 1

    def _deps(self, e, reads, writes):
        own = self.ctr[e][0] if e in self.ctr else None
        for b in reads:
            for (sem, ds), v in b.w.items():
                if sem is own and e == "pe":
                    continue
                self._ensure(e, sem, v, ds)
            if b.excl:
                for (sem, ds), v in b.r.items():
                    if sem is not own:
                        self._ensure(e, sem, v, ds)
        for b in writes:
            for d in (b.w, b.r):
                for (sem, ds), v in d.items():
                    if sem is own and e == "pe":
                        continue
                    self._ensure(e, sem, v, ds)

    def _record(self, ev, val, reads, writes):
        for b in writes:
            b.w = {ev: val}
            b.r = {}
        for b in reads:
            if b in writes:
                continue
            b.r[ev] = max(b.r.get(ev, 0), val)

    def _tick(self, e, ins):
        c = self.ctr[e]
        if c[1] >= LIM:
            self._newctr(e)
            c = self.ctr[e]
        c[1] += 1
        ins.then_inc(c[0], 1)
        return (c[0], None), c[1]

    def op(self, e, fn, reads=(), writes=()):
        self._deps(e, reads, writes)
        ins = fn(self.E[e])
        ev, val = self._tick(e, ins)
        self._record(ev, val, reads, writes)
        self.ninst += 1
        return ins

    def mm(self, fns, reads=(), writes=()):
        self._deps("pe", reads, writes)
        ins = None
        for fn in fns:
            ins = fn(self.E["pe"])
        ev, val = self._tick("pe", ins)
        self._record(ev, val, reads, writes)
        self.ninst += len(fns)

    def dma(self, q, out, in_, sb, reads=(), writes=(), **kw):
        self._deps(q, reads, writes)
        if sb.ds is None or sb.ds.cnt + 16 > LIM or sb.ds not in self.live_ds:
            sb.ds = self.free_ds.pop() if self.free_ds else None
            if sb.ds is None or sb.ds.cnt + 16 > LIM:
                sb.ds = DSem(self._sem("d"))
                self.all_ds = getattr(self, "all_ds", []) + [sb.ds]
            self.live_ds.append(sb.ds)
        ds = sb.ds
        ins = self.E[q].dma_start(out=out, in_=in_, **kw)
        ds.cnt += 16
        ins.then_inc(ds.sem, 16)
        self._record((ds.sem, ds), ds.cnt, reads, writes)
        self.ninst += 1
        return ins

    def barrier(self):
        for e in self.E:
            for k, c in self.ctr.items():
                if k != e and c[1] > 0:
                    self._ensure(e, c[0], c[1], None)
            for ds in getattr(self, "all_ds", []):
                self._ensure(e, ds.sem, ds.cnt, ds)
        self.free_ds.extend(d for d in self.live_ds if d.cnt + 16 <= LIM)
        self.live_ds = []
        for b in self.persist:
            b.w = {}
            b.r = {}


KB = 1024
RET_H, ATT_H, KV_H = 4, 8, 2
DFF = 5632
NFT = DFF // 128


class KB_:
    pass


class Builder:
    def __init__(self, nc, NS, want):
        self.nc = nc
        self.S = Sched(nc)
        self.NS = NS
        self.uid = 0
        self.rot = 0
        self.erot = 0
        S = self.S
        di = lambda n, s, dt=F32: nc.dram_tensor(n, s, dt, kind="ExternalInput").ap()
        ds = lambda n, s, dt=F32: (nc.dram_tensor(n, s, dt, kind="Internal").ap(), S.buf(n, persist=True))
        self.x = di("x", [NS, T, D])
        self.mem = di("mem", [NS, 256, D])
        self.y = nc.dram_tensor("y", [NS, T, D], F32, kind="ExternalOutput").ap()
        self.b_y = S.buf("y", persist=True)
        self.norm_g = di("norm_g", [2, 7, D])
        self.w = {}
        shapes = dict(ev_w_in=[D, 5632], ev_w_out=[D, D], od_w_in=[D, 10368], od_w_out=[4096, D],
                      xa_wq0=[D, D], xa_wkv0=[D, 2 * D], xa_wo0=[D, D], ffn_w_gu0=[D, 2 * DFF], ffn_w_down0=[DFF, D],
                      xa_wq1=[D, D], xa_wkv1=[D, 2 * D], xa_wo1=[D, D], ffn_w_gu1=[D, 2 * DFF], ffn_w_down1=[DFF, D])
        for k, s in shapes.items():
            if k in want:
                self.w[k] = di(k, s)
        self.qk_gain = di("qk_gain", [2, 128])
        self.c_ident = di("c_ident", [128, 128])
        self.c_rope1 = di("c_rope1", [2, 128, T])
        self.c_rope2 = di("c_rope2", [2, T, 128])
        self.c_rmask = di("c_rmask", [RET_H, 128, 31 * 128])
        self.c_tri = di("c_tri", [4, 128, 128])
        if "od_w_in" in want:
            self.od_conv_w = di("od_conv_w", [128, 48, 5])
            self.od_conv_b = di("od_conv_b", [128, 48])
            self.od_a_log = di("od_a_log", [128])
            self.od_dt_bias = di("od_dt_bias", [128])
            self.od_d = di("od_d", [64])
            self.od_norm_g = di("od_norm_g", [4096])
        self.rqT, self.b_rqT = ds("rqT", [RET_H, 2, 128, T], BF16)
        self.rkT, self.b_rkT = ds("rkT", [RET_H, 2, 128, T], BF16)
        self.rv, self.b_rv = ds("rv", [T, 1024], BF16)
        self.rg, self.b_rg = ds("rg", [T, 1024], F32)
        self.aqT, self.b_aqT = ds("aqT", [ATT_H, 128, T], BF16)
        self.akT, self.b_akT = ds("akT", [KV_H, 128, T], BF16)
        self.av, self.b_av = ds("av", [T, 256], BF16)
        self.catT, self.b_catT = ds("catT", [32, 128, T], BF16)
        self.xs1, self.b_xs1 = ds("xs1", [T, D])
        self.xs2, self.b_xs2 = ds("xs2", [T, D])
        self.xl, self.b_xl = ds("xl", [T, D])
        self.memk, self.b_memk = ds("memk", [16, 128, 256], BF16)
        self.memv, self.b_memv = ds("memv", [256, D], BF16)
        al = lambda n, s, dt: nc.alloc_sbuf_tensor(n, s, dt).ap()
        self.identb = al("identb", [128, 128], BF16)
        self.onesb = al("onesb", [128, 128], BF16)
        self.idf = al("idf", [128, 128], F32)
        self.b_const = S.buf("const")
        self.stat = al("stat", [128, 256], F32)
        self.b_stat = [S.buf("stat%d" % i) for i in range(32)]
        self.srot = 0
        S.dma("sp", self.idf, self.c_ident, self.b_const, writes=[self.b_const])
        S.op("dve", lambda e: e.tensor_copy(out=self.identb, in_=self.idf), reads=[self.b_const], writes=[self.b_const])
        S.op("dve", lambda e: e.memset(self.onesb, 1.0), writes=[self.b_const])
        self.stat2 = al("stat2", [128, 64], F32)
        self.b_stat2 = [S.buf("stat2_%d" % i) for i in range(2)]
        self.junk512 = al("junk512", [128, 512], BF16)
        self.b_junk512 = S.buf("junk512")
        self.banks = [nc.alloc_psum_tensor("bank%d" % i, [128, 512], F32).ap() for i in range(8)]
        self.b_bank = [S.buf("bank%d" % i) for i in range(8)]
        for b in self.b_bank:
            b.excl = True
        self.base = nc.sbuf_base
        assert nc.sbuf_bytes_remaining >= 12 * 16 * KB, nc.sbuf_bytes_remaining

    def A(self, off, shape, dt):
        self.uid += 1
        return self.nc.alloc_sbuf_tensor_at("a%d" % self.uid, shape, dt, offset=self.base + off).ap()

    def st(self, n=1):
        i = self.srot % 32
        self.srot += 1
        return self.stat[:, i * 8:i * 8 + n], self.b_stat[i]

    def bank(self):
        i = self.rot % getattr(self, "nrot", 8)
        self.rot += 1
        return self.banks[i], self.b_bank[i]

    def ev_eng(self):
        self.erot += 1
        return "act" if self.erot % 2 else "dve"

    def copy(self, eng, out, in_, reads, writes):
        if eng == "act":
            self.S.op("act", lambda e: e.activation(out=out, in_=in_, func=AF.Copy), reads=reads, writes=writes)
        else:
            self.S.op(eng, lambda e: e.tensor_copy(out=out, in_=in_), reads=reads, writes=writes)

    def rstd_from_ss(self, ss, b_ss, n, scale, eps=EPS):
        S = self.S
        S.op("dve", lambda e: e.tensor_scalar(out=ss, in0=ss, scalar1=scale, scalar2=eps, op0=ALU.mult, op1=ALU.add), reads=[b_ss], writes=[b_ss])
        S.op("act", lambda e: e.activation(out=ss, in_=ss, func=AF.Sqrt), reads=[b_ss], writes=[b_ss])
        S.op("dve", lambda e: e.reciprocal(out=ss, in_=ss), reads=[b_ss], writes=[b_ss])

    def norm_T(self, xt, b_x, gB, b_gB, xn, b_xn, junk, b_junk, dst, b_dst):
        S = self.S
        ss, b_ss = self.st(1)
        S.op("act", lambda e: e.activation(out=junk, in_=xt, func=AF.Square, accum_out=ss), reads=[b_x], writes=[b_junk, b_ss])
        self.rstd_from_ss(ss, b_ss, 1, 1.0 / D)
        S.op("dve", lambda e: e.scalar_tensor_tensor(out=xn, in0=xt, scalar=ss, in1=gB, op0=ALU.mult, op1=ALU.mult),
             reads=[b_x, b_ss, b_gB], writes=[b_xn])
        self.transpose_to(xn, b_xn, 16, dst, b_dst)

    def transpose_to(self, src, b_src, ntile, dst, b_dst):
        S = self.S
        for g0 in range(0, ntile, 4):
            n = min(4, ntile - g0)
            bk, b_bk = self.bank()
            pv = bk.bitcast(BF16)[:, 0:512].rearrange("p (a b) -> p a b", a=4)
            for j in range(n):
                S.op("pe", lambda e, j=j: e.transpose(out=pv[:, j, :], in_=src[:, (g0 + j) * 128:(g0 + j + 1) * 128], identity=self.identb),
                     reads=[b_src, self.b_const], writes=[b_bk])
            self.copy(self.ev_eng(), dst[:, g0:g0 + n, :], pv[:, 0:n, :], [b_bk], [b_dst])

    def load_gB(self, off, layer, slot):
        gB = self.A(off, [128, D], F32)
        b = self.S.buf("gB")
        self.S.dma("sp", gB, self.norm_g[layer, slot, :].partition_broadcast(128), b, writes=[b])
        return gB, b

    def phase_A(self, xsrc, b_xsrc, layer, ntok=T, gslot=0, hoff=0):
        S = self.S
        S.barrier()
        ntt = ntok // 128
        hT = self.A(hoff, [128, 16, ntok], BF16)
        b_hT = S.buf("hT")
        o = hoff + 32 * ntok
        gB, b_gB = self.load_gB(o, layer, gslot)
        xt = [self.A(o + 8 * KB + i * 8 * KB, [128, D], F32) for i in range(2)]
        b_xt = [S.buf("xt") for i in range(2)]
        xn = [self.A(o + 24 * KB + i * 4 * KB, [128, D], BF16) for i in range(2)]
        b_xn = [S.buf("xn") for i in range(2)]
        junk = self.A(o + 32 * KB, [128, D], BF16)
        b_junk = S.buf("junk")
        for tt in range(ntt):
            i = tt % 2
            S.dma("sp", xt[i], xsrc[tt * 128:(tt + 1) * 128, :], b_xt[i], reads=[b_xsrc], writes=[b_xt[i]])
            self.norm_T(xt[i], b_xt[i], gB, b_gB, xn[i], b_xn[i], junk, b_junk, hT[:, :, tt * 128:(tt + 1) * 128], b_hT)
        return hT, b_hT

    def wbufs(self, off, n=2):
        self.wb = [self.A(off + i * 16 * KB, [128, 16, 512], BF16) for i in range(n)]
        self.b_wb = [self.S.buf("wb") for i in range(n)]
        self.wi = 0

    def wload(self, wd, k0, kp, c0, nc_):
        i = self.wi % len(self.wb)
        self.wi += 1
        wb, b = self.wb[i], self.b_wb[i]
        self.S.dma("pool", wb[:, 0:kp, 0:nc_], wd[k0 * 128:(k0 + kp) * 128, c0:c0 + nc_].rearrange("(a p) c -> p a c", p=128), b, writes=[b])
        return wb, b

    def tm_linear(self, srcT, b_src, KT, wd, c0, nct, ntt, evac, cw=512):
        S = self.S
        pieces = [(k0, min(16, KT - k0)) for k0 in range(0, KT, 16)]
        for ct in range(nct):
            if len(pieces) == 1:
                wb, b_wb = self.wload(wd, 0, KT, c0 + ct * cw, cw)
                for tt in range(ntt):
                    bk, b_bk = self.bank()
                    S.mm([(lambda e, k=k: e.matmul(bk[:, 0:cw], lhsT=srcT[:, k, tt * 128:(tt + 1) * 128], rhs=wb[:, k, 0:cw], start=(k == 0), stop=(k == KT - 1)))
                          for k in range(KT)], reads=[b_src, b_wb], writes=[b_bk])
                    evac(ct, tt, bk[:, 0:cw], b_bk)
            else:
                assert ntt <= 4
                bks = [((ct % 2) * 4 + tt) for tt in range(ntt)]
                for pi, (k0, kp) in enumerate(pieces):
                    wb, b_wb = self.wload(wd, k0, kp, c0 + ct * cw, cw)
                    for tt in range(ntt):
                        bk, b_bk = self.banks[bks[tt]], self.b_bank[bks[tt]]
                        S.mm([(lambda e, k=k: e.matmul(bk[:, 0:cw], lhsT=srcT[:, k0 + k, tt * 128:(tt + 1) * 128], rhs=wb[:, k, 0:cw],
                                                       start=(pi == 0 and k == 0), stop=(pi == len(pieces) - 1 and k == kp - 1)))
                              for k in range(kp)], reads=[b_src, b_wb], writes=[b_bk])
                for tt in range(ntt):
                    evac(ct, tt, self.banks[bks[tt]][:, 0:cw], self.b_bank[bks[tt]])

    def fm_linear(self, srcT, b_src, wd, c0, ncols, chunks, evac, pair=False, pc=512):
        S = self.S
        for p0 in range(0, ncols, pc):
            n = min(pc, ncols - p0)
            wb, b_wb = self.wload(wd, 0, 16, c0 + p0, n)
            ncc = n // 128

            def one(cc, t0, tn):
                bk, b_bk = self.bank()
                S.mm([(lambda e, k=k: e.matmul(bk[:, 0:tn], lhsT=wb[:, k, cc * 128:(cc + 1) * 128], rhs=srcT[:, k, t0:t0 + tn], start=(k == 0), stop=(k == 15)))
                      for k in range(16)], reads=[b_src, b_wb], writes=[b_bk])
                return bk[:, 0:tn], b_bk
            if pair:
                for cp in range(ncc // 2):
                    for ci, (t0, tn) in enumerate(chunks):
                        r0 = one(2 * cp, t0, tn)
                        r1 = one(2 * cp + 1, t0, tn)
                        evac((p0 // 128) // 2 + cp, ci, r0, r1)
            else:
                for cc in range(ncc):
                    for ci, (t0, tn) in enumerate(chunks):
                        ps, b_ps = one(cc, t0, tn)
                        evac(p0 // 128 + cc, ci, ps, b_ps)

    def qknorm_rope(self, ps, b_ps, nh, gi, tt, P):
        S = self.S
        n = nh * 128
        if P.get("tab_gi") != gi:
            P["tab_gi"] = gi
            g3 = P["gainB"][:, gi, :]
            S.op("dve", lambda e: e.tensor_tensor(out=P["COSg"], in0=P["COS"], in1=P["gainB"][:, gi:gi + 1, :].broadcast_to([128, 16, 128]), op=ALU.mult),
                 reads=[P["b_rope2"], P["b_gainB"]], writes=[P["b_tabg"]])
            v = lambda ap, f: ap.rearrange("p a (r f i) -> p a r f i", r=2, f=2, i=32)[:, :, :, f, :]
            for f in range(2):
                gsw = g3.rearrange("p (r f i) -> p r f i", r=2, f=2, i=32)[:, :, 1 - f, :].unsqueeze(1).broadcast_to([128, 16, 2, 32])
                S.op("dve", lambda e, f=f, gsw=gsw: e.tensor_tensor(out=v(P["SINSg"], f), in0=v(P["SINS"], f), in1=gsw, op=ALU.mult),
                     reads=[P["b_rope2"], P["b_gainB"]], writes=[P["b_tabg"]])
        pp = P["par"] = 1 - P.get("par", 0)
        sqj, b_sqj, q1, b_q1, tmpr, b_tmpr, qo, b_qo, qc, b_qc = [P[k][pp] for k in ("sqj", "b_sqj", "q1", "b_q1", "tmpr", "b_tmpr", "qo", "b_qo", "qc", "b_qc")]
        ss, b_ss = self.st(nh)
        v3 = lambda ap: ap.rearrange("p (h d) -> p h d", h=nh)
        v5 = lambda ap, f: ap.rearrange("p (h r f i) -> p h r f i", h=nh, r=2, f=2, i=32)[:, :, :, f, :]
        t4 = lambda tab, f: tab[:, tt, :].rearrange("p (r f i) -> p r f i", r=2, f=2, i=32)[:, :, f, :].unsqueeze(1).broadcast_to([128, nh, 2, 32])
        S.op("act", lambda e: e.activation(out=sqj[:, 0:n], in_=ps, func=AF.Square), reads=[b_ps], writes=[b_sqj])
        S.op("act", lambda e: e.activation(out=qc[:, 0:n], in_=ps, func=AF.Copy), reads=[b_ps], writes=[b_qc])
        S.op("dve", lambda e: e.tensor_tensor(out=v3(q1[:, 0:n]), in0=v3(ps), in1=P["COSg"][:, tt:tt + 1, :].broadcast_to([128, nh, 128]), op=ALU.mult),
             reads=[b_ps, P["b_tabg"]], writes=[b_q1])
        for f in range(2):
            S.op("pool", lambda e, f=f: e.tensor_tensor(out=v5(tmpr[:, 0:n], f), in0=v5(qc[:, 0:n], 1 - f), in1=t4(P["SINSg"], f), op=ALU.mult),
                 reads=[b_qc, P["b_tabg"]], writes=[b_tmpr])
        S.op("dve", lambda e: e.tensor_reduce(out=ss, in_=sqj[:, 0:n].rearrange("p (h d) -> p h d", h=nh), axis=AX.X, op=ALU.add), reads=[b_sqj], writes=[b_ss])
        self.rstd_from_ss(ss, b_ss, nh, 1.0 / 128)
        S.op("dve", lambda e: e.tensor_tensor(out=q1[:, 0:n], in0=q1[:, 0:n], in1=tmpr[:, 0:n], op=ALU.add), reads=[b_q1, b_tmpr], writes=[b_q1])
        S.op("dve", lambda e: e.tensor_tensor(out=v3(qo[:, 0:n]), in0=v3(q1[:, 0:n]), in1=ss.unsqueeze(2).broadcast_to([128, nh, 128]), op=ALU.mult),
             reads=[b_q1, b_ss], writes=[b_qo])
        return qo, b_qo

    def phase_L0proj(self, hT, b_hT):
        S = self.S
        S.barrier()
        wd = self.w["ev_w_in"]
        self.wbufs(64 * KB)
        o = 96 * KB
        c1 = self.A(o, [128, T], F32)
        s1 = self.A(o + 8 * KB, [128, T], F32)
        b_rope1 = S.buf("rope1")
        S.dma("sp", c1, self.c_rope1[0], b_rope1, writes=[b_rope1])
        S.dma("sp", s1, self.c_rope1[1], b_rope1, writes=[b_rope1])
        COS = self.A(o + 16 * KB, [128, 16, 128], F32)
        SINS = self.A(o + 24 * KB, [128, 16, 128], F32)
        b_rope2 = S.buf("rope2")
        S.dma("sp", COS, self.c_rope2[0].rearrange("(a p) d -> p a d", p=128), b_rope2, writes=[b_rope2])
        S.dma("sp", SINS, self.c_rope2[1].rearrange("(a p) d -> p a d", p=128), b_rope2, writes=[b_rope2])
        o = 128 * KB
        tmp = [self.A(o + i * 2 * KB, [128, 512], F32) for i in range(4)]
        b_tmp = [S.buf("tmp") for i in range(4)]
        o += 8 * KB
        sto = [self.A(o + i * 2 * KB, [128, 2, 512], BF16) for i in range(2)]
        b_sto = [S.buf("sto") for i in range(2)]
        o += 4 * KB
        stv = [self.A(o + i * KB, [128, 512], BF16) for i in range(2)]
        b_stv = [S.buf("stv") for i in range(2)]
        o += 2 * KB
        stg = [self.A(o + i * 2 * KB, [128, 512], F32) for i in range(2)]
        b_stg = [S.buf("stg") for i in range(2)]
        o += 4 * KB
        P = {}
        for nm, sz, dt in (("q1", 2, F32), ("sqj", 2, F32), ("tmpr", 2, F32), ("qo", 1, BF16), ("qc", 2, F32)):
            P[nm] = [self.A(o + i * sz * KB, [128, 512], dt) for i in range(2)]
            P["b_" + nm] = [S.buf(nm) for i in range(2)]
            o += 2 * sz * KB
        qst = [self.A(o + i * 4 * KB, [128, 4, 512], BF16) for i in range(2)]
        b_qst = [S.buf("qst") for i in range(2)]
        o += 8 * KB
        P["gainB"] = self.A(o, [128, 2, 128], F32)
        P["b_gainB"] = S.buf("gainB")
        S.dma("sp", P["gainB"], self.qk_gain.partition_broadcast(128), P["b_gainB"], writes=[P["b_gainB"]])
        P["COS"], P["SINS"], P["b_rope2"] = COS, SINS, b_rope2
        o += KB
        P["COSg"] = self.A(o, [128, 16, 128], F32)
        P["SINSg"] = self.A(o + 8 * KB, [128, 16, 128], F32)
        P["b_tabg"] = S.buf("tabg")
        o += 16 * KB
        assert o <= 192 * KB, o
        chunks = [(i * 512, 512) for i in range(4)]
        cnt = [0]

        def mk_rope(dst, b_dst):
            def ev(hp, ci, r0, r1):
                (x1, b1), (x2, b2) = r0, r1
                t0 = ci * 512
                i = cnt[0] % 2
                cnt[0] += 1
                cs, sn = c1[:, t0:t0 + 512], s1[:, t0:t0 + 512]
                S.op("dve", lambda e: e.tensor_tensor(out=tmp[0], in0=x1, in1=cs, op=ALU.mult), reads=[b1, b_rope1], writes=[b_tmp[0]])
                S.op("dve", lambda e: e.tensor_tensor(out=tmp[1], in0=x2, in1=sn, op=ALU.mult), reads=[b2, b_rope1], writes=[b_tmp[1]])
                S.op("pool", lambda e: e.tensor_tensor(out=sto[i][:, 0, :], in0=tmp[0], in1=tmp[1], op=ALU.subtract), reads=[b_tmp[0], b_tmp[1]], writes=[b_sto[i]])
                S.op("dve", lambda e: e.tensor_tensor(out=tmp[2], in0=x2, in1=cs, op=ALU.mult), reads=[b2, b_rope1], writes=[b_tmp[2]])
                S.op("dve", lambda e: e.tensor_tensor(out=tmp[3], in0=x1, in1=sn, op=ALU.mult), reads=[b1, b_rope1], writes=[b_tmp[3]])
                S.op("pool", lambda e: e.tensor_tensor(out=sto[i][:, 1, :], in0=tmp[2], in1=tmp[3], op=ALU.add), reads=[b_tmp[2], b_tmp[3]], writes=[b_sto[i]])
                S.dma("sp", dst[hp, :, :, t0:t0 + 512].rearrange("j p t -> p j t"), sto[i], b_sto[i], reads=[b_sto[i]], writes=[b_dst])
            return ev
        self.fm_linear(hT, b_hT, wd, 0, 1024, chunks, mk_rope(self.rqT, self.b_rqT), pair=True)
        self.fm_linear(hT, b_hT, wd, 1024, 1024, chunks, mk_rope(self.rkT, self.b_rkT), pair=True)

        def ev_v(ct, tt, ps, b_ps):
            i = cnt[0] % 2
            cnt[0] += 1
            self.copy(self.ev_eng(), stv[i], ps, [b_ps], [b_stv[i]])
            S.dma("sp", self.rv[tt * 128:(tt + 1) * 128, ct * 512:(ct + 1) * 512], stv[i], b_stv[i], reads=[b_stv[i]], writes=[self.b_rv])
        self.tm_linear(hT, b_hT, 16, wd, 2048, 2, 16, ev_v)

        def ev_g(ct, tt, ps, b_ps):
            i = cnt[0] % 2
            cnt[0] += 1
            S.op("act", lambda e: e.activation(out=stg[i], in_=ps, func=AF.Silu), reads=[b_ps], writes=[b_stg[i]])
            S.dma("sp", self.rg[tt * 128:(tt + 1) * 128, ct * 512:(ct + 1) * 512], stg[i], b_stg[i], reads=[b_stg[i]], writes=[self.b_rg])
        self.tm_linear(hT, b_hT, 16, wd, 3072, 2, 16, ev_g)

        pend = [None]

        def flush():
            if pend[0] is not None:
                f, pend[0] = pend[0], None
                f()

        def ev_q(ct, tt, ps, b_ps):
            qo, b_qo = self.qknorm_rope(ps, b_ps, 4, 0, tt, P)
            flush()

            def fin():
                i = (tt // 4) % 2
                self.transpose_to(qo, b_qo, 4, qst[i][:, :, (tt % 4) * 128:(tt % 4 + 1) * 128], b_qst[i])
                if tt % 4 == 3:
                    t0 = (tt // 4) * 512
                    S.dma("sp", self.aqT[ct * 4:ct * 4 + 4, :, t0:t0 + 512].rearrange("h p t -> p h t"), qst[i], b_qst[i], reads=[b_qst[i]], writes=[self.b_aqT])
            pend[0] = fin
        self.tm_linear(hT, b_hT, 16, wd, 4096, 2, 16, ev_q)
        flush()

        def ev_kv(ct, tt, ps, b_ps):
            qo, b_qo = self.qknorm_rope(ps[:, 0:256], b_ps, 2, 1, tt, P)
            flush()

            def fin():
                i = (tt // 4) % 2
                self.transpose_to(qo, b_qo, 2, qst[i][:, 0:2, (tt % 4) * 128:(tt % 4 + 1) * 128], b_qst[i])
                if tt % 4 == 3:
                    t0 = (tt // 4) * 512
                    S.dma("sp", self.akT[:, :, t0:t0 + 512].rearrange("h p t -> p h t"), qst[i][:, 0:2, :], b_qst[i], reads=[b_qst[i]], writes=[self.b_akT])
            pend[0] = fin
            j = cnt[0] % 2
            cnt[0] += 1
            self.copy("dve", stv[j][:, 0:256], ps[:, 256:512], [b_ps], [b_stv[j]])
            S.dma("sp", self.av[tt * 128:(tt + 1) * 128, :], stv[j][:, 0:256], b_stv[j], reads=[b_stv[j]], writes=[self.b_av])
        self.tm_linear(hT, b_hT, 16, wd, 5120, 1, 16, ev_kv)
        flush()

    def bank4(self):
        return self.bank()

    def phase_ret(self):
        S = self.S
        S.barrier()
        self.nrot = 4
        SETB = 56 * KB
        o = 2 * SETB
        pT = [self.A(o + i * KB, [128, 512], BF16) for i in range(3)]
        b_pT = [S.buf("pT") for i in range(3)]
        o += 3 * KB
        osb = [self.A(o + i * KB, [128, 256], F32) for i in range(4)]
        b_osb = [S.buf("osb") for i in range(4)]
        o += 4 * KB
        junk = self.A(o, [128, 256], F32)
        b_junk = S.buf("junk")
        o += KB
        on = [self.A(o + i * KB, [128, 256], F32) for i in range(4)]
        b_on = [S.buf("on") for i in range(4)]
        o += 4 * KB
        og = [self.A(o + i * 512, [128, 256], BF16) for i in range(4)]
        b_og = [S.buf("og") for i in range(4)]
        o += 2 * KB
        stT = [self.A(o + i * 2 * KB, [128, 2, 512], BF16) for i in range(2)]
        b_stT = [S.buf("stT") for i in range(2)]
        pi = 0
        sets = []
        for k in range(2):
            so = k * SETB
            sets.append(((self.A(so, [128, 2, T], BF16), self.A(so + 8 * KB, [128, 2, T], BF16), self.A(so + 16 * KB, [128, 16, 256], BF16),
                          self.A(so + 24 * KB, [128, 16, 256], F32), self.A(so + 40 * KB, [128, 31 * 128], F32)),
                         [S.buf(n) for n in ("rq", "rk", "rv", "rg", "strip")]))

        def load_head(hh):
            (qT, kT, v, gate, strip), (b_q, b_k, b_v, b_g, b_s) = sets[hh % 2]
            S.dma("sp", qT, self.rqT[hh].rearrange("j p t -> p j t"), b_q, reads=[self.b_rqT], writes=[b_q])
            S.dma("sp", kT, self.rkT[hh].rearrange("j p t -> p j t"), b_k, reads=[self.b_rkT], writes=[b_k])
            S.dma("sp", v, self.rv[:, hh * 256:(hh + 1) * 256].rearrange("(a p) c -> p a c", p=128), b_v, reads=[self.b_rv], writes=[b_v])
            S.dma("sp", gate, self.rg[:, hh * 256:(hh + 1) * 256].rearrange("(a p) c -> p a c", p=128), b_g, reads=[self.b_rg], writes=[b_g])
            S.dma("sp", strip, self.c_rmask[hh], b_s, writes=[b_s])
        load_head(0)
        for hh in range(RET_H):
            if hh + 1 < RET_H:
                load_head(hh + 1)
            (qT, kT, v, gate, strip), (b_q, b_k, b_v, b_g, b_s) = sets[hh % 2]
            for c in range(4):
                ob = 4 + 2 * (c % 2)
                OA, OB = self.banks[ob].rearrange("p (a b) -> p a b", a=2), self.banks[ob + 1].rearrange("p (a b) -> p a b", a=2)
                b_OA, b_OB = self.b_bank[ob], self.b_bank[ob + 1]

                def smm(j):
                    bk, b_bk = self.bank4()
                    S.mm([(lambda e, kk=kk: e.matmul(bk, lhsT=kT[:, kk, j * 128:(j + 1) * 128], rhs=qT[:, kk, c * 512:(c + 1) * 512], start=(kk == 0), stop=(kk == 1)))
                          for kk in range(2)], reads=[b_q, b_k], writes=[b_bk])
                    return bk, b_bk
                nxt = smm(0)
                for j in range(16):
                    bk, b_bk = nxt
                    if j < 15:
                        nxt = smm(j + 1)
                    p, b_p = pT[pi % 3], b_pT[pi % 3]
                    pi += 1
                    d0 = c * 4 - j + 15
                    S.op("dve", lambda e: e.tensor_tensor(out=p, in0=bk, in1=strip[:, d0 * 128:d0 * 128 + 512], op=ALU.mult), reads=[b_bk, b_s], writes=[b_p])
                    S.mm([(lambda e, i=i: e.matmul((OA if i < 2 else OB)[:, i % 2, :], lhsT=p[:, i * 128:(i + 1) * 128], rhs=v[:, j, :], start=(j == 0 and i % 2 == 0), stop=(j == 15), skip_group_check=True))
                          for i in range(4)], reads=[b_p, b_v], writes=[b_OA, b_OB])
                si = c % 2
                Os = [((OA if i < 2 else OB)[:, i % 2, :], (b_OA if i < 2 else b_OB)) for i in range(4)]
                sta, b_sta = self.st(8)
                stb_, b_stb = self.st(8)
                for i in range(4):
                    O, b_O = Os[i]
                    S.op("act", lambda e, i=i, O=O: e.activation(out=osb[i], in_=O, func=AF.Copy, accum_out=sta[:, i:i + 1]), reads=[b_O], writes=[b_osb[i], b_sta])
                    S.op("act", lambda e, i=i, O=O: e.activation(out=junk, in_=O, func=AF.Square, accum_out=sta[:, 4 + i:5 + i]), reads=[b_O], writes=[b_junk, b_sta])
                mean, msq, var, nb = stb_[:, 0:4], stb_[:, 4:8], sta[:, 4:8], sta[:, 0:4]
                S.op("dve", lambda e: e.tensor_scalar(out=mean, in0=sta[:, 0:4], scalar1=1.0 / 256, scalar2=None, op0=ALU.mult), reads=[b_sta], writes=[b_stb])
                S.op("dve", lambda e: e.tensor_tensor(out=msq, in0=mean, in1=mean, op=ALU.mult), reads=[b_stb], writes=[b_stb])
                S.op("dve", lambda e: e.scalar_tensor_tensor(out=var, in0=var, scalar=1.0 / 256, in1=msq, op0=ALU.mult, op1=ALU.subtract), reads=[b_sta, b_stb], writes=[b_sta])
                self.rstd_from_ss(var, b_sta, 4, 1.0)
                S.op("dve", lambda e: e.scalar_tensor_tensor(out=nb, in0=mean, scalar=-1.0, in1=var, op0=ALU.mult, op1=ALU.mult), reads=[b_sta, b_stb], writes=[b_sta])
                for i in range(4):
                    tt = c * 4 + i
                    S.op("dve", lambda e, i=i: e.tensor_scalar(out=on[i], in0=osb[i], scalar1=var[:, i:i + 1], scalar2=nb[:, i:i + 1], op0=ALU.mult, op1=ALU.add),
                         reads=[b_osb[i], b_sta], writes=[b_on[i]])
                    S.op("pool", lambda e, i=i, tt=tt: e.tensor_tensor(out=og[i], in0=on[i], in1=gate[:, tt, :], op=ALU.mult), reads=[b_on[i], b_g], writes=[b_og[i]])
                for i in range(4):
                    self.transpose_to(og[i], b_og[i], 2, stT[si][:, :, i * 128:(i + 1) * 128], b_stT[si])
                S.dma("sp", self.catT[hh * 2:hh * 2 + 2, :, c * 512:(c + 1) * 512].rearrange("k p t -> p k t"), stT[si], b_stT[si], reads=[b_stT[si]], writes=[self.b_catT])

    def phase_att(self):
        S = self.S
        S.barrier()
        self.nrot = 4
        SETB = 13 * KB
        o = 2 * SETB
        pT = [self.A(o + i * KB, [128, 512], BF16) for i in range(3)]
        b_pT = [S.buf("pT") for i in range(3)]
        o += 3 * KB
        ob4 = [self.A(o + i * KB, [128, 4, 128], BF16) for i in range(2)]
        b_ob4 = [S.buf("ob4") for i in range(2)]
        o += 2 * KB
        stA = [self.A(o + i * KB, [128, 512], BF16) for i in range(2)]
        b_stA = [S.buf("stA") for i in range(2)]
        pi = 0
        scale = 128.0 ** -0.5
        sets = []
        for k in range(2):
            so = k * SETB
            sets.append(((self.A(so, [128, T], BF16), self.A(so + 4 * KB, [128, T], BF16), self.A(so + 8 * KB, [128, 16, 132], BF16)),
                         [S.buf(n) for n in ("aq", "ak", "av")]))

        def load_head(h):
            (qT, kT, vx), (b_q, b_k, b_v) = sets[h % 2]
            kv = h // 4
            S.dma("sp", qT, self.aqT[h], b_q, reads=[self.b_aqT], writes=[b_q])
            S.dma("sp", kT, self.akT[kv], b_k, reads=[self.b_akT], writes=[b_k])
            S.dma("sp", vx[:, :, 0:128], self.av[:, kv * 128:(kv + 1) * 128].rearrange("(a p) c -> p a c", p=128), b_v, reads=[self.b_av], writes=[b_v])
            S.op("pool", lambda e: e.memset(vx[:, :, 128:129], 1.0), writes=[b_v])
        load_head(0)
        for h in range(ATT_H):
            if h + 1 < ATT_H:
                load_head(h + 1)
            (qT, kT, vx), (b_q, b_k, b_v) = sets[h % 2]
            for c in range(4):
                obk = 4 + 2 * (c % 2)
                OA, OB = self.banks[obk].rearrange("p (a b) -> p a b", a=2), self.banks[obk + 1].rearrange("p (a b) -> p a b", a=2)
                b_OA, b_OB = self.b_bank[obk], self.b_bank[obk + 1]

                def smm(j):
                    bk, b_bk = self.bank4()
                    S.mm([lambda e: e.matmul(bk, lhsT=kT[:, j * 128:(j + 1) * 128], rhs=qT[:, c * 512:(c + 1) * 512], start=True, stop=True)],
                         reads=[b_q, b_k], writes=[b_bk])
                    return bk, b_bk
                nxt = smm(0)
                for j in range(16):
                    bk, b_bk = nxt
                    if j < 15:
                        nxt = smm(j + 1)
                    p, b_p = pT[pi % 3], b_pT[pi % 3]
                    pi += 1
                    S.op("act", lambda e: e.activation(out=p, in_=bk, func=AF.Exp, scale=scale), reads=[b_bk], writes=[b_p])
                    S.mm([(lambda e, i=i: e.matmul((OA if i < 2 else OB)[:, i % 2, 0:129], lhsT=p[:, i * 128:(i + 1) * 128], rhs=vx[:, j, 0:129], start=(j == 0 and i % 2 == 0), stop=(j == 15), skip_group_check=True))
                          for i in range(4)], reads=[b_p, b_v], writes=[b_OA, b_OB])
                si = c % 2
                for i in range(4):
                    O, b_O = (OA if i < 2 else OB)[:, i % 2, :], (b_OA if i < 2 else b_OB)
                    st, b_st = self.st(1)
                    S.op("dve", lambda e: e.reciprocal(out=st, in_=O[:, 128:129]), reads=[b_O], writes=[b_st])
                    S.op("dve", lambda e: e.tensor_scalar(out=ob4[si][:, i, :], in0=O[:, 0:128], scalar1=st, scalar2=None, op0=ALU.mult), reads=[b_O, b_st], writes=[b_ob4[si]])
                bk, b_bk = self.bank4()
                pv = bk.bitcast(BF16)[:, 0:512].rearrange("p (a b) -> p a b", a=4)
                for i in range(4):
                    S.op("pe", lambda e, i=i: e.transpose(out=pv[:, i, :], in_=ob4[si][:, i, :], identity=self.identb), reads=[b_ob4[si], self.b_const], writes=[b_bk])
                self.copy(self.ev_eng(), stA[si], bk.bitcast(BF16)[:, 0:512], [b_bk], [b_stA[si]])
                S.dma("sp", self.catT[8 + h, :, c * 512:(c + 1) * 512], stA[si], b_stA[si], reads=[b_stA[si]], writes=[self.b_catT])

    def phase_mem(self, s, layer):
        S = self.S
        self.nrot = 8
        mT, b_mT = self.phase_A(self.mem[s], S.buf("memin", persist=True), layer, ntok=256, gslot=4, hoff=0)
        wd = self.w["xa_wkv%d" % layer]
        self.wbufs(64 * KB)
        o = 96 * KB
        stk = [self.A(o + i * 2 * KB, [128, 4, 256], BF16) for i in range(2)]
        b_stk = [S.buf("stk") for i in range(2)]
        o += 4 * KB
        stv = [self.A(o + i * KB, [128, 512], BF16) for i in range(2)]
        b_stv = [S.buf("stv") for i in range(2)]
        cnt = [0]

        def ev_k(ctile, ci, ps, b_ps):
            i = (ctile // 4) % 2
            self.copy(self.ev_eng(), stk[i][:, ctile % 4, :], ps, [b_ps], [b_stk[i]])
            if ctile % 4 == 3:
                S.dma("sp", self.memk[ctile - 3:ctile + 1].rearrange("c p m -> p c m"), stk[i], b_stk[i], reads=[b_stk[i]], writes=[self.b_memk])
        self.fm_linear(mT, b_mT, wd, 0, D, [(0, 256)], ev_k)

        def ev_v(ct, tt, ps, b_ps):
            i = cnt[0] % 2
            cnt[0] += 1
            self.copy(self.ev_eng(), stv[i], ps, [b_ps], [b_stv[i]])
            S.dma("sp", self.memv[tt * 128:(tt + 1) * 128, ct * 512:(ct + 1) * 512], stv[i], b_stv[i], reads=[b_stv[i]], writes=[self.b_memv])
        self.tm_linear(mT, b_mT, 16, wd, D, 4, 2, ev_v)

    def resid_stage(self, srcT, b_src, KT, wd, layer, gpost, xold_dram, b_xold, t0, xnew_dram, b_xnew, gpre, dstT, b_dstT, M):
        S = self.S
        XR, b_XR = M["XR"], M["b_XR"]
        gP, b_gP, gN, b_gN = M["gP"], M["b_gP"], M["gN"], M["b_gN"]
        S.dma("sp", gP, self.norm_g[layer, gpost, :].partition_broadcast(128), b_gP, writes=[b_gP])
        if gpre is not None:
            S.dma("sp", gN, self.norm_g[layer, gpre, :].partition_broadcast(128), b_gN, writes=[b_gN])
        k = M["srot"] = 1 - M.get("srot", 0)
        st, b_st = self.stat2[:, k * 32:(k + 1) * 32], self.b_stat2[k]

        def ev(ct, tt, ps, b_ps):
            S.op("dve", lambda e: e.tensor_tensor(out=XR[tt][:, ct * 512:(ct + 1) * 512], in0=ps, in1=gP[:, ct * 512:(ct + 1) * 512], op=ALU.mult),
                 reads=[b_ps, b_gP], writes=[b_XR[tt]])
            S.op("act", lambda e: e.activation(out=M["junk"][:, 0:512], in_=ps, func=AF.Square, accum_out=st[:, tt * 4 + ct:tt * 4 + ct + 1]), reads=[b_ps], writes=[M["b_junk"], b_st])
        def load_xo(tt):
            S.dma("sp", M["xo"][tt % 2], xold_dram[t0 + tt * 128:t0 + (tt + 1) * 128, :], M["b_xo"][tt % 2], reads=[b_xold], writes=[M["b_xo"][tt % 2]])
        load_xo(0)
        load_xo(1)
        self.tm_linear(srcT, b_src, KT, wd, 0, 4, 4, ev)
        S.op("dve", lambda e: e.tensor_reduce(out=st[:, 16:20], in_=st[:, 0:16].rearrange("p (t c) -> p t c", t=4), axis=AX.X, op=ALU.add), reads=[b_st], writes=[b_st])
        self.rstd_from_ss(st[:, 16:20], b_st, 4, 1.0 / D)
        for tt in range(4):
            xo, b_xo = M["xo"][tt % 2], M["b_xo"][tt % 2]
            S.op("dve", lambda e: e.scalar_tensor_tensor(out=XR[tt], in0=XR[tt], scalar=st[:, 16 + tt:17 + tt], in1=xo, op0=ALU.mult, op1=ALU.add),
                 reads=[b_XR[tt], b_st, b_xo], writes=[b_XR[tt]])
            if tt + 2 < 4:
                load_xo(tt + 2)
            S.dma("sp", xnew_dram[t0 + tt * 128:t0 + (tt + 1) * 128, :], XR[tt], b_XR[tt], reads=[b_XR[tt]], writes=[b_xnew])
            if gpre is not None:
                xn, b_xn = M["xn"][tt % 2], M["b_xn"][tt % 2]
                S.op("act", lambda e: e.activation(out=xn, in_=XR[tt], func=AF.Square, accum_out=st[:, 20 + tt:21 + tt]), reads=[b_XR[tt]], writes=[b_xn, b_st])
        if gpre is None:
            return
        self.rstd_from_ss(st[:, 20:24], b_st, 4, 1.0 / D)
        for tt in range(4):
            xn, b_xn = M["xn"][tt % 2], M["b_xn"][tt % 2]
            S.op("dve", lambda e: e.scalar_tensor_tensor(out=xn, in0=XR[tt], scalar=st[:, 20 + tt:21 + tt], in1=gN, op0=ALU.mult, op1=ALU.mult),
                 reads=[b_XR[tt], b_st, b_gN], writes=[b_xn])
            self.transpose_to(xn, b_xn, 16, dstT[:, :, tt * 128:(tt + 1) * 128], b_dstT)

    def phase_chain(self, s, layer, KT_mix, w_out_name, x_in, b_x_in, x_out, b_x_out):
        S = self.S
        S.barrier()
        if True:
            cstop = getattr(self, 'chain_stop', 9)
            M = {}
            M["XR"] = [self.A(i * 8 * KB, [128, D], F32) for i in range(4)]
            M["b_XR"] = [S.buf("XR") for i in range(4)]
            TA = self.A(32 * KB, [128, 16, 512], BF16)
            TB = self.A(48 * KB, [128, 16, 512], BF16)
            b_TA, b_TB = S.buf("TA"), S.buf("TB")
            ACTo = 64 * KB
            self.wbufs(112 * KB)
            M["misc"] = 144 * KB
            o = 160 * KB
            M["xo"] = [self.A(o + i * 8 * KB, [128, D], F32) for i in range(2)]
            M["b_xo"] = [S.buf("xo") for i in range(2)]
            o += 16 * KB
            M["xn"] = [self.A(o + i * 4 * KB, [128, D], BF16) for i in range(2)]
            M["b_xn"] = [S.buf("xn") for i in range(2)]
            o += 8 * KB
            M["junk"] = self.junk512
            M["b_junk"] = self.b_junk512
            M["gP"], M["gN"] = self.A(M["misc"], [128, D], F32), self.A(M["misc"] + 8 * KB, [128, D], F32)
            M["b_gP"], M["b_gN"] = S.buf("gP"), S.buf("gN")
            pT = [self.A(o + i * KB, [128, 512], BF16) for i in range(2)]
            b_pT = [S.buf("pT") for i in range(2)]
            o += 2 * KB
            rinv = self.A(o, [128, 512], F32)
            b_rinv = S.buf("rinv")
            o += 2 * KB
            sg = [self.A(o + i * 2 * KB, [128, 512], F32) for i in range(2)]
            b_sg = [S.buf("sg") for i in range(2)]
            o += 4 * KB
            b_ACT = S.buf("ACT")
            mk = self.A(ACTo, [128, 16, 256], BF16)
            mv = self.A(ACTo + 8 * KB, [128, 2, D], BF16)
            oT = self.A(ACTo + 16 * KB, [128, 16, 512], BF16)
            actT = self.A(ACTo, [128, NFT, 512], BF16)
            cat32 = self.A(ACTo, [128, 32, 512], BF16)
        for blk in range(getattr(self, 'chain_blocks', 4)):
            t0 = blk * 512
            if KT_mix == 16:
                if "nocat" not in getattr(self, "variant", ""):
                    S.dma("sp", TA, self.catT[0:16, :, t0:t0 + 512].rearrange("k p t -> p k t"), b_TA, reads=[self.b_catT], writes=[b_TA])
                src, b_src = TA, b_TA
            else:
                src = cat32
                b_src = b_ACT
                S.dma("sp", src, self.catT[0:32, :, t0:t0 + 512].rearrange("k p t -> p k t"), b_src, reads=[self.b_catT], writes=[b_src])
            self.resid_stage(src, b_src, KT_mix, self.w[w_out_name], layer, 1, x_in, b_x_in, t0, self.xs1, self.b_xs1, 2, TB, b_TB, M)
            if cstop < 2:
                continue
            b_mk = b_mv = b_oT = b_ACT
            S.dma("sp", mk, self.memk.rearrange("c p m -> p c m"), b_mk, reads=[self.b_memk], writes=[b_mk])
            S.dma("sp", mv, self.memv.rearrange("(a p) d -> p a d", p=128), b_mv, reads=[self.b_memv], writes=[b_mv])

            def ev_q(ctile, ci, ps, b_ps):
                self.copy(self.ev_eng(), TA[:, ctile, :], ps, [b_ps], [b_TA])
            self.fm_linear(TB, b_TB, self.w["xa_wq%d" % layer], 0, D, [(0, 512)], ev_q)
            xscale = 512.0 ** -0.5
            pT4, b_pT4 = [pT[0], pT[1], sg[1][:, 0:256].bitcast(BF16), sg[1][:, 256:512].bitcast(BF16)], [b_pT[0], b_pT[1], b_sg[1], b_sg[1]]
            rinv2, b_rinv2 = [rinv, sg[0]], [b_rinv, b_sg[0]]
            for h in range(4):
                pT, b_pT = pT4[(h % 2) * 2:(h % 2) * 2 + 2], b_pT4[(h % 2) * 2:(h % 2) * 2 + 2]
                rinv, b_rinv = rinv2[h % 2], b_rinv2[h % 2]
                for mt in range(2):
                    bk, b_bk = self.bank()
                    S.mm([(lambda e, dd=dd: e.matmul(bk, lhsT=mk[:, h * 4 + dd, mt * 128:(mt + 1) * 128], rhs=TA[:, h * 4 + dd, :], start=(dd == 0), stop=(dd == 3)))
                          for dd in range(4)], reads=[b_mk, b_TA], writes=[b_bk])
                    S.op("act", lambda e: e.activation(out=pT[mt], in_=bk, func=AF.Exp, scale=xscale), reads=[b_bk], writes=[b_pT[mt]])
                bk, b_bk = self.bank()
                S.mm([(lambda e, mt=mt: e.matmul(bk, lhsT=self.onesb, rhs=pT[mt], start=(mt == 0), stop=(mt == 1))) for mt in range(2)],
                     reads=[b_pT[0], b_pT[1], self.b_const], writes=[b_bk])
                S.op("dve", lambda e: e.reciprocal(out=rinv, in_=bk), reads=[b_bk], writes=[b_rinv])
                for ee in range(4):
                    bk, b_bk = self.bank()
                    S.mm([(lambda e, mt=mt: e.matmul(bk, lhsT=mv[:, mt, h * 512 + ee * 128:h * 512 + (ee + 1) * 128], rhs=pT[mt], start=(mt == 0), stop=(mt == 1)))
                          for mt in range(2)], reads=[b_mv, b_pT[0], b_pT[1]], writes=[b_bk])
                    S.op("dve", lambda e: e.tensor_tensor(out=oT[:, h * 4 + ee, :], in0=bk, in1=rinv, op=ALU.mult), reads=[b_bk, b_rinv], writes=[b_oT])
            if cstop < 3:
                continue
            pT, b_pT, rinv, b_rinv = pT4[0:2], b_pT4[0:2], rinv2[0], b_rinv2[0]
            self.resid_stage(oT, b_oT, 16, self.w["xa_wo%d" % layer], layer, 3, self.xs1, self.b_xs1, t0, self.xs2, self.b_xs2, 5, TB, b_TB, M)
            if cstop < 4:
                continue
            b_actT = b_ACT
            wgu = self.w["ffn_w_gu%d" % layer]
            for f2 in range(NFT // 2):
                i = self.wi % 2
                self.wi += 1
                wb, b_wb = self.wb[i], self.b_wb[i]
                S.dma("pool", wb[:, :, 0:256], wgu[:, f2 * 256:(f2 + 1) * 256].rearrange("(a p) c -> p a c", p=128), b_wb, writes=[b_wb])
                S.dma("pool", wb[:, :, 256:512], wgu[:, DFF + f2 * 256:DFF + (f2 + 1) * 256].rearrange("(a p) c -> p a c", p=128), b_wb, writes=[b_wb])
                for cc in range(2):
                    f = f2 * 2 + cc
                    bg, b_bg = self.bank()
                    S.mm([(lambda e, k=k: e.matmul(bg, lhsT=wb[:, k, cc * 128:(cc + 1) * 128], rhs=TB[:, k, :], start=(k == 0), stop=(k == 15))) for k in range(16)],
                         reads=[b_TB, b_wb], writes=[b_bg])
                    bu, b_bu = self.bank()
                    S.mm([(lambda e, k=k: e.matmul(bu, lhsT=wb[:, k, 256 + cc * 128:256 + (cc + 1) * 128], rhs=TB[:, k, :], start=(k == 0), stop=(k == 15))) for k in range(16)],
                         reads=[b_TB, b_wb], writes=[b_bu])
                    S.op("act", lambda e: e.activation(out=sg[f % 2], in_=bg, func=AF.Silu), reads=[b_bg], writes=[b_sg[f % 2]])
                    S.op("dve", lambda e: e.tensor_tensor(out=actT[:, f, :], in0=bu, in1=sg[f % 2], op=ALU.mult), reads=[b_bu, b_sg[f % 2]], writes=[b_actT])
            if cstop < 5:
                continue
            self.resid_stage(actT, b_actT, NFT, self.w["ffn_w_down%d" % layer], layer, 6, self.xs2, self.b_xs2, t0, x_out, b_x_out, None, None, None, M)

    def layer0(self, s, x_in, b_x_in, x_out, b_x_out):
        hT, b_hT = self.phase_A(x_in, b_x_in, 0)
        self.phase_L0proj(hT, b_hT)
        self.phase_ret()
        self.phase_att()
        self.phase_mem(s, 0)
        self.phase_chain(s, 0, 16, "ev_w_out", x_in, b_x_in, x_out, b_x_out)

    def l1_scratch(self):
        nc, S = self.nc, self.S
        if hasattr(self, "zs"):
            return
        ds = lambda n, s, dt=F32: (nc.dram_tensor(n, s, dt, kind="Internal").ap(), S.buf(n, persist=True))
        self.zs, self.b_zs = ds("zs", [T, 4096])
        self.xbcT, self.b_xbcT = ds("xbcT", [48, 128, T], BF16)
        self.dtv, self.b_dtv = ds("dtv", [T, 128])
        self.dta, self.b_dta = ds("dta", [T, 128])

    def phase_L1proj(self, hT, b_hT):
        S = self.S
        S.barrier()
        self.l1_scratch()
        self.nrot = 8
        wd = self.w["od_w_in"]
        self.wbufs(64 * KB)
        o = 96 * KB
        stg = [self.A(o + i * 2 * KB, [128, 512], F32) for i in range(2)]
        b_stg = [S.buf("stg") for i in range(2)]
        o += 4 * KB
        cnt = [0]

        def ev_z(ct, tt, ps, b_ps):
            i = cnt[0] % 2
            cnt[0] += 1
            S.op("act", lambda e: e.activation(out=stg[i], in_=ps, func=AF.Silu), reads=[b_ps], writes=[b_stg[i]])
            S.dma("sp", self.zs[tt * 128:(tt + 1) * 128, ct * 512:(ct + 1) * 512], stg[i], b_stg[i], reads=[b_stg[i]], writes=[self.b_zs])
        self.tm_linear(hT, b_hT, 16, wd, 0, 8, 16, ev_z)
        biasB = self.A(o, [128, 128], F32)
        aB = self.A(o + 512, [128, 128], F32)
        b_cb = S.buf("cb")
        S.dma("sp", biasB, self.od_dt_bias.partition_broadcast(128), b_cb, writes=[b_cb])
        S.dma("sp", aB, self.od_a_log.partition_broadcast(128), b_cb, writes=[b_cb])
        S.op("act", lambda e: e.activation(out=aB, in_=aB, func=AF.Exp), reads=[b_cb], writes=[b_cb])
        S.op("dve", lambda e: e.tensor_scalar(out=aB, in0=aB, scalar1=-1.0, scalar2=None, op0=ALU.mult), reads=[b_cb], writes=[b_cb])
        o += KB
        dtt = [self.A(o + i * KB, [128, 2, 128], F32) for i in range(2)]
        b_dtt = [S.buf("dtt") for i in range(2)]
        o += 2 * KB

        def ev_dt(ct, tt, ps, b_ps):
            i = tt % 2
            d, b_d = dtt[i], b_dtt[i]
            S.op("dve", lambda e: e.tensor_tensor(out=d[:, 0, :], in0=ps, in1=biasB, op=ALU.add), reads=[b_ps, b_cb], writes=[b_d])
            S.op("act", lambda e: e.activation(out=d[:, 0, :], in_=d[:, 0, :], func=AF.Exp), reads=[b_d], writes=[b_d])
            S.op("act", lambda e: e.activation(out=d[:, 0, :], in_=d[:, 0, :], func=AF.Ln, bias=1.0), reads=[b_d], writes=[b_d])
            S.op("dve", lambda e: e.tensor_tensor(out=d[:, 1, :], in0=d[:, 0, :], in1=aB, op=ALU.mult), reads=[b_d, b_cb], writes=[b_d])
            S.dma("sp", self.dtv[tt * 128:(tt + 1) * 128, :], d[:, 0, :], b_d, reads=[b_d], writes=[self.b_dtv])
            S.dma("sp", self.dta[tt * 128:(tt + 1) * 128, :], d[:, 1, :], b_d, reads=[b_d], writes=[self.b_dta])
        self.tm_linear(hT, b_hT, 16, wd, 4096 + 6144, 1, 16, ev_dt, cw=128)
        cw_t = self.A(o, [128, 48, 5], F32)
        cb_t = self.A(o + KB, [128, 48], F32)
        b_cw = S.buf("cw")
        S.dma("sp", cw_t, self.od_conv_w, b_cw, writes=[b_cw])
        S.dma("sp", cb_t, self.od_conv_b, b_cw, writes=[b_cw])
        o += 2 * KB
        raw = [self.A(o + i * 8224, [128, T + 4], F32) for i in range(2)]
        b_raw = [S.buf("raw") for i in range(2)]
        o += 2 * 8224
        acc = [self.A(o + i * 8 * KB, [128, T], F32) for i in range(2)]
        b_acc = [S.buf("acc") for i in range(2)]
        o += 16 * KB
        cvo = [self.A(o + i * 4 * KB, [128, T], BF16) for i in range(2)]
        b_cvo = [S.buf("cvo") for i in range(2)]
        for i in range(2):
            S.op("pool", lambda e, i=i: e.memset(raw[i][:, 0:2], 0.0), writes=[b_raw[i]])
            S.op("pool", lambda e, i=i: e.memset(raw[i][:, T + 2:T + 4], 0.0), writes=[b_raw[i]])

        def ev_x(ctile, ci, ps, b_ps):
            i = ctile % 2
            self.copy("act", raw[i][:, 2 + ci * 512:2 + (ci + 1) * 512], ps, [b_ps], [b_raw[i]])
            if ci == 3:
                a, b_a = acc[i], b_acc[i]
                S.op("dve", lambda e: e.tensor_scalar(out=a, in0=raw[i][:, 0:T], scalar1=cw_t[:, ctile, 0:1], scalar2=None, op0=ALU.mult),
                     reads=[b_raw[i], b_cw], writes=[b_a])
                for k in range(1, 5):
                    S.op("dve", lambda e, k=k: e.scalar_tensor_tensor(out=a, in0=raw[i][:, k:k + T], scalar=cw_t[:, ctile, k:k + 1], in1=a, op0=ALU.mult, op1=ALU.add),
                         reads=[b_raw[i], b_cw, b_a], writes=[b_a])
                S.op("act", lambda e: e.activation(out=cvo[i], in_=a, func=AF.Silu, bias=cb_t[:, ctile:ctile + 1]), reads=[b_a, b_cw], writes=[b_cvo[i]])
                S.dma("sp", self.xbcT[ctile], cvo[i], b_cvo[i], reads=[b_cvo[i]], writes=[self.b_xbcT])
        self.fm_linear(hT, b_hT, wd, 4096, 6144, [(i * 512, 512) for i in range(4)], ev_x)

    def phase_ssd(self):
        S = self.S
        S.barrier()
        self.nrot = 4
        bf = lambda n: S.buf(n)
        o = 0
        tri = self.A(o, [128, 4, 128], F32); o += 2 * KB
        trib = self.A(o, [128, 4, 128], BF16); o += KB
        b_tri = bf("tri")
        S.dma("sp", tri, self.c_tri.rearrange("k p l -> p k l"), b_tri, writes=[b_tri])
        S.op("dve", lambda e: e.tensor_copy(out=trib, in_=tri), reads=[b_tri], writes=[b_tri])
        dtv = self.A(o, [128, 16, 128], F32); o += 8 * KB
        o_dta = o
        dta = self.A(o, [128, 16, 128], F32); o += 8 * KB
        dtmp = self.A(o, [128, 16, 128], F32); o += 8 * KB
        dth = self.A(o, [128, 16, 128], BF16); o += 4 * KB
        dtl = self.A(o, [128, 16, 128], BF16); o += 4 * KB
        dtl32 = self.A(o, [128, 16, 128], F32); o += 8 * KB
        b_dt = bf("dt")
        S.dma("sp", dtv, self.dtv.rearrange("(a p) c -> p a c", p=128), b_dt, reads=[self.b_dtv], writes=[b_dt])
        S.dma("sp", dta, self.dta.rearrange("(a p) c -> p a c", p=128), b_dt, reads=[self.b_dta], writes=[b_dt])
        S.op("dve", lambda e: e.tensor_copy(out=dth, in_=dta), reads=[b_dt], writes=[b_dt])
        S.op("dve", lambda e: e.tensor_tensor(out=dtmp, in0=dta, in1=dth, op=ALU.subtract), reads=[b_dt], writes=[b_dt])
        S.op("dve", lambda e: e.tensor_copy(out=dtl, in_=dtmp), reads=[b_dt], writes=[b_dt])
        S.op("dve", lambda e: e.tensor_copy(out=dtl32, in_=dtl), reads=[b_dt], writes=[b_dt])
        dB = self.A(o, [128, 64], F32); o += 256
        ngB = self.A(o, [128, 512], F32); o += 2 * KB
        b_dB = bf("dB")
        S.dma("sp", dB, self.od_d.partition_broadcast(128), b_dB, writes=[b_dB])
        S.barrier()
        o_reuse = o_dta
        xsT = self.A(o, [128, 4, T], BF16); o += 16 * KB
        xs = self.A(o, [128, 16, 512], BF16); o += 16 * KB
        Bt = self.A(o, [128, 16, 128], BF16); o += 4 * KB
        BT = self.A(o, [128, T], BF16); o += 4 * KB
        CT = self.A(o, [128, T], BF16); o += 4 * KB
        xdt = [self.A(o + i * 16 * KB, [128, 16, 512], BF16) for i in range(2)]
        zg = self.A(o, [128, 16, 512], F32); o += 32 * KB
        y = self.A(o, [128, 16, 512], F32); o += 32 * KB
        NR = 3
        Ah = [self.A(o_reuse + i * 2 * KB, [128, 8, 128], BF16) for i in range(NR)]; o_reuse += NR * 2 * KB
        Al = [self.A(o_reuse + i * 2 * KB, [128, 8, 128], BF16) for i in range(NR)]; o_reuse += NR * 2 * KB
        assert o_reuse <= o_dta + 16 * KB
        Ee = [self.A(o + i * 4 * KB, [128, 1024], F32) for i in range(2)]; o += 2 * 4 * KB
        MT = [self.A(o + i * 2 * KB, [128, 8, 128], BF16) for i in range(NR)]; o += NR * 2 * KB
        cbm = [self.A(o + i * 512, [128, 128], F32) for i in range(2)]; o += 2 * 512
        xd = [self.A(o + i * KB, [128, 512], BF16) for i in range(NR)]; o += NR * KB
        t1 = [self.A(o + i * 2 * KB, [128, 512], F32) for i in range(2)]; o += 2 * 2 * KB
        st32 = [self.A(o + i * 2 * KB, [128, 512], F32) for i in range(2)]; o += 4 * KB
        stb = [self.A(o + i * KB, [128, 512], BF16) for i in range(2)]; o += 2 * KB
        sm = [self.A(o + i * 128, [128, 24], F32) for i in range(NR)]; o += 512
        yb = [self.A(o, [128, 512], BF16), self.junk512]; o += KB
        yst = [self.A(o + i * 4 * KB, [128, 4, 512], BF16) for i in range(2)]; o += 8 * KB
        junk = self.A(o, [128, 512], BF16); o += KB
        M2 = {}
        assert o <= 192 * KB, o
        b_Ah, b_Al, b_MT, b_xd, b_sm = [[bf("w") for i in range(NR)] for _ in range(5)]
        b_Ee, b_cbm, b_t1 = [[bf("w") for i in range(2)] for _ in range(3)]
        b_yst = [bf("yst") for i in range(2)]
        b_st = [bf("st") for i in range(2)]
        b_yb, b_junk = [bf("yb"), self.b_junk512], bf("junk")
        b_xsT, b_xs, b_Bt, b_BT, b_CT, b_X = [bf(n) for n in ("xsT", "xs", "Bt", "BT", "CT", "xdt_zg")]
        b_y = [bf("y") for tt in range(16)]
        Ule, Uge, Sgt, Slt = 0, 1, 2, 3
        h3 = lambda ap: ap.rearrange("p (h q) -> p h q", h=8)
        it = [0]
        for g in range(8):
            S.dma("sp", xsT, self.xbcT[g * 4:(g + 1) * 4].rearrange("c p t -> p c t"), b_xsT, reads=[self.b_xbcT], writes=[b_xsT])
            S.dma("sp", BT, self.xbcT[32 + g], b_BT, reads=[self.b_xbcT], writes=[b_BT])
            S.dma("sp", CT, self.xbcT[40 + g], b_CT, reads=[self.b_xbcT], writes=[b_CT])
            S.dma("sp", ngB, self.od_norm_g[g * 512:(g + 1) * 512].partition_broadcast(128), b_dB, writes=[b_dB])
            for tt in range(16):
                bk, b_bk = self.bank()
                pv = bk.bitcast(BF16)[:, 0:512].rearrange("p (a b) -> p a b", a=4)
                for c in range(4):
                    S.op("pe", lambda e, c=c: e.transpose(out=pv[:, c, :], in_=xsT[:, c, tt * 128:(tt + 1) * 128], identity=self.identb),
                         reads=[b_xsT, self.b_const], writes=[b_bk])
                self.copy(self.ev_eng(), xs[:, tt, :], bk.bitcast(BF16)[:, 0:512], [b_bk], [b_xs])
                bk, b_bk = self.bank()
                pv = bk.bitcast(BF16)[:, 0:128]
                S.op("pe", lambda e: e.transpose(out=pv, in_=BT[:, tt * 128:(tt + 1) * 128], identity=self.identb), reads=[b_BT, self.b_const], writes=[b_bk])
                self.copy(self.ev_eng(), Bt[:, tt, :], pv, [b_bk], [b_Bt])
            for tt in range(16):
                x3 = h3(xs[:, tt, :])
                for dr in range(2):
                    S.op("pool" if dr else "dve", lambda e, dr=dr: e.tensor_tensor(out=h3(xdt[dr][:, tt, :]), in0=x3,
                                                                                   in1=dtv[:, tt, dr * 64 + g * 8:dr * 64 + g * 8 + 8].unsqueeze(2).broadcast_to([128, 8, 64]), op=ALU.mult),
                         reads=[b_xs, b_dt], writes=[b_X])
                S.op("dve", lambda e: e.tensor_tensor(out=h3(y[:, tt, :]), in0=x3, in1=dB[:, g * 8:(g + 1) * 8].unsqueeze(2).broadcast_to([128, 8, 64]), op=ALU.mult),
                     reads=[b_xs, b_dB], writes=[b_y[tt]])

            def indep(dr, ci, tt, step):
                i, j = step % NR, step % 2
                U, Sx = (Ule, Sgt) if dr == 0 else (Uge, Slt)
                hsl = slice(dr * 64 + g * 8, dr * 64 + g * 8 + 8)
                S.op("pool", lambda e: e.tensor_tensor(out=Ah[i], in0=trib[:, U:U + 1, :].broadcast_to([128, 8, 128]),
                                                       in1=dth[:, tt, hsl].unsqueeze(2).broadcast_to([128, 8, 128]), op=ALU.mult),
                     reads=[b_tri, b_dt], writes=[b_Ah[i]])
                for h in range(8):
                    S.op("act", lambda e, h=h: e.activation(out=Al[i][:, h, :], in_=trib[:, U, :], func=AF.Copy, scale=dtl32[:, tt, hsl.start + h:hsl.start + h + 1]),
                         reads=[b_tri, b_dt], writes=[b_Al[i]])
                for half in range(2):
                    bk, b_bk = self.banks[4 + half], self.b_bank[4 + half]
                    S.mm([lambda e: e.matmul(bk, lhsT=trib[:, Sx, :], rhs=Ah[i][:, half * 4:(half + 1) * 4, :].rearrange("p a b -> p (a b)"), start=True, stop=False),
                          lambda e: e.matmul(bk, lhsT=trib[:, Sx, :], rhs=Al[i][:, half * 4:(half + 1) * 4, :].rearrange("p a b -> p (a b)"), start=False, stop=True)],
                         reads=[b_tri, b_Ah[i], b_Al[i]], writes=[b_bk])
                    S.op("act", lambda e: e.activation(out=Ee[j][:, half * 512:(half + 1) * 512], in_=bk, func=AF.Exp), reads=[b_bk], writes=[b_Ee[j]])
                b6, b_b6 = self.banks[6], self.b_bank[6]
                fns = []
                for k, lt in enumerate((trib[:, U, :], trib[:, Sx, :], self.onesb)):
                    fns.append(lambda e, k=k, lt=lt: e.matmul(b6[:, k * 8:(k + 1) * 8], lhsT=lt, rhs=dth[:, tt, hsl], start=(k == 0), stop=False, skip_group_check=True))
                    fns.append(lambda e, k=k, lt=lt: e.matmul(b6[:, k * 8:(k + 1) * 8], lhsT=lt, rhs=dtl[:, tt, hsl], start=False, stop=True, skip_group_check=True))
                fns.append(lambda e: e.matmul(b6[:, 128:256], lhsT=BT[:, tt * 128:(tt + 1) * 128], rhs=CT[:, tt * 128:(tt + 1) * 128], start=False, stop=True, skip_group_check=True))
                S.mm(fns, reads=[b_tri, b_dt, self.b_const, b_BT, b_CT], writes=[b_b6])
                S.op("act", lambda e: e.activation(out=sm[i], in_=b6[:, 0:24], func=AF.Exp), reads=[b_b6], writes=[b_sm[i]])
                S.op("dve", lambda e: e.tensor_tensor(out=cbm[j], in0=b6[:, 128:256], in1=tri[:, U, :], op=ALU.mult), reads=[b_b6, b_tri], writes=[b_cbm[j]])
                S.op("dve", lambda e: e.tensor_tensor(out=MT[i], in0=Ee[j].rearrange("p (a b) -> p a b", a=8), in1=cbm[j].unsqueeze(1).broadcast_to([128, 8, 128]), op=ALU.mult),
                     reads=[b_Ee[j], b_cbm[j]], writes=[b_MT[i]])
                bY, b_bY = self.bank()
                S.mm([(lambda e, h=h: e.matmul(bY[:, h * 64:(h + 1) * 64], lhsT=MT[i][:, h, :], rhs=xdt[dr][:, tt, h * 64:(h + 1) * 64], start=(h == 0), stop=True, skip_group_check=True))
                      for h in range(8)], reads=[b_MT[i], b_X], writes=[b_bY])
                S.op("dve", lambda e: e.tensor_tensor(out=y[:, tt, :], in0=y[:, tt, :], in1=bY, op=ALU.add), reads=[b_y[tt], b_bY], writes=[b_y[tt]])
                if ci < 15:
                    S.op("pool", lambda e: e.tensor_tensor(out=h3(xd[i]), in0=h3(xdt[dr][:, tt, :]), in1=sm[i][:, 8:16].unsqueeze(2).broadcast_to([128, 8, 64]), op=ALU.mult),
                         reads=[b_X, b_sm[i]], writes=[b_xd[i]])

            def chain(dr, ci, tt, step):
                i, j = step % NR, step % 2
                if ci > 0:
                    bO, b_bO = self.bank()
                    S.mm([lambda e: e.matmul(bO, lhsT=CT[:, tt * 128:(tt + 1) * 128], rhs=stb[dr], start=True, stop=True)], reads=[b_CT, b_st[dr]], writes=[b_bO])
                    S.op("dve", lambda e: e.tensor_tensor(out=h3(t1[j]), in0=h3(bO), in1=sm[i][:, 0:8].unsqueeze(2).broadcast_to([128, 8, 64]), op=ALU.mult),
                         reads=[b_bO, b_sm[i]], writes=[b_t1[j]])
                    S.op("dve", lambda e: e.tensor_tensor(out=y[:, tt, :], in0=y[:, tt, :], in1=t1[j], op=ALU.add), reads=[b_y[tt], b_t1[j]], writes=[b_y[tt]])
                if ci < 15:
                    b7, b_b7 = self.banks[7], self.b_bank[7]
                    S.mm([lambda e: e.matmul(b7, lhsT=Bt[:, tt, :], rhs=xd[i], start=True, stop=True)], reads=[b_Bt, b_xd[i]], writes=[b_b7])
                    if ci == 0:
                        S.op("dve", lambda e: e.tensor_copy(out=st32[dr], in_=b7), reads=[b_b7], writes=[b_st[dr]])
                    else:
                        S.op("dve", lambda e: e.tensor_tensor(out=h3(st32[dr]), in0=h3(st32[dr]), in1=sm[i][:, 16:24].unsqueeze(2).broadcast_to([128, 8, 64]), op=ALU.mult),
                             reads=[b_st[dr], b_sm[i]], writes=[b_st[dr]])
                        S.op("dve", lambda e: e.tensor_tensor(out=st32[dr], in0=st32[dr], in1=b7, op=ALU.add), reads=[b_st[dr], b_b7], writes=[b_st[dr]])
                    S.op("dve", lambda e: e.tensor_copy(out=stb[dr], in_=st32[dr]), reads=[b_st[dr]], writes=[b_st[dr]])
            steps = []
            for ci in range(16):
                steps.append((0, ci, ci))
                steps.append((1, ci, 15 - ci))
            base = it[0]
            indep(*steps[0], base)
            for k in range(len(steps)):
                if k + 1 < len(steps):
                    indep(*steps[k + 1], base + k + 1)
                chain(*steps[k], base + k)
            it[0] = base + len(steps)
            S.dma("sp", zg, self.zs[:, g * 512:(g + 1) * 512].rearrange("(a p) c -> p a c", p=128), b_X, reads=[self.b_zs], writes=[b_X])
            kq = M2["srot"] = 1 - M2.get("srot", 0)
            ss16, b_ss16 = self.stat2[:, kq * 32:kq * 32 + 16], self.b_stat2[kq]
            for tt in range(16):
                S.op("dve", lambda e: e.tensor_tensor(out=y[:, tt, :], in0=y[:, tt, :], in1=zg[:, tt, :], op=ALU.mult), reads=[b_y[tt], b_X], writes=[b_y[tt]])
                S.op("act", lambda e: e.activation(out=junk, in_=y[:, tt, :], func=AF.Square, accum_out=ss16[:, tt:tt + 1]), reads=[b_y[tt]], writes=[b_junk, b_ss16])
            self.rstd_from_ss(ss16, b_ss16, 16, 1.0 / 512)
            for tt in range(16):
                S.op("dve", lambda e: e.scalar_tensor_tensor(out=yb[tt % 2], in0=y[:, tt, :], scalar=ss16[:, tt:tt + 1], in1=ngB, op0=ALU.mult, op1=ALU.mult),
                     reads=[b_y[tt], b_ss16, b_dB], writes=[b_yb[tt % 2]])
                si = (tt // 4) % 2
                self.transpose_to(yb[tt % 2], b_yb[tt % 2], 4, yst[si][:, :, (tt % 4) * 128:(tt % 4 + 1) * 128], b_yst[si])
                if tt % 4 == 3:
                    t0 = (tt // 4) * 512
                    S.dma("sp", self.catT[g * 4:(g + 1) * 4, :, t0:t0 + 512].rearrange("k p t -> p k t"), yst[si], b_yst[si], reads=[b_yst[si]], writes=[self.b_catT])
        self.nrot = 8

    def layer1(self, s, x_in, b_x_in, x_out, b_x_out):
        hT, b_hT = self.phase_A(x_in, b_x_in, 1)
        self.phase_L1proj(hT, b_hT)
        self.phase_ssd()
        self.phase_mem(s, 1)
        self.phase_chain(s, 1, 32, "od_w_out", x_in, b_x_in, x_out, b_x_out)


THETA = 10000.0
def host_consts():
    c = {}
    c["c_ident"] = np.eye(128, dtype=np.float32)
    t = np.arange(2048, dtype=np.float32)
    inv = (THETA ** (-np.arange(0, 256, 2, dtype=np.float32) / 256)).astype(np.float32)
    ang = (t[None, :] * inv[:, None]).astype(np.float32)
    c["c_rope1"] = np.stack([np.cos(ang), np.sin(ang)]).astype(np.float32)
    inv2 = (THETA ** (-np.arange(0, 64, 2, dtype=np.float32) / 64)).astype(np.float32)
    row = (np.arange(2048) // 64).astype(np.float32); col = (np.arange(2048) % 64).astype(np.float32)
    ar = (row[:, None] * inv2[None, :]).astype(np.float32); ac = (col[:, None] * inv2[None, :]).astype(np.float32)
    COS = np.concatenate([np.cos(ar), np.cos(ar), np.cos(ac), np.cos(ac)], axis=1)
    SINS = np.concatenate([-np.sin(ar), np.sin(ar), -np.sin(ac), np.sin(ac)], axis=1)
    c["c_rope2"] = np.stack([COS, SINS]).astype(np.float32)
    heads = np.arange(4, dtype=np.float32)
    lgf = np.log1p(-np.exp2(-(5.0 + heads))).astype(np.float32)
    lgb = np.log1p(-np.exp2(-(5.5 + heads))).astype(np.float32)
    dd = np.arange(-15, 16)[None, :, None] * 128 + np.arange(128)[None, None, :] - np.arange(128)[:, None, None]
    dd = dd.astype(np.float64)
    m = np.zeros((4, 128, 31, 128), np.float64)
    for h in range(4):
        m[h] = np.where(dd >= 0, np.exp(np.float64(lgf[h]) * np.maximum(dd, 0)), np.exp(np.float64(lgb[h]) * np.maximum(-dd, 0)))
    c["c_rmask"] = (m / 16.0).astype(np.float32).reshape(4, 128, 31 * 128)
    j = np.arange(128)[:, None]; l = np.arange(128)[None, :]
    c["c_tri"] = np.stack([(j <= l), (j >= l), (j > l), (j < l)]).astype(np.float32)
    return c


NS = 3
CORE_SEQS = [[0, 1, 2], [3, 4, 5], [6, 7, 8], [9, 10, 11], [12, 13], [14, 15], [16, 17], [18, 19]]
WANT = {"ev_w_in", "ev_w_out", "od_w_in", "od_w_out", "xa_wq0", "xa_wkv0", "xa_wo0", "ffn_w_gu0", "ffn_w_down0",
        "xa_wq1", "xa_wkv1", "xa_wo1", "ffn_w_gu1", "ffn_w_down1"}
_NC = None


def build_program():
    nc = bass.Bass("TRN2", target_bir_lowering=False)
    B = Builder(nc, NS, WANT)
    S = B.S
    bx = S.buf("xin", persist=True)
    for s in range(NS):
        B.layer0(s, B.x[s], bx, B.xl, B.b_xl)
        B.layer1(s, B.xl, B.b_xl, B.y[s], B.b_y)
    S.barrier()
    return nc


def kernel(**inputs):
    global _NC
    f = lambda a: np.ascontiguousarray(np.asarray(a, dtype=np.float32))
    xs = [inputs["x_prompt"][i] for i in range(16)] + [inputs["x_sample"][i] for i in range(4)]
    ms = [inputs["mem_prompt"][i] for i in range(16)] + [inputs["mem_sample"][i] for i in range(4)]
    shared = dict(
        norm_g=f(inputs["norm_g"]),
        ev_w_in=f(inputs["ev_w_in"][0]), ev_w_out=f(inputs["ev_w_out"][0]),
        od_w_in=f(inputs["od_w_in"][0]), od_w_out=f(inputs["od_w_out"][0]),
        qk_gain=f(np.stack([inputs["ev_q_gain"][0], inputs["ev_k_gain"][0]])),
        od_conv_w=f(np.asarray(inputs["od_conv_w"][0]).reshape(5, 48, 128).transpose(2, 1, 0)),
        od_conv_b=f(np.asarray(inputs["od_conv_b"][0]).reshape(48, 128).T),
        od_a_log=f(np.asarray(inputs["od_a_log"][0]).reshape(128)),
        od_dt_bias=f(np.asarray(inputs["od_dt_bias"][0]).reshape(128)),
        od_d=f(inputs["od_d"][0]), od_norm_g=f(inputs["od_norm_g"][0]),
    )
    for l in range(2):
        for nm in ("xa_wq", "xa_wkv", "xa_wo", "ffn_w_gu", "ffn_w_down"):
            shared["%s%d" % (nm, l)] = f(inputs[nm][l])
    shared.update(host_consts())
    in_maps = []
    for c in range(8):
        ids = list(CORE_SEQS[c])
        while len(ids) < NS:
            ids.append(ids[0])
        m = dict(shared)
        m["x"] = f(np.stack([xs[i] for i in ids]))
        m["mem"] = f(np.stack([ms[i] for i in ids]))
        in_maps.append(m)
    if _NC is None:
        _NC = build_program()
    res = run_bass_kernel_spmd(_NC, in_maps, core_ids=list(range(8)))
    outs = [None] * 20
    for c in range(8):
        y = np.asarray(res.results[c]["y"], dtype=np.float32)
        for slot, i in enumerate(CORE_SEQS[c]):
            outs[i] = y[slot]
    y_prompt = np.stack(outs[:16]).astype(np.float32)
    y_sample = np.stack(outs[16:]).astype(np.float32)
    return (y_prompt, y_sample)
```

```python
import numpy as np
import ml_dtypes
import concourse.bass as bass
import concourse.mybir as mybir
from concourse.bass_utils import run_bass_kernel_spmd

F32 = mybir.dt.float32
BF16 = mybir.dt.bfloat16
AF = mybir.ActivationFunctionType
ALU = mybir.AluOpType
AX = mybir.AxisListType

T = 2048
D = 2048
NT = 16
EPS = 1e-6
LIM = 30000


class DSem:
    __slots__ = ("sem", "cnt")

    def __init__(self, sem):
        self.sem = sem
        self.cnt = 0


class Buf:
    __slots__ = ("name", "w", "r", "ds", "excl")

    def __init__(self, name):
        self.name = name
        self.w = {}
        self.r = {}
        self.ds = None
        self.excl = False


class Sched:
    def __init__(self, nc):
        self.nc = nc
        self.E = {"pe": nc.tensor, "act": nc.scalar, "dve": nc.vector, "pool": nc.gpsimd, "sp": nc.sync}
        self.ctr = {}
        self.nsem = 0
        for k in ("pe", "act", "dve", "pool"):
            self._newctr(k)
        self.waited = {e: {} for e in self.E}
        self.free_ds = []
        self.live_ds = []
        self.persist = []
        self.ninst = 0
        self.nwait = 0

    def _sem(self, name):
        self.nsem += 1
        return self.nc.alloc_semaphore(name="%s_%d" % (name, self.nsem))

    def _newctr(self, k):
        self.ctr[k] = [self._sem("c" + k), 0]

    def buf(self, name="b", persist=False):
        b = Buf(name)
        if persist:
            self.persist.append(b)
        return b

    def _ensure(self, e, sem, val, ds):
        if ds is not None:
            val = max(val, ds.cnt)
        key = id(sem)
        if self.waited[e].get(key, 0) >= val:
            return
        self.E[e].wait_ge(sem, val)
        self.waited[e][key] = val
        self.nwait += 1

    def _deps(self, e, reads, writes):
        own = self.ctr[e][0] if e in self.ctr else None
        for b in reads:
            for (sem, ds), v in b.w.items():
                if sem is own and e == "pe":
                    continue
                self._ensure(e, sem, v, ds)
            if b.excl:
                for (sem, ds), v in b.r.items():
                    if sem is not own:
                        self._ensure(e, sem, v, ds)
        for b in writes:
            for d in (b.w, b.r):
                for (sem, ds), v in d.items():
                    if sem is own and e == "pe":
                        continue
                    self._ensure(e, sem, v, ds)

    def _record(self, ev, val, reads, writes):
        for b in writes:
            b.w = {ev: val}
            b.r = {}
        for b in reads:
            if b in writes:
                continue
            b.r[ev] = max(b.r.get(ev, 0), val)

    def _tick(self, e, ins):
        c = self.ctr[e]
        if c[1] >= LIM:
            self._newctr(e)
            c = self.ctr[e]
        c[1] += 1
        ins.then_inc(c[0], 1)
        return (c[0], None), c[1]

    def op(self, e, fn, reads=(), writes=()):
        self._deps(e, reads, writes)
        ins = fn(self.E[e])
        ev, val = self._tick(e, ins)
        self._record(ev, val, reads, writes)
        self.ninst += 1
        return ins

    def mm(self, fns, reads=(), writes=()):
        self._deps("pe", reads, writes)
        ins = None
        for fn in fns:
            ins = fn(self.E["pe"])
        ev, val = self._tick("pe", ins)
        self._record(ev, val, reads, writes)
        self.ninst += len(fns)

    def dma(self, q, out, in_, sb, reads=(), writes=(), **kw):
        self._deps(q, reads, writes)
        if sb.ds is None or sb.ds.cnt + 16 > LIM or sb.ds not in self.live_ds:
            sb.ds = self.free_ds.pop() if self.free_ds else None
            if sb.ds is None or sb.ds.cnt + 16 > LIM:
                sb.ds = DSem(self._sem("d"))
                self.all_ds = getattr(self, "all_ds", []) + [sb.ds]
            self.live_ds.append(sb.ds)
        ds = sb.ds
        ins = self.E[q].dma_start(out=out, in_=in_, **kw)
        ds.cnt += 16
        ins.then_inc(ds.sem, 16)
        self._record((ds.sem, ds), ds.cnt, reads, writes)
        self.ninst += 1
        return ins

    def barrier(self):
        for e in self.E:
            for k, c in self.ctr.items():
                if k != e and c[1] > 0:
                    self._ensure(e, c[0], c[1], None)
            for ds in getattr(self, "all_ds", []):
                self._ensure(e, ds.sem, ds.cnt, ds)
        self.free_ds.extend(d for d in self.live_ds if d.cnt + 16 <= LIM)
        self.live_ds = []
        for b in self.persist:
            b.w = {}
            b.r = {}


KB = 1024
RET_H, ATT_H, KV_H = 4, 8, 2
DFF = 5632
NFT = DFF // 128


class KB_:
    pass


class Builder:
    def __init__(self, nc, NS, want):
        self.nc = nc
        self.S = Sched(nc)
        self.NS = NS
        self.uid = 0
        self.rot = 0
        self.erot = 0
        S = self.S
        di = lambda n, s, dt=F32: nc.dram_tensor(n, s, dt, kind="ExternalInput").ap()
        ds = lambda n, s, dt=F32: (nc.dram_tensor(n, s, dt, kind="Internal").ap(), S.buf(n, persist=True))
        self.x = di("x", [NS, T, D])
        self.mem = di("mem", [NS, 256, D])
        self.y = nc.dram_tensor("y", [NS, T, D], F32, kind="ExternalOutput").ap()
        self.b_y = S.buf("y", persist=True)
        self.norm_g = di("norm_g", [2, 7, D])
        self.w = {}
        shapes = dict(ev_w_in=[D, 5632], ev_w_out=[D, D], od_w_in=[D, 10368], od_w_out=[4096, D],
                      xa_wq0=[D, D], xa_wkv0=[D, 2 * D], xa_wo0=[D, D], ffn_w_gu0=[D, 2 * DFF], ffn_w_down0=[DFF, D],
                      xa_wq1=[D, D], xa_wkv1=[D, 2 * D], xa_wo1=[D, D], ffn_w_gu1=[D, 2 * DFF], ffn_w_down1=[DFF, D])
        for k, s in shapes.items():
            if k in want:
                self.w[k] = di(k, s)
        self.qk_gain = di("qk_gain", [2, 128])
        self.c_ident = di("c_ident", [128, 128])
        self.c_rope1 = di("c_rope1", [2, 128, T])
        self.c_rope2 = di("c_rope2", [2, T, 128])
        self.c_rmask = di("c_rmask", [RET_H, 128, 31 * 128])
        self.c_tri = di("c_tri", [4, 128, 128])
        if "od_w_in" in want:
            self.od_conv_w = di("od_conv_w", [128, 48, 5])
            self.od_conv_b = di("od_conv_b", [128, 48])
            self.od_a_log = di("od_a_log", [128])
            self.od_dt_bias = di("od_dt_bias", [128])
            self.od_d = di("od_d", [64])
            self.od_norm_g = di("od_norm_g", [4096])
        self.rqT, self.b_rqT = ds("rqT", [RET_H, 2, 128, T], BF16)
        self.rkT, self.b_rkT = ds("rkT", [RET_H, 2, 128, T], BF16)
        self.rv, self.b_rv = ds("rv", [T, 1024], BF16)
        self.rg, self.b_rg = ds("rg", [T, 1024], F32)
        self.aqT, self.b_aqT = ds("aqT", [ATT_H, 128, T], BF16)
        self.akT, self.b_akT = ds("akT", [KV_H, 128, T], BF16)
        self.av, self.b_av = ds("av", [T, 256], BF16)
        self.catT, self.b_catT = ds("catT", [32, 128, T], BF16)
        self.xs1, self.b_xs1 = ds("xs1", [T, D])
        self.xs2, self.b_xs2 = ds("xs2", [T, D])
        self.xl, self.b_xl = ds("xl", [T, D])
        self.memk, self.b_memk = ds("memk", [16, 128, 256], BF16)
        self.memv, self.b_memv = ds("memv", [256, D], BF16)
        al = lambda n, s, dt: nc.alloc_sbuf_tensor(n, s, dt).ap()
        self.identb = al("identb", [128, 128], BF16)
        self.onesb = al("onesb", [128, 128], BF16)
        self.idf = al("idf", [128, 128], F32)
        self.b_const = S.buf("const")
        self.stat = al("stat", [128, 256], F32)
        self.b_stat = [S.buf("stat%d" % i) for i in range(32)]
        self.srot = 0
        S.dma("sp", self.idf, self.c_ident, self.b_const, writes=[self.b_const])
        S.op("dve", lambda e: e.tensor_copy(out=self.identb, in_=self.idf), reads=[self.b_const], writes=[self.b_const])
        S.op("dve", lambda e: e.memset(self.onesb, 1.0), writes=[self.b_const])
        self.stat2 = al("stat2", [128, 64], F32)
        self.b_stat2 = [S.buf("stat2_%d" % i) for i in range(2)]
        self.junk512 = al("junk512", [128, 512], BF16)
        self.b_junk512 = S.buf("junk512")
        self.banks = [nc.alloc_psum_tensor("bank%d" % i, [128, 512], F32).ap() for i in range(8)]
        self.b_bank = [S.buf("bank%d" % i) for i in range(8)]
        for b in self.b_bank:
            b.excl = True
        self.base = nc.sbuf_base
        assert nc.sbuf_bytes_remaining >= 12 * 16 * KB, nc.sbuf_bytes_remaining

    def A(self, off, shape, dt):
        self.uid += 1
        return self.nc.alloc_sbuf_tensor_at("a%d" % self.uid, shape, dt, offset=self.base + off).ap()

    def st(self, n=1):
        i = self.srot % 32
        self.srot += 1
        return self.stat[:, i * 8:i * 8 + n], self.b_stat[i]

    def bank(self):
        i = self.rot % getattr(self, "nrot", 8)
        self.rot += 1
        return self.banks[i], self.b_bank[i]

    def ev_eng(self):
        self.erot += 1
        return "act" if self.erot % 2 else "dve"

    def copy(self, eng, out, in_, reads, writes):
        if eng == "act":
            self.S.op("act", lambda e: e.activation(out=out, in_=in_, func=AF.Copy), reads=reads, writes=writes)
        else:
            self.S.op(eng, lambda e: e.tensor_copy(out=out, in_=in_), reads=reads, writes=writes)

    def rstd_from_ss(self, ss, b_ss, n, scale, eps=EPS):
        S = self.S
        S.op("dve", lambda e: e.tensor_scalar(out=ss, in0=ss, scalar1=scale, scalar2=eps, op0=ALU.mult, op1=ALU.add), reads=[b_ss], writes=[b_ss])
        S.op("act", lambda e: e.activation(out=ss, in_=ss, func=AF.Sqrt), reads=[b_ss], writes=[b_ss])
        S.op("dve", lambda e: e.reciprocal(out=ss, in_=ss), reads=[b_ss], writes=[b_ss])

    def norm_T(self, xt, b_x, gB, b_gB, xn, b_xn, junk, b_junk, dst, b_dst):
        S = self.S
        ss, b_ss = self.st(1)
        S.op("act", lambda e: e.activation(out=junk, in_=xt, func=AF.Square, accum_out=ss), reads=[b_x], writes=[b_junk, b_ss])
        self.rstd_from_ss(ss, b_ss, 1, 1.0 / D)
        S.op("dve", lambda e: e.scalar_tensor_tensor(out=xn, in0=xt, scalar=ss, in1=gB, op0=ALU.mult, op1=ALU.mult),
             reads=[b_x, b_ss, b_gB], writes=[b_xn])
        self.transpose_to(xn, b_xn, 16, dst, b_dst)

    def transpose_to(self, src, b_src, ntile, dst, b_dst, bank=None):
        S = self.S
        for g0 in range(0, ntile, 4):
            n = min(4, ntile - g0)
            bk, b_bk = self.bank() if bank is None else (self.banks[bank], self.b_bank[bank])
            pv = bk.bitcast(BF16)[:, 0:512].rearrange("p (a b) -> p a b", a=4)
            for j in range(n):
                S.op("pe", lambda e, j=j: e.transpose(out=pv[:, j, :], in_=src[:, (g0 + j) * 128:(g0 + j + 1) * 128], identity=self.identb),
                     reads=[b_src, self.b_const], writes=[b_bk])
            self.copy(self.ev_eng(), dst[:, g0:g0 + n, :], pv[:, 0:n, :], [b_bk], [b_dst])

    def load_gB(self, off, layer, slot):
        gB = self.A(off, [128, D], F32)
        b = self.S.buf("gB")
        self.S.dma("sp", gB, self.norm_g[layer, slot, :].partition_broadcast(128), b, writes=[b])
        return gB, b

    def phase_A(self, xsrc, b_xsrc, layer, ntok=T, gslot=0, hoff=0):
        S = self.S
        S.barrier()
        ntt = ntok // 128
        hT = self.A(hoff, [128, 16, ntok], BF16)
        b_hT = S.buf("hT")
        o = hoff + 32 * ntok
        gB, b_gB = self.load_gB(o, layer, gslot)
        xt = [self.A(o + 8 * KB + i * 8 * KB, [128, D], F32) for i in range(2)]
        b_xt = [S.buf("xt") for i in range(2)]
        xn = [self.A(o + 24 * KB + i * 4 * KB, [128, D], BF16) for i in range(2)]
        b_xn = [S.buf("xn") for i in range(2)]
        junk = self.A(o + 32 * KB, [128, D], BF16)
        b_junk = S.buf("junk")
        for tt in range(ntt):
            i = tt % 2
            S.dma("sp", xt[i], xsrc[tt * 128:(tt + 1) * 128, :], b_xt[i], reads=[b_xsrc], writes=[b_xt[i]])
            self.norm_T(xt[i], b_xt[i], gB, b_gB, xn[i], b_xn[i], junk, b_junk, hT[:, :, tt * 128:(tt + 1) * 128], b_hT)
        return hT, b_hT

    def wbufs(self, off, n=2):
        self.wb = [self.A(off + i * 16 * KB, [128, 16, 512], BF16) for i in range(n)]
        self.b_wb = [self.S.buf("wb") for i in range(n)]
        self.wi = 0

    def wload(self, wd, k0, kp, c0, nc_):
        i = self.wi % len(self.wb)
        self.wi += 1
        wb, b = self.wb[i], self.b_wb[i]
        self.S.dma("pool", wb[:, 0:kp, 0:nc_], wd[k0 * 128:(k0 + kp) * 128, c0:c0 + nc_].rearrange("(a p) c -> p a c", p=128), b, writes=[b])
        return wb, b

    def tm_linear(self, srcT, b_src, KT, wd, c0, nct, ntt, evac, cw=512):
        S = self.S
        pieces = [(k0, min(16, KT - k0)) for k0 in range(0, KT, 16)]
        for ct in range(nct):
            if len(pieces) == 1:
                wb, b_wb = self.wload(wd, 0, KT, c0 + ct * cw, cw)
                for tt in range(ntt):
                    bk, b_bk = self.bank()
                    S.mm([(lambda e, k=k: e.matmul(bk[:, 0:cw], lhsT=srcT[:, k, tt * 128:(tt + 1) * 128], rhs=wb[:, k, 0:cw], start=(k == 0), stop=(k == KT - 1)))
                          for k in range(KT)], reads=[b_src, b_wb], writes=[b_bk])
                    evac(ct, tt, bk[:, 0:cw], b_bk)
            else:
                assert ntt <= 4
                bks = [((ct % 2) * 4 + tt) for tt in range(ntt)]
                for pi, (k0, kp) in enumerate(pieces):
                    wb, b_wb = self.wload(wd, k0, kp, c0 + ct * cw, cw)
                    for tt in range(ntt):
                        bk, b_bk = self.banks[bks[tt]], self.b_bank[bks[tt]]
                        S.mm([(lambda e, k=k: e.matmul(bk[:, 0:cw], lhsT=srcT[:, k0 + k, tt * 128:(tt + 1) * 128], rhs=wb[:, k, 0:cw],
                                                       start=(pi == 0 and k == 0), stop=(pi == len(pieces) - 1 and k == kp - 1)))
                              for k in range(kp)], reads=[b_src, b_wb], writes=[b_bk])
                for tt in range(ntt):
                    evac(ct, tt, self.banks[bks[tt]][:, 0:cw], self.b_bank[bks[tt]])

    def fm_linear(self, srcT, b_src, wd, c0, ncols, chunks, evac, pair=False, pc=512):
        S = self.S
        for p0 in range(0, ncols, pc):
            n = min(pc, ncols - p0)
            wb, b_wb = self.wload(wd, 0, 16, c0 + p0, n)
            ncc = n // 128

            def one(cc, t0, tn):
                bk, b_bk = self.bank()
                S.mm([(lambda e, k=k: e.matmul(bk[:, 0:tn], lhsT=wb[:, k, cc * 128:(cc + 1) * 128], rhs=srcT[:, k, t0:t0 + tn], start=(k == 0), stop=(k == 15)))
                      for k in range(16)], reads=[b_src, b_wb], writes=[b_bk])
                return bk[:, 0:tn], b_bk
            if pair:
                for cp in range(ncc // 2):
                    for ci, (t0, tn) in enumerate(chunks):
                        r0 = one(2 * cp, t0, tn)
                        r1 = one(2 * cp + 1, t0, tn)
                        evac((p0 // 128) // 2 + cp, ci, r0, r1)
            else:
                for cc in range(ncc):
                    for ci, (t0, tn) in enumerate(chunks):
                        ps, b_ps = one(cc, t0, tn)
                        evac(p0 // 128 + cc, ci, ps, b_ps)

    def qknorm_rope(self, ps, b_ps, nh, gi, tt, P):
        S = self.S
        n = nh * 128
        if P.get("tab_gi") != gi:
            P["tab_gi"] = gi
            g3 = P["gainB"][:, gi, :]
            S.op("dve", lambda e: e.tensor_tensor(out=P["COSg"], in0=P["COS"], in1=P["gainB"][:, gi:gi + 1, :].broadcast_to([128, 16, 128]), op=ALU.mult),
                 reads=[P["b_rope2"], P["b_gainB"]], writes=[P["b_tabg"]])
            v = lambda ap, f: ap.rearrange("p a (r f i) -> p a r f i", r=2, f=2, i=32)[:, :, :, f, :]
            for f in range(2):
                gsw = g3.rearrange("p (r f i) -> p r f i", r=2, f=2, i=32)[:, :, 1 - f, :].unsqueeze(1).broadcast_to([128, 16, 2, 32])
                S.op("dve", lambda e, f=f, gsw=gsw: e.tensor_tensor(out=v(P["SINSg"], f), in0=v(P["SINS"], f), in1=gsw, op=ALU.mult),
                     reads=[P["b_rope2"], P["b_gainB"]], writes=[P["b_tabg"]])
        pp = P["par"] = 1 - P.get("par", 0)
        sqj, b_sqj, q1, b_q1, tmpr, b_tmpr, qo, b_qo, qc, b_qc = [P[k][pp] for k in ("sqj", "b_sqj", "q1", "b_q1", "tmpr", "b_tmpr", "qo", "b_qo", "qc", "b_qc")]
        ss, b_ss = self.st(nh)
        v3 = lambda ap: ap.rearrange("p (h d) -> p h d", h=nh)
        v5 = lambda ap, f: ap.rearrange("p (h r f i) -> p h r f i", h=nh, r=2, f=2, i=32)[:, :, :, f, :]
        t4 = lambda tab, f: tab[:, tt, :].rearrange("p (r f i) -> p r f i", r=2, f=2, i=32)[:, :, f, :].unsqueeze(1).broadcast_to([128, nh, 2, 32])
        S.op("act", lambda e: e.activation(out=sqj[:, 0:n], in_=ps, func=AF.Square), reads=[b_ps], writes=[b_sqj])
        S.op("act", lambda e: e.activation(out=qc[:, 0:n], in_=ps, func=AF.Copy), reads=[b_ps], writes=[b_qc])
        S.op("dve", lambda e: e.tensor_tensor(out=v3(q1[:, 0:n]), in0=v3(ps), in1=P["COSg"][:, tt:tt + 1, :].broadcast_to([128, nh, 128]), op=ALU.mult),
             reads=[b_ps, P["b_tabg"]], writes=[b_q1])
        for f in range(2):
            S.op("pool", lambda e, f=f: e.tensor_tensor(out=v5(tmpr[:, 0:n], f), in0=v5(qc[:, 0:n], 1 - f), in1=t4(P["SINSg"], f), op=ALU.mult),
                 reads=[b_qc, P["b_tabg"]], writes=[b_tmpr])
        S.op("dve", lambda e: e.tensor_reduce(out=ss, in_=sqj[:, 0:n].rearrange("p (h d) -> p h d", h=nh), axis=AX.X, op=ALU.add), reads=[b_sqj], writes=[b_ss])
        self.rstd_from_ss(ss, b_ss, nh, 1.0 / 128)
        S.op("dve", lambda e: e.tensor_tensor(out=q1[:, 0:n], in0=q1[:, 0:n], in1=tmpr[:, 0:n], op=ALU.add), reads=[b_q1, b_tmpr], writes=[b_q1])
        S.op("dve", lambda e: e.tensor_tensor(out=v3(qo[:, 0:n]), in0=v3(q1[:, 0:n]), in1=ss.unsqueeze(2).broadcast_to([128, nh, 128]), op=ALU.mult),
             reads=[b_q1, b_ss], writes=[b_qo])
        return qo, b_qo

    def phase_L0proj(self, hT, b_hT):
        S = self.S
        S.barrier()
        wd = self.w["ev_w_in"]
        self.wbufs(64 * KB)
        o = 96 * KB
        c1 = self.A(o, [128, T], F32)
        s1 = self.A(o + 8 * KB, [128, T], F32)
        b_rope1 = S.buf("rope1")
        S.dma("sp", c1, self.c_rope1[0], b_rope1, writes=[b_rope1])
        S.dma("sp", s1, self.c_rope1[1], b_rope1, writes=[b_rope1])
        COS = self.A(o + 16 * KB, [128, 16, 128], F32)
        SINS = self.A(o + 24 * KB, [128, 16, 128], F32)
        b_rope2 = S.buf("rope2")
        S.dma("sp", COS, self.c_rope2[0].rearrange("(a p) d -> p a d", p=128), b_rope2, writes=[b_rope2])
        S.dma("sp", SINS, self.c_rope2[1].rearrange("(a p) d -> p a d", p=128), b_rope2, writes=[b_rope2])
        o = 128 * KB
        tmp = [self.A(o + i * 2 * KB, [128, 512], F32) for i in range(4)]
        b_tmp = [S.buf("tmp") for i in range(4)]
        o += 8 * KB
        sto = [self.A(o + i * 2 * KB, [128, 2, 512], BF16) for i in range(2)]
        b_sto = [S.buf("sto") for i in range(2)]
        o += 4 * KB
        stv = [self.A(o + i * KB, [128, 512], BF16) for i in range(2)]
        b_stv = [S.buf("stv") for i in range(2)]
        o += 2 * KB
        stg = [self.A(o + i * 2 * KB, [128, 512], F32) for i in range(2)]
        b_stg = [S.buf("stg") for i in range(2)]
        o += 4 * KB
        P = {}
        for nm, sz, dt in (("q1", 2, F32), ("sqj", 2, F32), ("tmpr", 2, F32), ("qo", 1, BF16), ("qc", 2, F32)):
            P[nm] = [self.A(o + i * sz * KB, [128, 512], dt) for i in range(2)]
            P["b_" + nm] = [S.buf(nm) for i in range(2)]
            o += 2 * sz * KB
        qst = [self.A(o + i * 4 * KB, [128, 4, 512], BF16) for i in range(2)]
        b_qst = [S.buf("qst") for i in range(2)]
        o += 8 * KB
        P["gainB"] = self.A(o, [128, 2, 128], F32)
        P["b_gainB"] = S.buf("gainB")
        S.dma("sp", P["gainB"], self.qk_gain.partition_broadcast(128), P["b_gainB"], writes=[P["b_gainB"]])
        P["COS"], P["SINS"], P["b_rope2"] = COS, SINS, b_rope2
        o += KB
        P["COSg"] = self.A(o, [128, 16, 128], F32)
        P["SINSg"] = self.A(o + 8 * KB, [128, 16, 128], F32)
        P["b_tabg"] = S.buf("tabg")
        o += 16 * KB
        assert o <= 192 * KB, o
        chunks = [(i * 512, 512) for i in range(4)]
        cnt = [0]

        def mk_rope(dst, b_dst):
            def ev(hp, ci, r0, r1):
                (x1, b1), (x2, b2) = r0, r1
                t0 = ci * 512
                i = cnt[0] % 2
                cnt[0] += 1
                cs, sn = c1[:, t0:t0 + 512], s1[:, t0:t0 + 512]
                S.op("dve", lambda e: e.tensor_tensor(out=tmp[0], in0=x1, in1=cs, op=ALU.mult), reads=[b1, b_rope1], writes=[b_tmp[0]])
                S.op("dve", lambda e: e.tensor_tensor(out=tmp[1], in0=x2, in1=sn, op=ALU.mult), reads=[b2, b_rope1], writes=[b_tmp[1]])
                S.op("pool", lambda e: e.tensor_tensor(out=sto[i][:, 0, :], in0=tmp[0], in1=tmp[1], op=ALU.subtract), reads=[b_tmp[0], b_tmp[1]], writes=[b_sto[i]])
                S.op("dve", lambda e: e.tensor_tensor(out=tmp[2], in0=x2, in1=cs, op=ALU.mult), reads=[b2, b_rope1], writes=[b_tmp[2]])
                S.op("dve", lambda e: e.tensor_tensor(out=tmp[3], in0=x1, in1=sn, op=ALU.mult), reads=[b1, b_rope1], writes=[b_tmp[3]])
                S.op("pool", lambda e: e.tensor_tensor(out=sto[i][:, 1, :], in0=tmp[2], in1=tmp[3], op=ALU.add), reads=[b_tmp[2], b_tmp[3]], writes=[b_sto[i]])
                S.dma("sp", dst[hp, :, :, t0:t0 + 512].rearrange("j p t -> p j t"), sto[i], b_sto[i], reads=[b_sto[i]], writes=[b_dst])
            return ev
        self.fm_linear(hT, b_hT, wd, 0, 1024, chunks, mk_rope(self.rqT, self.b_rqT), pair=True)
        self.fm_linear(hT, b_hT, wd, 1024, 1024, chunks, mk_rope(self.rkT, self.b_rkT), pair=True)

        def ev_v(ct, tt, ps, b_ps):
            i = cnt[0] % 2
            cnt[0] += 1
            self.copy(self.ev_eng(), stv[i], ps, [b_ps], [b_stv[i]])
            S.dma("sp", self.rv[tt * 128:(tt + 1) * 128, ct * 512:(ct + 1) * 512], stv[i], b_stv[i], reads=[b_stv[i]], writes=[self.b_rv])
        self.tm_linear(hT, b_hT, 16, wd, 2048, 2, 16, ev_v)

        def ev_g(ct, tt, ps, b_ps):
            i = cnt[0] % 2
            cnt[0] += 1
            S.op("act", lambda e: e.activation(out=stg[i], in_=ps, func=AF.Silu), reads=[b_ps], writes=[b_stg[i]])
            S.dma("sp", self.rg[tt * 128:(tt + 1) * 128, ct * 512:(ct + 1) * 512], stg[i], b_stg[i], reads=[b_stg[i]], writes=[self.b_rg])
        self.tm_linear(hT, b_hT, 16, wd, 3072, 2, 16, ev_g)

        pend = [None]

        def flush():
            if pend[0] is not None:
                f, pend[0] = pend[0], None
                f()

        def ev_q(ct, tt, ps, b_ps):
            qo, b_qo = self.qknorm_rope(ps, b_ps, 4, 0, tt, P)
            flush()

            def fin():
                i = (tt // 4) % 2
                self.transpose_to(qo, b_qo, 4, qst[i][:, :, (tt % 4) * 128:(tt % 4 + 1) * 128], b_qst[i])
                if tt % 4 == 3:
                    t0 = (tt // 4) * 512
                    S.dma("sp", self.aqT[ct * 4:ct * 4 + 4, :, t0:t0 + 512].rearrange("h p t -> p h t"), qst[i], b_qst[i], reads=[b_qst[i]], writes=[self.b_aqT])
            pend[0] = fin
        self.tm_linear(hT, b_hT, 16, wd, 4096, 2, 16, ev_q)
        flush()

        def ev_kv(ct, tt, ps, b_ps):
            qo, b_qo = self.qknorm_rope(ps[:, 0:256], b_ps, 2, 1, tt, P)
            flush()

            def fin():
                i = (tt // 4) % 2
                self.transpose_to(qo, b_qo, 2, qst[i][:, 0:2, (tt % 4) * 128:(tt % 4 + 1) * 128], b_qst[i])
                if tt % 4 == 3:
                    t0 = (tt // 4) * 512
                    S.dma("sp", self.akT[:, :, t0:t0 + 512].rearrange("h p t -> p h t"), qst[i][:, 0:2, :], b_qst[i], reads=[b_qst[i]], writes=[self.b_akT])
            pend[0] = fin
            j = cnt[0] % 2
            cnt[0] += 1
            self.copy("dve", stv[j][:, 0:256], ps[:, 256:512], [b_ps], [b_stv[j]])
            S.dma("sp", self.av[tt * 128:(tt + 1) * 128, :], stv[j][:, 0:256], b_stv[j], reads=[b_stv[j]], writes=[self.b_av])
        self.tm_linear(hT, b_hT, 16, wd, 5120, 1, 16, ev_kv)
        flush()

    def bank4(self):
        return self.bank()

    def phase_ret(self):
        S = self.S
        S.barrier()
        self.nrot = 3
        SETB = 56 * KB
        o = 2 * SETB
        pT = [self.A(o + i * KB, [128, 512], BF16) for i in range(3)]
        b_pT = [S.buf("pT") for i in range(3)]
        o += 3 * KB
        osb = [self.A(o + i * KB, [128, 256], F32) for i in range(4)]
        b_osb = [S.buf("osb") for i in range(4)]
        o += 4 * KB
        junk = self.A(o, [128, 256], F32)
        b_junk = S.buf("junk")
        o += KB
        on = [self.A(o + i * KB, [128, 256], F32) for i in range(4)]
        b_on = [S.buf("on") for i in range(4)]
        o += 4 * KB
        og = [self.A(o + i * 512, [128, 256], BF16) for i in range(4)]
        b_og = [S.buf("og") for i in range(4)]
        o += 2 * KB
        stT = [self.A(o + i * 2 * KB, [128, 2, 512], BF16) for i in range(2)]
        b_stT = [S.buf("stT") for i in range(2)]
        pi = 0
        sets = []
        for k in range(2):
            so = k * SETB
            sets.append(((self.A(so, [128, 2, T], BF16), self.A(so + 8 * KB, [128, 2, T], BF16), self.A(so + 16 * KB, [128, 16, 256], BF16),
                          self.A(so + 24 * KB, [128, 16, 256], F32), self.A(so + 40 * KB, [128, 31 * 128], F32)),
                         [S.buf(n) for n in ("rq", "rk", "rv", "rg", "strip")]))

        def load_head(hh):
            (qT, kT, v, gate, strip), (b_q, b_k, b_v, b_g, b_s) = sets[hh % 2]
            S.dma("sp", qT, self.rqT[hh].rearrange("j p t -> p j t"), b_q, reads=[self.b_rqT], writes=[b_q])
            S.dma("sp", kT, self.rkT[hh].rearrange("j p t -> p j t"), b_k, reads=[self.b_rkT], writes=[b_k])
            S.dma("sp", v, self.rv[:, hh * 256:(hh + 1) * 256].rearrange("(a p) c -> p a c", p=128), b_v, reads=[self.b_rv], writes=[b_v])
            S.dma("sp", gate, self.rg[:, hh * 256:(hh + 1) * 256].rearrange("(a p) c -> p a c", p=128), b_g, reads=[self.b_rg], writes=[b_g])
            S.dma("sp", strip, self.c_rmask[hh], b_s, writes=[b_s])
        load_head(0)
        pending = []

        def hooks(j):
            if pending:
                if j == 2 and pending[0][0] is not None:
                    pending[0][0]()
                    pending[0][0] = None
                if j == 8:
                    if pending[0][0] is not None:
                        pending[0][0]()
                    pending.pop(0)[1]()
        for hh in range(RET_H):
            (qT, kT, v, gate, strip), (b_q, b_k, b_v, b_g, b_s) = sets[hh % 2]
            for c in range(4):
                ob = 4 + 2 * (c % 2)
                OA, OB = self.banks[ob].rearrange("p (a b) -> p a b", a=2), self.banks[ob + 1].rearrange("p (a b) -> p a b", a=2)
                b_OA, b_OB = self.b_bank[ob], self.b_bank[ob + 1]

                def smm(j):
                    bk, b_bk = self.bank4()
                    S.mm([(lambda e, kk=kk: e.matmul(bk, lhsT=kT[:, kk, j * 128:(j + 1) * 128], rhs=qT[:, kk, c * 512:(c + 1) * 512], start=(kk == 0), stop=(kk == 1)))
                          for kk in range(2)], reads=[b_q, b_k], writes=[b_bk])
                    return bk, b_bk
                nxt = smm(0)
                for j in range(16):
                    bk, b_bk = nxt
                    if j < 15:
                        nxt = smm(j + 1)
                    p, b_p = pT[pi % 3], b_pT[pi % 3]
                    pi += 1
                    d0 = c * 4 - j + 15
                    S.op("dve", lambda e: e.tensor_tensor(out=p, in0=bk, in1=strip[:, d0 * 128:d0 * 128 + 512], op=ALU.mult), reads=[b_bk, b_s], writes=[b_p])
                    S.mm([(lambda e, i=i: e.matmul((OA if i < 2 else OB)[:, i % 2, :], lhsT=p[:, i * 128:(i + 1) * 128], rhs=v[:, j, :], start=(j == 0 and i % 2 == 0), stop=(j == 15), skip_group_check=True))
                          for i in range(4)], reads=[b_p, b_v], writes=[b_OA, b_OB])
                    hooks(j)
                    if c == 0 and j == 9 and hh + 1 < RET_H:
                        load_head(hh + 1)

                def mk_post(hh=hh, c=c, OA=OA, OB=OB, b_OA=b_OA, b_OB=b_OB, gate=gate, b_g=b_g):
                    si = c % 2
                    Os = [((OA if i < 2 else OB)[:, i % 2, :], (b_OA if i < 2 else b_OB)) for i in range(4)]

                    def elem():
                        sta, b_sta = self.st(8)
                        stb_, b_stb = self.st(8)
                        for i in range(4):
                            O, b_O = Os[i]
                            S.op("act", lambda e, i=i, O=O: e.activation(out=osb[i], in_=O, func=AF.Copy, accum_out=sta[:, i:i + 1]), reads=[b_O], writes=[b_osb[i], b_sta])
                            S.op("act", lambda e, i=i, O=O: e.activation(out=junk, in_=O, func=AF.Square, accum_out=sta[:, 4 + i:5 + i]), reads=[b_O], writes=[b_junk, b_sta])
                        mean, msq, var, nb = stb_[:, 0:4], stb_[:, 4:8], sta[:, 4:8], sta[:, 0:4]
                        S.op("dve", lambda e: e.tensor_scalar(out=mean, in0=sta[:, 0:4], scalar1=1.0 / 256, scalar2=None, op0=ALU.mult), reads=[b_sta], writes=[b_stb])
                        S.op("dve", lambda e: e.tensor_tensor(out=msq, in0=mean, in1=mean, op=ALU.mult), reads=[b_stb], writes=[b_stb])
                        S.op("dve", lambda e: e.scalar_tensor_tensor(out=var, in0=var, scalar=1.0 / 256, in1=msq, op0=ALU.mult, op1=ALU.subtract), reads=[b_sta, b_stb], writes=[b_sta])
                        self.rstd_from_ss(var, b_sta, 4, 1.0)
                        S.op("dve", lambda e: e.scalar_tensor_tensor(out=nb, in0=mean, scalar=-1.0, in1=var, op0=ALU.mult, op1=ALU.mult), reads=[b_sta, b_stb], writes=[b_sta])
                        for i in range(4):
                            tt = c * 4 + i
                            S.op("dve", lambda e, i=i: e.tensor_scalar(out=on[i], in0=osb[i], scalar1=var[:, i:i + 1], scalar2=nb[:, i:i + 1], op0=ALU.mult, op1=ALU.add),
                                 reads=[b_osb[i], b_sta], writes=[b_on[i]])
                            S.op("pool", lambda e, i=i, tt=tt: e.tensor_tensor(out=og[i], in0=on[i], in1=gate[:, tt, :], op=ALU.mult), reads=[b_on[i], b_g], writes=[b_og[i]])

                    def trans():
                        for i in range(4):
                            self.transpose_to(og[i], b_og[i], 2, stT[si][:, :, i * 128:(i + 1) * 128], b_stT[si], bank=3)
                        S.dma("sp", self.catT[hh * 2:hh * 2 + 2, :, c * 512:(c + 1) * 512].rearrange("k p t -> p k t"), stT[si], b_stT[si], reads=[b_stT[si]], writes=[self.b_catT])
                    return [elem, trans]
                pending.append(mk_post())
        while pending:
            el, tr = pending.pop(0)
            if el is not None:
                el()
            tr()

    def phase_att(self):
        S = self.S
        S.barrier()
        self.nrot = 4
        SETB = 13 * KB
        o = 2 * SETB
        pT = [self.A(o + i * KB, [128, 512], BF16) for i in range(3)]
        b_pT = [S.buf("pT") for i in range(3)]
        o += 3 * KB
        ob4 = [self.A(o + i * KB, [128, 4, 128], BF16) for i in range(2)]
        b_ob4 = [S.buf("ob4") for i in range(2)]
        o += 2 * KB
        stA = [self.A(o + i * KB, [128, 512], BF16) for i in range(2)]
        b_stA = [S.buf("stA") for i in range(2)]
        pi = 0
        scale = 128.0 ** -0.5
        sets = []
        for k in range(2):
            so = k * SETB
            sets.append(((self.A(so, [128, T], BF16), self.A(so + 4 * KB, [128, T], BF16), self.A(so + 8 * KB, [128, 16, 132], BF16)),
                         [S.buf(n) for n in ("aq", "ak", "av")]))

        def load_head(h):
            (qT, kT, vx), (b_q, b_k, b_v) = sets[h % 2]
            kv = h // 4
            S.dma("sp", qT, self.aqT[h], b_q, reads=[self.b_aqT], writes=[b_q])
            S.dma("sp", kT, self.akT[kv], b_k, reads=[self.b_akT], writes=[b_k])
            S.dma("sp", vx[:, :, 0:128], self.av[:, kv * 128:(kv + 1) * 128].rearrange("(a p) c -> p a c", p=128), b_v, reads=[self.b_av], writes=[b_v])
            S.op("pool", lambda e: e.memset(vx[:, :, 128:129], 1.0), writes=[b_v])
        load_head(0)
        pending = []

        def hooks(j):
            if pending:
                if j == 2 and pending[0][0] is not None:
                    pending[0][0]()
                    pending[0][0] = None
                if j == 8:
                    if pending[0][0] is not None:
                        pending[0][0]()
                    pending.pop(0)[1]()
        for h in range(ATT_H):
            if h + 1 < ATT_H:
                load_head(h + 1)
            (qT, kT, vx), (b_q, b_k, b_v) = sets[h % 2]
            for c in range(4):
                obk = 4 + 2 * (c % 2)
                OA, OB = self.banks[obk].rearrange("p (a b) -> p a b", a=2), self.banks[obk + 1].rearrange("p (a b) -> p a b", a=2)
                b_OA, b_OB = self.b_bank[obk], self.b_bank[obk + 1]

                def smm(j):
                    bk, b_bk = self.bank4()
                    S.mm([lambda e: e.matmul(bk, lhsT=kT[:, j * 128:(j + 1) * 128], rhs=qT[:, c * 512:(c + 1) * 512], start=True, stop=True)],
                         reads=[b_q, b_k], writes=[b_bk])
                    return bk, b_bk
                nxt = smm(0)
                for j in range(16):
                    bk, b_bk = nxt
                    if j < 15:
                        nxt = smm(j + 1)
                    p, b_p = pT[pi % 3], b_pT[pi % 3]
                    pi += 1
                    S.op("act", lambda e: e.activation(out=p, in_=bk, func=AF.Exp, scale=scale), reads=[b_bk], writes=[b_p])
                    S.mm([(lambda e, i=i: e.matmul((OA if i < 2 else OB)[:, i % 2, 0:129], lhsT=p[:, i * 128:(i + 1) * 128], rhs=vx[:, j, 0:129], start=(j == 0 and i % 2 == 0), stop=(j == 15), skip_group_check=True))
                          for i in range(4)], reads=[b_p, b_v], writes=[b_OA, b_OB])
                    hooks(j)

                def mk_post(h=h, c=c, OA=OA, OB=OB, b_OA=b_OA, b_OB=b_OB):
                    si = c % 2

                    def elem():
                        for i in range(4):
                            O, b_O = (OA if i < 2 else OB)[:, i % 2, :], (b_OA if i < 2 else b_OB)
                            st, b_st = self.st(1)
                            S.op("dve", lambda e: e.reciprocal(out=st, in_=O[:, 128:129]), reads=[b_O], writes=[b_st])
                            S.op("dve", lambda e: e.tensor_scalar(out=ob4[si][:, i, :], in0=O[:, 0:128], scalar1=st, scalar2=None, op0=ALU.mult), reads=[b_O, b_st], writes=[b_ob4[si]])

                    def trans():
                        bk, b_bk = self.bank4()
                        pv = bk.bitcast(BF16)[:, 0:512].rearrange("p (a b) -> p a b", a=4)
                        for i in range(4):
                            S.op("pe", lambda e, i=i: e.transpose(out=pv[:, i, :], in_=ob4[si][:, i, :], identity=self.identb), reads=[b_ob4[si], self.b_const], writes=[b_bk])
                        self.copy(self.ev_eng(), stA[si], bk.bitcast(BF16)[:, 0:512], [b_bk], [b_stA[si]])
                        S.dma("sp", self.catT[8 + h, :, c * 512:(c + 1) * 512], stA[si], b_stA[si], reads=[b_stA[si]], writes=[self.b_catT])
                    return [elem, trans]
                pending.append(mk_post())
        while pending:
            el, tr = pending.pop(0)
            if el is not None:
                el()
            tr()

    def phase_mem(self, s, layer):
        S = self.S
        self.nrot = 8
        mT, b_mT = self.phase_A(self.mem[s], S.buf("memin", persist=True), layer, ntok=256, gslot=4, hoff=0)
        wd = self.w["xa_wkv%d" % layer]
        self.wbufs(64 * KB)
        o = 96 * KB
        stk = [self.A(o + i * 2 * KB, [128, 4, 256], BF16) for i in range(2)]
        b_stk = [S.buf("stk") for i in range(2)]
        o += 4 * KB
        stv = [self.A(o + i * KB, [128, 512], BF16) for i in range(2)]
        b_stv = [S.buf("stv") for i in range(2)]
        cnt = [0]

        def ev_k(ctile, ci, ps, b_ps):
            i = (ctile // 4) % 2
            self.copy(self.ev_eng(), stk[i][:, ctile % 4, :], ps, [b_ps], [b_stk[i]])
            if ctile % 4 == 3:
                S.dma("sp", self.memk[ctile - 3:ctile + 1].rearrange("c p m -> p c m"), stk[i], b_stk[i], reads=[b_stk[i]], writes=[self.b_memk])
        self.fm_linear(mT, b_mT, wd, 0, D, [(0, 256)], ev_k)

        def ev_v(ct, tt, ps, b_ps):
            i = cnt[0] % 2
            cnt[0] += 1
            self.copy(self.ev_eng(), stv[i], ps, [b_ps], [b_stv[i]])
            S.dma("sp", self.memv[tt * 128:(tt + 1) * 128, ct * 512:(ct + 1) * 512], stv[i], b_stv[i], reads=[b_stv[i]], writes=[self.b_memv])
        self.tm_linear(mT, b_mT, 16, wd, D, 4, 2, ev_v)

    def resid_stage(self, srcT, b_src, KT, wd, layer, gpost, xold_dram, b_xold, t0, xnew_dram, b_xnew, gpre, dstT, b_dstT, M):
        S = self.S
        XR, b_XR = M["XR"], M["b_XR"]
        gP, b_gP, gN, b_gN = M["gP"], M["b_gP"], M["gN"], M["b_gN"]
        S.dma("sp", gP, self.norm_g[layer, gpost, :].partition_broadcast(128), b_gP, writes=[b_gP])
        if gpre is not None:
            S.dma("sp", gN, self.norm_g[layer, gpre, :].partition_broadcast(128), b_gN, writes=[b_gN])
        k = M["srot"] = 1 - M.get("srot", 0)
        st, b_st = self.stat2[:, k * 32:(k + 1) * 32], self.b_stat2[k]

        def ev(ct, tt, ps, b_ps):
            S.op("dve", lambda e: e.tensor_tensor(out=XR[tt][:, ct * 512:(ct + 1) * 512], in0=ps, in1=gP[:, ct * 512:(ct + 1) * 512], op=ALU.mult),
                 reads=[b_ps, b_gP], writes=[b_XR[tt]])
            S.op("act", lambda e: e.activation(out=M["junk"][:, 0:512], in_=ps, func=AF.Square, accum_out=st[:, tt * 4 + ct:tt * 4 + ct + 1]), reads=[b_ps], writes=[M["b_junk"], b_st])
        def load_xo(tt):
            S.dma("sp", M["xo"][tt % 2], xold_dram[t0 + tt * 128:t0 + (tt + 1) * 128, :], M["b_xo"][tt % 2], reads=[b_xold], writes=[M["b_xo"][tt % 2]])
        load_xo(0)
        load_xo(1)
        self.tm_linear(srcT, b_src, KT, wd, 0, 4, 4, ev)
        S.op("dve", lambda e: e.tensor_reduce(out=st[:, 16:20], in_=st[:, 0:16].rearrange("p (t c) -> p t c", t=4), axis=AX.X, op=ALU.add), reads=[b_st], writes=[b_st])
        self.rstd_from_ss(st[:, 16:20], b_st, 4, 1.0 / D)
        for tt in range(4):
            xo, b_xo = M["xo"][tt % 2], M["b_xo"][tt % 2]
            S.op("dve", lambda e: e.scalar_tensor_tensor(out=XR[tt], in0=XR[tt], scalar=st[:, 16 + tt:17 + tt], in1=xo, op0=ALU.mult, op1=ALU.add),
                 reads=[b_XR[tt], b_st, b_xo], writes=[b_XR[tt]])
            if tt + 2 < 4:
                load_xo(tt + 2)
            S.dma("sp", xnew_dram[t0 + tt * 128:t0 + (tt + 1) * 128, :], XR[tt], b_XR[tt], reads=[b_XR[tt]], writes=[b_xnew])
            if gpre is not None:
                xn, b_xn = M["xn"][tt % 2], M["b_xn"][tt % 2]
                S.op("act", lambda e: e.activation(out=xn, in_=XR[tt], func=AF.Square, accum_out=st[:, 20 + tt:21 + tt]), reads=[b_XR[tt]], writes=[b_xn, b_st])
        if gpre is None:
            return
        self.rstd_from_ss(st[:, 20:24], b_st, 4, 1.0 / D)
        for tt in range(4):
            xn, b_xn = M["xn"][tt % 2], M["b_xn"][tt % 2]
            S.op("dve", lambda e: e.scalar_tensor_tensor(out=xn, in0=XR[tt], scalar=st[:, 20 + tt:21 + tt], in1=gN, op0=ALU.mult, op1=ALU.mult),
                 reads=[b_XR[tt], b_st, b_gN], writes=[b_xn])
            self.transpose_to(xn, b_xn, 16, dstT[:, :, tt * 128:(tt + 1) * 128], b_dstT)

    def phase_chain(self, s, layer, KT_mix, w_out_name, x_in, b_x_in, x_out, b_x_out):
        S = self.S
        S.barrier()
        if True:
            cstop = getattr(self, 'chain_stop', 9)
            M = {}
            M["XR"] = [self.A(i * 8 * KB, [128, D], F32) for i in range(4)]
            M["b_XR"] = [S.buf("XR") for i in range(4)]
            TA = self.A(32 * KB, [128, 16, 512], BF16)
            TB = self.A(48 * KB, [128, 16, 512], BF16)
            b_TA, b_TB = S.buf("TA"), S.buf("TB")
            ACTo = 64 * KB
            self.wbufs(112 * KB)
            M["misc"] = 144 * KB
            o = 160 * KB
            M["xo"] = [self.A(o + i * 8 * KB, [128, D], F32) for i in range(2)]
            M["b_xo"] = [S.buf("xo") for i in range(2)]
            o += 16 * KB
            M["xn"] = [self.A(o + i * 4 * KB, [128, D], BF16) for i in range(2)]
            M["b_xn"] = [S.buf("xn") for i in range(2)]
            o += 8 * KB
            M["junk"] = self.junk512
            M["b_junk"] = self.b_junk512
            M["gP"], M["gN"] = self.A(M["misc"], [128, D], F32), self.A(M["misc"] + 8 * KB, [128, D], F32)
            M["b_gP"], M["b_gN"] = S.buf("gP"), S.buf("gN")
            pT = [self.A(o + i * KB, [128, 512], BF16) for i in range(2)]
            b_pT = [S.buf("pT") for i in range(2)]
            o += 2 * KB
            rinv = self.A(o, [128, 512], F32)
            b_rinv = S.buf("rinv")
            o += 2 * KB
            sg = [self.A(o + i * 2 * KB, [128, 512], F32) for i in range(2)]
            b_sg = [S.buf("sg") for i in range(2)]
            o += 4 * KB
            b_ACT = S.buf("ACT")
            mk = self.A(ACTo, [128, 16, 256], BF16)
            mv = self.A(ACTo + 8 * KB, [128, 2, D], BF16)
            oT = self.A(ACTo + 16 * KB, [128, 16, 512], BF16)
            actT = self.A(ACTo, [128, NFT, 512], BF16)
            cat32 = self.A(ACTo, [128, 32, 512], BF16)
        for blk in range(getattr(self, 'chain_blocks', 4)):
            t0 = blk * 512
            if KT_mix == 16:
                if "nocat" not in getattr(self, "variant", ""):
                    S.dma("sp", TA, self.catT[0:16, :, t0:t0 + 512].rearrange("k p t -> p k t"), b_TA, reads=[self.b_catT], writes=[b_TA])
                src, b_src = TA, b_TA
            else:
                src = cat32
                b_src = b_ACT
                S.dma("sp", src, self.catT[0:32, :, t0:t0 + 512].rearrange("k p t -> p k t"), b_src, reads=[self.b_catT], writes=[b_src])
            self.resid_stage(src, b_src, KT_mix, self.w[w_out_name], layer, 1, x_in, b_x_in, t0, self.xs1, self.b_xs1, 2, TB, b_TB, M)
            if cstop < 2:
                continue
            b_mk = b_mv = b_oT = b_ACT
            S.dma("sp", mk, self.memk.rearrange("c p m -> p c m"), b_mk, reads=[self.b_memk], writes=[b_mk])
            S.dma("sp", mv, self.memv.rearrange("(a p) d -> p a d", p=128), b_mv, reads=[self.b_memv], writes=[b_mv])

            def ev_q(ctile, ci, ps, b_ps):
                self.copy(self.ev_eng(), TA[:, ctile, :], ps, [b_ps], [b_TA])
            self.fm_linear(TB, b_TB, self.w["xa_wq%d" % layer], 0, D, [(0, 512)], ev_q)
            xscale = 512.0 ** -0.5
            pT4, b_pT4 = [pT[0], pT[1], sg[1][:, 0:256].bitcast(BF16), sg[1][:, 256:512].bitcast(BF16)], [b_pT[0], b_pT[1], b_sg[1], b_sg[1]]
            rinv2, b_rinv2 = [rinv, sg[0]], [b_rinv, b_sg[0]]
            for h in range(4):
                pT, b_pT = pT4[(h % 2) * 2:(h % 2) * 2 + 2], b_pT4[(h % 2) * 2:(h % 2) * 2 + 2]
                rinv, b_rinv = rinv2[h % 2], b_rinv2[h % 2]
                for mt in range(2):
                    bk, b_bk = self.bank()
                    S.mm([(lambda e, dd=dd: e.matmul(bk, lhsT=mk[:, h * 4 + dd, mt * 128:(mt + 1) * 128], rhs=TA[:, h * 4 + dd, :], start=(dd == 0), stop=(dd == 3)))
                          for dd in range(4)], reads=[b_mk, b_TA], writes=[b_bk])
                    S.op("act", lambda e: e.activation(out=pT[mt], in_=bk, func=AF.Exp, scale=xscale), reads=[b_bk], writes=[b_pT[mt]])
                bk, b_bk = self.bank()
                S.mm([(lambda e, mt=mt: e.matmul(bk, lhsT=self.onesb, rhs=pT[mt], start=(mt == 0), stop=(mt == 1))) for mt in range(2)],
                     reads=[b_pT[0], b_pT[1], self.b_const], writes=[b_bk])
                S.op("dve", lambda e: e.reciprocal(out=rinv, in_=bk), reads=[b_bk], writes=[b_rinv])
                for ee in range(4):
                    bk, b_bk = self.bank()
                    S.mm([(lambda e, mt=mt: e.matmul(bk, lhsT=mv[:, mt, h * 512 + ee * 128:h * 512 + (ee + 1) * 128], rhs=pT[mt], start=(mt == 0), stop=(mt == 1)))
                          for mt in range(2)], reads=[b_mv, b_pT[0], b_pT[1]], writes=[b_bk])
                    S.op("dve", lambda e: e.tensor_tensor(out=oT[:, h * 4 + ee, :], in0=bk, in1=rinv, op=ALU.mult), reads=[b_bk, b_rinv], writes=[b_oT])
            if cstop < 3:
                continue
            pT, b_pT, rinv, b_rinv = pT4[0:2], b_pT4[0:2], rinv2[0], b_rinv2[0]
            self.resid_stage(oT, b_oT, 16, self.w["xa_wo%d" % layer], layer, 3, self.xs1, self.b_xs1, t0, self.xs2, self.b_xs2, 5, TB, b_TB, M)
            if cstop < 4:
                continue
            b_actT = b_ACT
            wgu = self.w["ffn_w_gu%d" % layer]
            for f2 in range(NFT // 2):
                i = self.wi % 2
                self.wi += 1
                wb, b_wb = self.wb[i], self.b_wb[i]
                S.dma("pool", wb[:, :, 0:256], wgu[:, f2 * 256:(f2 + 1) * 256].rearrange("(a p) c -> p a c", p=128), b_wb, writes=[b_wb])
                S.dma("pool", wb[:, :, 256:512], wgu[:, DFF + f2 * 256:DFF + (f2 + 1) * 256].rearrange("(a p) c -> p a c", p=128), b_wb, writes=[b_wb])
                for cc in range(2):
                    f = f2 * 2 + cc
                    bg, b_bg = self.bank()
                    S.mm([(lambda e, k=k: e.matmul(bg, lhsT=wb[:, k, cc * 128:(cc + 1) * 128], rhs=TB[:, k, :], start=(k == 0), stop=(k == 15))) for k in range(16)],
                         reads=[b_TB, b_wb], writes=[b_bg])
                    bu, b_bu = self.bank()
                    S.mm([(lambda e, k=k: e.matmul(bu, lhsT=wb[:, k, 256 + cc * 128:256 + (cc + 1) * 128], rhs=TB[:, k, :], start=(k == 0), stop=(k == 15))) for k in range(16)],
                         reads=[b_TB, b_wb], writes=[b_bu])
                    S.op("act", lambda e: e.activation(out=sg[f % 2], in_=bg, func=AF.Silu), reads=[b_bg], writes=[b_sg[f % 2]])
                    S.op("dve", lambda e: e.tensor_tensor(out=actT[:, f, :], in0=bu, in1=sg[f % 2], op=ALU.mult), reads=[b_bu, b_sg[f % 2]], writes=[b_actT])
            if cstop < 5:
                continue
            self.resid_stage(actT, b_actT, NFT, self.w["ffn_w_down%d" % layer], layer, 6, self.xs2, self.b_xs2, t0, x_out, b_x_out, None, None, None, M)

    def layer0(self, s, x_in, b_x_in, x_out, b_x_out):
        hT, b_hT = self.phase_A(x_in, b_x_in, 0)
        self.phase_L0proj(hT, b_hT)
        self.phase_ret()
        self.phase_att()
        self.phase_mem(s, 0)
        self.phase_chain(s, 0, 16, "ev_w_out", x_in, b_x_in, x_out, b_x_out)

    def l1_scratch(self):
        nc, S = self.nc, self.S
        if hasattr(self, "zs"):
            return
        ds = lambda n, s, dt=F32: (nc.dram_tensor(n, s, dt, kind="Internal").ap(), S.buf(n, persist=True))
        self.zs, self.b_zs = ds("zs", [T, 4096])
        self.xbcT, self.b_xbcT = ds("xbcT", [48, 128, T], BF16)
        self.dtv, self.b_dtv = ds("dtv", [T, 128])
        self.dta, self.b_dta = ds("dta", [T, 128])

    def phase_L1proj(self, hT, b_hT):
        S = self.S
        S.barrier()
        self.l1_scratch()
        self.nrot = 8
        wd = self.w["od_w_in"]
        self.wbufs(64 * KB)
        o = 96 * KB
        stg = [self.A(o + i * 2 * KB, [128, 512], F32) for i in range(2)]
        b_stg = [S.buf("stg") for i in range(2)]
        o += 4 * KB
        cnt = [0]

        def ev_z(ct, tt, ps, b_ps):
            i = cnt[0] % 2
            cnt[0] += 1
            S.op("act", lambda e: e.activation(out=stg[i], in_=ps, func=AF.Silu), reads=[b_ps], writes=[b_stg[i]])
            S.dma("sp", self.zs[tt * 128:(tt + 1) * 128, ct * 512:(ct + 1) * 512], stg[i], b_stg[i], reads=[b_stg[i]], writes=[self.b_zs])
        self.tm_linear(hT, b_hT, 16, wd, 0, 8, 16, ev_z)
        biasB = self.A(o, [128, 128], F32)
        aB = self.A(o + 512, [128, 128], F32)
        b_cb = S.buf("cb")
        S.dma("sp", biasB, self.od_dt_bias.partition_broadcast(128), b_cb, writes=[b_cb])
        S.dma("sp", aB, self.od_a_log.partition_broadcast(128), b_cb, writes=[b_cb])
        S.op("act", lambda e: e.activation(out=aB, in_=aB, func=AF.Exp), reads=[b_cb], writes=[b_cb])
        S.op("dve", lambda e: e.tensor_scalar(out=aB, in0=aB, scalar1=-1.0, scalar2=None, op0=ALU.mult), reads=[b_cb], writes=[b_cb])
        o += KB
        dtt = [self.A(o + i * KB, [128, 2, 128], F32) for i in range(2)]
        b_dtt = [S.buf("dtt") for i in range(2)]
        o += 2 * KB

        def ev_dt(ct, tt, ps, b_ps):
            i = tt % 2
            d, b_d = dtt[i], b_dtt[i]
            S.op("dve", lambda e: e.tensor_tensor(out=d[:, 0, :], in0=ps, in1=biasB, op=ALU.add), reads=[b_ps, b_cb], writes=[b_d])
            S.op("act", lambda e: e.activation(out=d[:, 0, :], in_=d[:, 0, :], func=AF.Exp), reads=[b_d], writes=[b_d])
            S.op("act", lambda e: e.activation(out=d[:, 0, :], in_=d[:, 0, :], func=AF.Ln, bias=1.0), reads=[b_d], writes=[b_d])
            S.op("dve", lambda e: e.tensor_tensor(out=d[:, 1, :], in0=d[:, 0, :], in1=aB, op=ALU.mult), reads=[b_d, b_cb], writes=[b_d])
            S.dma("sp", self.dtv[tt * 128:(tt + 1) * 128, :], d[:, 0, :], b_d, reads=[b_d], writes=[self.b_dtv])
            S.dma("sp", self.dta[tt * 128:(tt + 1) * 128, :], d[:, 1, :], b_d, reads=[b_d], writes=[self.b_dta])
        self.tm_linear(hT, b_hT, 16, wd, 4096 + 6144, 1, 16, ev_dt, cw=128)
        cw_t = self.A(o, [128, 48, 5], F32)
        cb_t = self.A(o + KB, [128, 48], F32)
        b_cw = S.buf("cw")
        S.dma("sp", cw_t, self.od_conv_w, b_cw, writes=[b_cw])
        S.dma("sp", cb_t, self.od_conv_b, b_cw, writes=[b_cw])
        o += 2 * KB
        raw = [self.A(o + i * 8224, [128, T + 4], F32) for i in range(2)]
        b_raw = [S.buf("raw") for i in range(2)]
        o += 2 * 8224
        acc = [self.A(o + i * 8 * KB, [128, T], F32) for i in range(2)]
        b_acc = [S.buf("acc") for i in range(2)]
        o += 16 * KB
        cvo = [self.A(o + i * 4 * KB, [128, T], BF16) for i in range(2)]
        b_cvo = [S.buf("cvo") for i in range(2)]
        for i in range(2):
            S.op("pool", lambda e, i=i: e.memset(raw[i][:, 0:2], 0.0), writes=[b_raw[i]])
            S.op("pool", lambda e, i=i: e.memset(raw[i][:, T + 2:T + 4], 0.0), writes=[b_raw[i]])

        def ev_x(ctile, ci, ps, b_ps):
            i = ctile % 2
            self.copy("act", raw[i][:, 2 + ci * 512:2 + (ci + 1) * 512], ps, [b_ps], [b_raw[i]])
            if ci == 3:
                a, b_a = acc[i], b_acc[i]
                S.op("dve", lambda e: e.tensor_scalar(out=a, in0=raw[i][:, 0:T], scalar1=cw_t[:, ctile, 0:1], scalar2=None, op0=ALU.mult),
                     reads=[b_raw[i], b_cw], writes=[b_a])
                for k in range(1, 5):
                    S.op("dve", lambda e, k=k: e.scalar_tensor_tensor(out=a, in0=raw[i][:, k:k + T], scalar=cw_t[:, ctile, k:k + 1], in1=a, op0=ALU.mult, op1=ALU.add),
                         reads=[b_raw[i], b_cw, b_a], writes=[b_a])
                S.op("act", lambda e: e.activation(out=cvo[i], in_=a, func=AF.Silu, bias=cb_t[:, ctile:ctile + 1]), reads=[b_a, b_cw], writes=[b_cvo[i]])
                S.dma("sp", self.xbcT[ctile], cvo[i], b_cvo[i], reads=[b_cvo[i]], writes=[self.b_xbcT])
        self.fm_linear(hT, b_hT, wd, 4096, 6144, [(i * 512, 512) for i in range(4)], ev_x)

    def phase_ssd(self):
        S = self.S
        S.barrier()
        self.nrot = 4
        bf = lambda n: S.buf(n)
        o = 0
        tri = self.A(o, [128, 4, 128], F32); o += 2 * KB
        trib = self.A(o, [128, 4, 128], BF16); o += KB
        b_tri = bf("tri")
        S.dma("sp", tri, self.c_tri.rearrange("k p l -> p k l"), b_tri, writes=[b_tri])
        S.op("dve", lambda e: e.tensor_copy(out=trib, in_=tri), reads=[b_tri], writes=[b_tri])
        dtv = self.A(o, [128, 16, 128], F32); o += 8 * KB
        o_dta = o
        dta = self.A(o, [128, 16, 128], F32); o += 8 * KB
        dtmp = self.A(o, [128, 16, 128], F32); o += 8 * KB
        dth = self.A(o, [128, 16, 128], BF16); o += 4 * KB
        dtl = self.A(o, [128, 16, 128], BF16); o += 4 * KB
        dtl32 = self.A(o, [128, 16, 128], F32); o += 8 * KB
        b_dt = bf("dt")
        S.dma("sp", dtv, self.dtv.rearrange("(a p) c -> p a c", p=128), b_dt, reads=[self.b_dtv], writes=[b_dt])
        S.dma("sp", dta, self.dta.rearrange("(a p) c -> p a c", p=128), b_dt, reads=[self.b_dta], writes=[b_dt])
        S.op("dve", lambda e: e.tensor_copy(out=dth, in_=dta), reads=[b_dt], writes=[b_dt])
        S.op("dve", lambda e: e.tensor_tensor(out=dtmp, in0=dta, in1=dth, op=ALU.subtract), reads=[b_dt], writes=[b_dt])
        S.op("dve", lambda e: e.tensor_copy(out=dtl, in_=dtmp), reads=[b_dt], writes=[b_dt])
        S.op("dve", lambda e: e.tensor_copy(out=dtl32, in_=dtl), reads=[b_dt], writes=[b_dt])
        dB = self.A(o, [128, 64], F32); o += 256
        ngB = self.A(o, [128, 512], F32); o += 2 * KB
        b_dB = bf("dB")
        S.dma("sp", dB, self.od_d.partition_broadcast(128), b_dB, writes=[b_dB])
        S.barrier()
        o_reuse = o_dta
        xsT = self.A(o, [128, 4, T], BF16); o += 16 * KB
        xs = self.A(o, [128, 16, 512], BF16); o += 16 * KB
        Bt = self.A(o, [128, 16, 128], BF16); o += 4 * KB
        BT = self.A(o, [128, T], BF16); o += 4 * KB
        CT = self.A(o, [128, T], BF16); o += 4 * KB
        xdt = [self.A(o + i * 16 * KB, [128, 16, 512], BF16) for i in range(2)]
        zg = self.A(o, [128, 16, 512], F32); o += 32 * KB
        y = self.A(o, [128, 16, 512], F32); o += 32 * KB
        NR = 3
        Ah = [self.A(o_reuse + i * 2 * KB, [128, 8, 128], BF16) for i in range(NR)]; o_reuse += NR * 2 * KB
        Al = [self.A(o_reuse + i * 2 * KB, [128, 8, 128], BF16) for i in range(NR)]; o_reuse += NR * 2 * KB
        assert o_reuse <= o_dta + 16 * KB
        Ee = [self.A(o + i * 4 * KB, [128, 1024], F32) for i in range(2)]; o += 2 * 4 * KB
        MT = [self.A(o + i * 2 * KB, [128, 8, 128], BF16) for i in range(NR)]; o += NR * 2 * KB
        cbm = [self.A(o + i * 512, [128, 128], F32) for i in range(2)]; o += 2 * 512
        xd = [self.A(o + i * KB, [128, 512], BF16) for i in range(NR)]; o += NR * KB
        t1 = [self.A(o + i * 2 * KB, [128, 512], F32) for i in range(2)]; o += 2 * 2 * KB
        st32 = [self.A(o + i * 2 * KB, [128, 512], F32) for i in range(2)]; o += 4 * KB
        stb = [self.A(o + i * KB, [128, 512], BF16) for i in range(2)]; o += 2 * KB
        sm = [self.A(o + i * 128, [128, 24], F32) for i in range(NR)]; o += 512
        yb = [self.A(o, [128, 512], BF16), self.junk512]; o += KB
        yst = [self.A(o + i * 4 * KB, [128, 4, 512], BF16) for i in range(2)]; o += 8 * KB
        junk = self.A(o, [128, 512], BF16); o += KB
        M2 = {}
        assert o <= 192 * KB, o
        b_Ah, b_Al, b_MT, b_xd, b_sm = [[bf("w") for i in range(NR)] for _ in range(5)]
        b_Ee, b_cbm, b_t1 = [[bf("w") for i in range(2)] for _ in range(3)]
        b_yst = [bf("yst") for i in range(2)]
        b_st = [bf("st") for i in range(2)]
        b_yb, b_junk = [bf("yb"), self.b_junk512], bf("junk")
        b_xsT, b_xs, b_Bt, b_BT, b_CT, b_X = [bf(n) for n in ("xsT", "xs", "Bt", "BT", "CT", "xdt_zg")]
        b_y = [bf("y") for tt in range(16)]
        Ule, Uge, Sgt, Slt = 0, 1, 2, 3
        h3 = lambda ap: ap.rearrange("p (h q) -> p h q", h=8)
        it = [0]
        for g in range(8):
            S.dma("sp", xsT, self.xbcT[g * 4:(g + 1) * 4].rearrange("c p t -> p c t"), b_xsT, reads=[self.b_xbcT], writes=[b_xsT])
            S.dma("sp", BT, self.xbcT[32 + g], b_BT, reads=[self.b_xbcT], writes=[b_BT])
            S.dma("sp", CT, self.xbcT[40 + g], b_CT, reads=[self.b_xbcT], writes=[b_CT])
            S.dma("sp", ngB, self.od_norm_g[g * 512:(g + 1) * 512].partition_broadcast(128), b_dB, writes=[b_dB])
            for tt in range(16):
                bk, b_bk = self.bank()
                pv = bk.bitcast(BF16)[:, 0:512].rearrange("p (a b) -> p a b", a=4)
                for c in range(4):
                    S.op("pe", lambda e, c=c: e.transpose(out=pv[:, c, :], in_=xsT[:, c, tt * 128:(tt + 1) * 128], identity=self.identb),
                         reads=[b_xsT, self.b_const], writes=[b_bk])
                self.copy(self.ev_eng(), xs[:, tt, :], bk.bitcast(BF16)[:, 0:512], [b_bk], [b_xs])
                bk, b_bk = self.bank()
                pv = bk.bitcast(BF16)[:, 0:128]
                S.op("pe", lambda e: e.transpose(out=pv, in_=BT[:, tt * 128:(tt + 1) * 128], identity=self.identb), reads=[b_BT, self.b_const], writes=[b_bk])
                self.copy(self.ev_eng(), Bt[:, tt, :], pv, [b_bk], [b_Bt])
            for tt in range(16):
                x3 = h3(xs[:, tt, :])
                for dr in range(2):
                    S.op("pool" if dr else "dve", lambda e, dr=dr: e.tensor_tensor(out=h3(xdt[dr][:, tt, :]), in0=x3,
                                                                                   in1=dtv[:, tt, dr * 64 + g * 8:dr * 64 + g * 8 + 8].unsqueeze(2).broadcast_to([128, 8, 64]), op=ALU.mult),
                         reads=[b_xs, b_dt], writes=[b_X])
                S.op("dve", lambda e: e.tensor_tensor(out=h3(y[:, tt, :]), in0=x3, in1=dB[:, g * 8:(g + 1) * 8].unsqueeze(2).broadcast_to([128, 8, 64]), op=ALU.mult),
                     reads=[b_xs, b_dB], writes=[b_y[tt]])

            def indep(dr, ci, tt, step):
                i, j = step % NR, step % 2
                U, Sx = (Ule, Sgt) if dr == 0 else (Uge, Slt)
                hsl = slice(dr * 64 + g * 8, dr * 64 + g * 8 + 8)
                S.op("pool", lambda e: e.tensor_tensor(out=Ah[i], in0=trib[:, U:U + 1, :].broadcast_to([128, 8, 128]),
                                                       in1=dth[:, tt, hsl].unsqueeze(2).broadcast_to([128, 8, 128]), op=ALU.mult),
                     reads=[b_tri, b_dt], writes=[b_Ah[i]])
                for h in range(8):
                    S.op("act", lambda e, h=h: e.activation(out=Al[i][:, h, :], in_=trib[:, U, :], func=AF.Copy, scale=dtl32[:, tt, hsl.start + h:hsl.start + h + 1]),
                         reads=[b_tri, b_dt], writes=[b_Al[i]])
                for half in range(2):
                    bk, b_bk = self.banks[4 + half], self.b_bank[4 + half]
                    S.mm([lambda e: e.matmul(bk, lhsT=trib[:, Sx, :], rhs=Ah[i][:, half * 4:(half + 1) * 4, :].rearrange("p a b -> p (a b)"), start=True, stop=False),
                          lambda e: e.matmul(bk, lhsT=trib[:, Sx, :], rhs=Al[i][:, half * 4:(half + 1) * 4, :].rearrange("p a b -> p (a b)"), start=False, stop=True)],
                         reads=[b_tri, b_Ah[i], b_Al[i]], writes=[b_bk])
                    S.op("act", lambda e: e.activation(out=Ee[j][:, half * 512:(half + 1) * 512], in_=bk, func=AF.Exp), reads=[b_bk], writes=[b_Ee[j]])
                b6, b_b6 = self.banks[6], self.b_bank[6]
                fns = []
                for k, lt in enumerate((trib[:, U, :], trib[:, Sx, :], self.onesb)):
                    fns.append(lambda e, k=k, lt=lt: e.matmul(b6[:, k * 8:(k + 1) * 8], lhsT=lt, rhs=dth[:, tt, hsl], start=(k == 0), stop=False, skip_group_check=True))
                    fns.append(lambda e, k=k, lt=lt: e.matmul(b6[:, k * 8:(k + 1) * 8], lhsT=lt, rhs=dtl[:, tt, hsl], start=False, stop=True, skip_group_check=True))
                fns.append(lambda e: e.matmul(b6[:, 128:256], lhsT=BT[:, tt * 128:(tt + 1) * 128], rhs=CT[:, tt * 128:(tt + 1) * 128], start=False, stop=True, skip_group_check=True))
                S.mm(fns, reads=[b_tri, b_dt, self.b_const, b_BT, b_CT], writes=[b_b6])
                S.op("act", lambda e: e.activation(out=sm[i], in_=b6[:, 0:24], func=AF.Exp), reads=[b_b6], writes=[b_sm[i]])
                S.op("dve", lambda e: e.tensor_tensor(out=cbm[j], in0=b6[:, 128:256], in1=tri[:, U, :], op=ALU.mult), reads=[b_b6, b_tri], writes=[b_cbm[j]])
                S.op("dve", lambda e: e.tensor_tensor(out=MT[i], in0=Ee[j].rearrange("p (a b) -> p a b", a=8), in1=cbm[j].unsqueeze(1).broadcast_to([128, 8, 128]), op=ALU.mult),
                     reads=[b_Ee[j], b_cbm[j]], writes=[b_MT[i]])
            def indep_b(dr, ci, tt, step):
                i, j = step % NR, step % 2
                bY, b_bY = self.bank()
                S.mm([(lambda e, h=h: e.matmul(bY[:, h * 64:(h + 1) * 64], lhsT=MT[i][:, h, :], rhs=xdt[dr][:, tt, h * 64:(h + 1) * 64], start=(h == 0), stop=True, skip_group_check=True))
                      for h in range(8)], reads=[b_MT[i], b_X], writes=[b_bY])
                S.op("dve", lambda e: e.tensor_tensor(out=y[:, tt, :], in0=y[:, tt, :], in1=bY, op=ALU.add), reads=[b_y[tt], b_bY], writes=[b_y[tt]])
                if ci < 15:
                    S.op("pool", lambda e: e.tensor_tensor(out=h3(xd[i]), in0=h3(xdt[dr][:, tt, :]), in1=sm[i][:, 8:16].unsqueeze(2).broadcast_to([128, 8, 64]), op=ALU.mult),
                         reads=[b_X, b_sm[i]], writes=[b_xd[i]])

            def chain(dr, ci, tt, step):
                i, j = step % NR, step % 2
                if ci > 0:
                    bO, b_bO = self.bank()
                    S.mm([lambda e: e.matmul(bO, lhsT=CT[:, tt * 128:(tt + 1) * 128], rhs=stb[dr], start=True, stop=True)], reads=[b_CT, b_st[dr]], writes=[b_bO])
                    S.op("dve", lambda e: e.tensor_tensor(out=h3(t1[j]), in0=h3(bO), in1=sm[i][:, 0:8].unsqueeze(2).broadcast_to([128, 8, 64]), op=ALU.mult),
                         reads=[b_bO, b_sm[i]], writes=[b_t1[j]])
                    S.op("dve", lambda e: e.tensor_tensor(out=y[:, tt, :], in0=y[:, tt, :], in1=t1[j], op=ALU.add), reads=[b_y[tt], b_t1[j]], writes=[b_y[tt]])
                if ci < 15:
                    b7, b_b7 = self.banks[7], self.b_bank[7]
                    S.mm([lambda e: e.matmul(b7, lhsT=Bt[:, tt, :], rhs=xd[i], start=True, stop=True)], reads=[b_Bt, b_xd[i]], writes=[b_b7])
                    if ci == 0:
                        S.op("dve", lambda e: e.tensor_copy(out=st32[dr], in_=b7), reads=[b_b7], writes=[b_st[dr]])
                    else:
                        S.op("dve", lambda e: e.tensor_tensor(out=h3(st32[dr]), in0=h3(st32[dr]), in1=sm[i][:, 16:24].unsqueeze(2).broadcast_to([128, 8, 64]), op=ALU.mult),
                             reads=[b_st[dr], b_sm[i]], writes=[b_st[dr]])
                        S.op("dve", lambda e: e.tensor_tensor(out=st32[dr], in0=st32[dr], in1=b7, op=ALU.add), reads=[b_st[dr], b_b7], writes=[b_st[dr]])
                    S.op("dve", lambda e: e.tensor_copy(out=stb[dr], in_=st32[dr]), reads=[b_st[dr]], writes=[b_st[dr]])
            steps = []
            for ci in range(16):
                steps.append((0, ci, ci))
                steps.append((1, ci, 15 - ci))
            base = it[0]
            indep(*steps[0], base)
            indep_b(*steps[0], base)
            for k in range(len(steps)):
                if k + 1 < len(steps):
                    indep(*steps[k + 1], base + k + 1)
                chain(*steps[k], base + k)
                if k + 1 < len(steps):
                    indep_b(*steps[k + 1], base + k + 1)
            it[0] = base + len(steps)
            S.dma("sp", zg, self.zs[:, g * 512:(g + 1) * 512].rearrange("(a p) c -> p a c", p=128), b_X, reads=[self.b_zs], writes=[b_X])
            kq = M2["srot"] = 1 - M2.get("srot", 0)
            ss16, b_ss16 = self.stat2[:, kq * 32:kq * 32 + 16], self.b_stat2[kq]
            for tt in range(16):
                S.op("dve", lambda e: e.tensor_tensor(out=y[:, tt, :], in0=y[:, tt, :], in1=zg[:, tt, :], op=ALU.mult), reads=[b_y[tt], b_X], writes=[b_y[tt]])
                S.op("act", lambda e: e.activation(out=junk, in_=y[:, tt, :], func=AF.Square, accum_out=ss16[:, tt:tt + 1]), reads=[b_y[tt]], writes=[b_junk, b_ss16])
            self.rstd_from_ss(ss16, b_ss16, 16, 1.0 / 512)
            for tt in range(16):
                S.op("dve", lambda e: e.scalar_tensor_tensor(out=yb[tt % 2], in0=y[:, tt, :], scalar=ss16[:, tt:tt + 1], in1=ngB, op0=ALU.mult, op1=ALU.mult),
                     reads=[b_y[tt], b_ss16, b_dB], writes=[b_yb[tt % 2]])
                si = (tt // 4) % 2
                self.transpose_to(yb[tt % 2], b_yb[tt % 2], 4, yst[si][:, :, (tt % 4) * 128:(tt % 4 + 1) * 128], b_yst[si])
                if tt % 4 == 3:
                    t0 = (tt // 4) * 512
                    S.dma("sp", self.catT[g * 4:(g + 1) * 4, :, t0:t0 + 512].rearrange("k p t -> p k t"), yst[si], b_yst[si], reads=[b_yst[si]], writes=[self.b_catT])
        self.nrot = 8

    def layer1(self, s, x_in, b_x_in, x_out, b_x_out):
        hT, b_hT = self.phase_A(x_in, b_x_in, 1)
        self.phase_L1proj(hT, b_hT)
        self.phase_ssd()
        self.phase_mem(s, 1)
        self.phase_chain(s, 1, 32, "od_w_out", x_in, b_x_in, x_out, b_x_out)


THETA = 10000.0
def host_consts():
    c = {}
    c["c_ident"] = np.eye(128, dtype=np.float32)
    t = np.arange(2048, dtype=np.float32)
    inv = (THETA ** (-np.arange(0, 256, 2, dtype=np.float32) / 256)).astype(np.float32)
    ang = (t[None, :] * inv[:, None]).astype(np.float32)
    c["c_rope1"] = np.stack([np.cos(ang), np.sin(ang)]).astype(np.float32)
    inv2 = (THETA ** (-np.arange(0, 64, 2, dtype=np.float32) / 64)).astype(np.float32)
    row = (np.arange(2048) // 64).astype(np.float32); col = (np.arange(2048) % 64).astype(np.float32)
    ar = (row[:, None] * inv2[None, :]).astype(np.float32); ac = (col[:, None] * inv2[None, :]).astype(np.float32)
    COS = np.concatenate([np.cos(ar), np.cos(ar), np.cos(ac), np.cos(ac)], axis=1)
    SINS = np.concatenate([-np.sin(ar), np.sin(ar), -np.sin(ac), np.sin(ac)], axis=1)
    c["c_rope2"] = np.stack([COS, SINS]).astype(np.float32)
    heads = np.arange(4, dtype=np.float32)
    lgf = np.log1p(-np.exp2(-(5.0 + heads))).astype(np.float32)
    lgb = np.log1p(-np.exp2(-(5.5 + heads))).astype(np.float32)
    dd = np.arange(-15, 16)[None, :, None] * 128 + np.arange(128)[None, None, :] - np.arange(128)[:, None, None]
    dd = dd.astype(np.float64)
    m = np.zeros((4, 128, 31, 128), np.float64)
    for h in range(4):
        m[h] = np.where(dd >= 0, np.exp(np.float64(lgf[h]) * np.maximum(dd, 0)), np.exp(np.float64(lgb[h]) * np.maximum(-dd, 0)))
    c["c_rmask"] = (m / 16.0).astype(np.float32).reshape(4, 128, 31 * 128)
    j = np.arange(128)[:, None]; l = np.arange(128)[None, :]
    c["c_tri"] = np.stack([(j <= l), (j >= l), (j > l), (j < l)]).astype(np.float32)
    return c


NS = 3
CORE_SEQS = [[0, 1, 2], [3, 4, 5], [6, 7, 8], [9, 10, 11], [12, 13], [14, 15], [16, 17], [18, 19]]
WANT = {"ev_w_in", "ev_w_out", "od_w_in", "od_w_out", "xa_wq0", "xa_wkv0", "xa_wo0", "ffn_w_gu0", "ffn_w_down0",
        "xa_wq1", "xa_wkv1", "xa_wo1", "ffn_w_gu1", "ffn_w_down1"}
_NC = None


def build_program():
    nc = bass.Bass("TRN2", target_bir_lowering=False)
    B = Builder(nc, NS, WANT)
    S = B.S
    bx = S.buf("xin", persist=True)
    for s in range(NS):
        B.layer0(s, B.x[s], bx, B.xl, B.b_xl)
        B.layer1(s, B.xl, B.b_xl, B.y[s], B.b_y)
    S.barrier()
    return nc


def kernel(**inputs):
    global _NC
    f = lambda a: np.ascontiguousarray(np.asarray(a, dtype=np.float32))
    xs = [inputs["x_prompt"][i] for i in range(16)] + [inputs["x_sample"][i] for i in range(4)]
    ms = [inputs["mem_prompt"][i] for i in range(16)] + [inputs["mem_sample"][i] for i in range(4)]
    shared = dict(
        norm_g=f(inputs["norm_g"]),
        ev_w_in=f(inputs["ev_w_in"][0]), ev_w_out=f(inputs["ev_w_out"][0]),
        od_w_in=f(inputs["od_w_in"][0]), od_w_out=f(inputs["od_w_out"][0]),
        qk_gain=f(np.stack([inputs["ev_q_gain"][0], inputs["ev_k_gain"][0]])),
        od_conv_w=f(np.asarray(inputs["od_conv_w"][0]).reshape(5, 48, 128).transpose(2, 1, 0)),
        od_conv_b=f(np.asarray(inputs["od_conv_b"][0]).reshape(48, 128).T),
        od_a_log=f(np.asarray(inputs["od_a_log"][0]).reshape(128)),
        od_dt_bias=f(np.asarray(inputs["od_dt_bias"][0]).reshape(128)),
        od_d=f(inputs["od_d"][0]), od_norm_g=f(inputs["od_norm_g"][0]),
    )
    for l in range(2):
        for nm in ("xa_wq", "xa_wkv", "xa_wo", "ffn_w_gu", "ffn_w_down"):
            shared["%s%d" % (nm, l)] = f(inputs[nm][l])
    shared.update(host_consts())
    in_maps = []
    for c in range(8):
        ids = list(CORE_SEQS[c])
        while len(ids) < NS:
            ids.append(ids[0])
        m = dict(shared)
        m["x"] = f(np.stack([xs[i] for i in ids]))
        m["mem"] = f(np.stack([ms[i] for i in ids]))
        in_maps.append(m)
    if _NC is None:
        _NC = build_program()
    res = run_bass_kernel_spmd(_NC, in_maps, core_ids=list(range(8)))
    outs = [None] * 20
    for c in range(8):
        y = np.asarray(res.results[c]["y"], dtype=np.float32)
        for slot, i in enumerate(CORE_SEQS[c]):
            outs[i] = y[slot]
    y_prompt = np.stack(outs[:16]).astype(np.float32)
    y_sample = np.stack(outs[16:]).astype(np.float32)
    return (y_prompt, y_sample)
```
